# Optimizing a Trainium2 kernel written in Bass

```python
import jax, jax.numpy as jnp
from jax import lax
import numpy as np

D_MODEL = 1024
BATCH = 8
SEQ = 4096
DEPTH = 2

N_META = 16
CHUNK = 128
PAD_FRONT = (-N_META) % CHUNK
RET_HEADS = 4
RET_QK_DIM = D_MODEL // 8
RET_V_DIM = 2 * RET_QK_DIM
RET_QK_W = RET_HEADS * RET_QK_DIM
RET_V_W = RET_HEADS * RET_V_DIM
CONV_CH = D_MODEL
CONV_WIDTH = 31
MIX_IN_EVEN = 2 * RET_QK_W + 2 * RET_V_W + 2 * CONV_CH
MIX_OUT_EVEN = RET_V_W + CONV_CH
RET_DECAY_OFFSET = 5.0
ROPE_BASE = 10000.0
SB_HEADS = 16
SB_HEAD_DIM = D_MODEL // SB_HEADS
D_FF = 4 * D_MODEL
EPS = 1e-6
N_EVEN = (DEPTH + 1) // 2
N_ODD = DEPTH // 2

kernel_name = "hybrid_retention_conformer_stickbreaking_trunk"


def rmsnorm(x, g):
    xf = x.astype(jnp.float32)
    y = xf * lax.rsqrt(jnp.mean(xf * xf, axis=-1, keepdims=True) + EPS) * g.astype(jnp.float32)
    return y.astype(x.dtype)


def layernorm(x, g, b):
    xf = x.astype(jnp.float32)
    mu = jnp.mean(xf, axis=-1, keepdims=True)
    var = jnp.mean(jnp.square(xf - mu), axis=-1, keepdims=True)
    y = (xf - mu) * lax.rsqrt(var + EPS) * g.astype(jnp.float32) + b.astype(jnp.float32)
    return y.astype(x.dtype)


def rotary(x):
    P, d = x.shape[1], x.shape[-1]
    half = d // 2
    inv_freq = ROPE_BASE ** (-jnp.arange(half, dtype=jnp.float32) / half)
    ang = jnp.arange(P, dtype=jnp.float32)[:, None] * inv_freq[None, :]
    cos = jnp.cos(ang)[None, :, None, :]
    sin = jnp.sin(ang)[None, :, None, :]
    x1, x2 = x[..., :half], x[..., half:]
    return jnp.concatenate([x1 * cos - x2 * sin, x1 * sin + x2 * cos], axis=-1)


def retention_chunkwise(q, k, v):
    b, P, H, dk = q.shape
    dv = v.shape[-1]
    n = P // CHUNK
    log_g = jnp.log1p(-jnp.exp2(-RET_DECAY_OFFSET - jnp.arange(H, dtype=jnp.float32)))
    idx = jnp.arange(CHUNK, dtype=jnp.float32)
    diff = idx[:, None] - idx[None, :]
    inner_decay = jnp.where(diff[None] >= 0, jnp.exp(jnp.maximum(diff, 0.0)[None] * log_g[:, None, None]), 0.0)
    qc = q.reshape(b, n, CHUNK, H, dk)
    kc = k.reshape(b, n, CHUNK, H, dk)
    vc = v.reshape(b, n, CHUNK, H, dv)
    scores = jnp.einsum('bnihd,bnjhd->bnhij', qc, kc) * inner_decay
    o_inner = jnp.einsum('bnhij,bnjhe->bnihe', scores, vc)
    k_dec = kc * jnp.exp((CHUNK - 1 - idx)[:, None] * log_g[None, :])[:, :, None]
    kv = jnp.einsum('bnjhd,bnjhe->nbhde', k_dec, vc)
    chunk_decay = jnp.exp(CHUNK * log_g)[None, :, None, None]

    def step(state, kv_n):
        return chunk_decay * state + kv_n, state

    _, prev = lax.scan(step, jnp.zeros((b, H, dk, dv), jnp.float32), kv)
    q_dec = qc * jnp.exp((idx + 1.0)[:, None] * log_g[None, :])[:, :, None]
    o_cross = jnp.einsum('bnihd,nbhde->bnihe', q_dec, prev)
    return (o_inner + o_cross).reshape(b, P, H, dv)


def head_groupnorm(o, g):
    mu = jnp.mean(o, axis=-1, keepdims=True)
    var = jnp.mean(jnp.square(o - mu), axis=-1, keepdims=True)
    return (o - mu) * lax.rsqrt(var + EPS) * g.astype(jnp.float32)


def conformer_conv(u, conv_w, conv_b, ln_g, ln_b):
    a, gate = jnp.split(u, 2, axis=-1)
    hdn = a * jax.nn.sigmoid(gate)
    y = lax.conv_general_dilated(
        hdn, conv_w[:, None, :].astype(hdn.dtype), window_strides=(1,),
        padding=[(CONV_WIDTH - 1, 0)], dimension_numbers=('NWC', 'WIO', 'NWC'),
        feature_group_count=CONV_CH)
    y = y + conv_b.astype(y.dtype)
    return jax.nn.silu(layernorm(y, ln_g, ln_b))


def even_mixer(h, w_in, gn_g, conv_w, conv_b, ln_g, ln_b, w_out):
    b, L, _ = h.shape
    proj = h @ w_in.astype(h.dtype)
    q, k, v, g, u = jnp.split(proj, [RET_QK_W, 2 * RET_QK_W, 2 * RET_QK_W + RET_V_W,
                                     2 * RET_QK_W + 2 * RET_V_W], axis=-1)
    pad = ((0, 0), (PAD_FRONT, 0), (0, 0), (0, 0))
    q = jnp.pad(q.astype(jnp.float32).reshape(b, L, RET_HEADS, RET_QK_DIM), pad)
    k = jnp.pad(k.astype(jnp.float32).reshape(b, L, RET_HEADS, RET_QK_DIM), pad)
    v = jnp.pad(v.astype(jnp.float32).reshape(b, L, RET_HEADS, RET_V_DIM), pad)
    q = rotary(q)
    k = rotary(k) * (RET_QK_DIM ** -0.5)
    o = retention_chunkwise(q, k, v)[:, PAD_FRONT:]
    o = head_groupnorm(o, gn_g).reshape(b, L, RET_V_W).astype(h.dtype)
    o = jax.nn.silu(g) * o
    c = conformer_conv(u, conv_w, conv_b, ln_g, ln_b)
    return jnp.concatenate([o, c], axis=-1) @ w_out.astype(h.dtype)


def stick_breaking(q, k, v, n_pad):
    b, H, P, d = q.shape
    n = P // CHUNK
    scale = d ** -0.5
    key_pos = jnp.arange(P)

    def block(i):
        qb = lax.dynamic_slice_in_dim(q, i * CHUNK, CHUNK, axis=2)
        z = jnp.einsum('bhqd,bhkd->bhqk', qb, k) * scale
        q_pos = i * CHUNK + jnp.arange(CHUNK)
        valid = (key_pos[None, :] < q_pos[:, None]) & (key_pos[None, :] >= n_pad)
        log_keep = jnp.where(valid, jax.nn.log_sigmoid(-z), 0.0)
        after = lax.cumsum(log_keep, axis=3, reverse=True) - log_keep
        w = jnp.where(valid, jnp.exp(jax.nn.log_sigmoid(z) + after), 0.0)
        return jnp.einsum('bhqk,bhkd->bhqd', w, v)

    out = lax.map(block, jnp.arange(n))
    return jnp.transpose(out, (1, 0, 3, 2, 4)).reshape(b, P, H, d)


def odd_mixer(h, w_qkv, qn_g, kn_g, w_o):
    b, L, _ = h.shape
    qkv = h @ w_qkv.astype(h.dtype)
    q, k, v = jnp.split(qkv, 3, axis=-1)
    q = rmsnorm(q.reshape(b, L, SB_HEADS, SB_HEAD_DIM), qn_g)
    k = rmsnorm(k.reshape(b, L, SB_HEADS, SB_HEAD_DIM), kn_g)
    v = v.reshape(b, L, SB_HEADS, SB_HEAD_DIM)
    pad = ((0, 0), (PAD_FRONT, 0), (0, 0), (0, 0))
    to_bhpd = lambda t: jnp.transpose(jnp.pad(t.astype(jnp.float32), pad), (0, 2, 1, 3))
    o = stick_breaking(to_bhpd(q), to_bhpd(k), to_bhpd(v), PAD_FRONT)[:, PAD_FRONT:]
    o = o.reshape(b, L, D_MODEL).astype(h.dtype)
    return o @ w_o.astype(h.dtype)


def sq_relu_mlp(h, w1, w2):
    return jnp.square(jax.nn.relu(h @ w1.astype(h.dtype))) @ w2.astype(h.dtype)


def setup_inputs(seed: int = 0) -> dict:
    key = jax.random.key(seed)
    ks = jax.random.split(key, 17)
    nrm = lambda kk, shape, s: jax.random.normal(kk, shape, jnp.float32) * s
    return {
        "x": nrm(ks[0], (BATCH, SEQ, D_MODEL), 1.0),
        "meta": nrm(ks[1], (N_META, D_MODEL), 1.0),
        "norm_mix_g": 1.0 + nrm(ks[2], (DEPTH, D_MODEL), 0.02),
        "norm_mlp_g": 1.0 + nrm(ks[3], (DEPTH, D_MODEL), 0.02),
        "even_w_in": nrm(ks[4], (N_EVEN, D_MODEL, MIX_IN_EVEN), D_MODEL ** -0.5),
        "even_ret_gn_g": 1.0 + nrm(ks[5], (N_EVEN, RET_HEADS, RET_V_DIM), 0.02),
        "even_conv_w": nrm(ks[6], (N_EVEN, CONV_WIDTH, CONV_CH), CONV_WIDTH ** -0.5),
        "even_conv_b": nrm(ks[7], (N_EVEN, CONV_CH), 0.01),
        "even_conv_ln_g": 1.0 + nrm(ks[8], (N_EVEN, CONV_CH), 0.02),
        "even_conv_ln_b": nrm(ks[9], (N_EVEN, CONV_CH), 0.01),
        "even_w_out": nrm(ks[10], (N_EVEN, MIX_OUT_EVEN, D_MODEL), MIX_OUT_EVEN ** -0.5),
        "odd_w_qkv": nrm(ks[11], (N_ODD, D_MODEL, 3 * D_MODEL), D_MODEL ** -0.5),
        "odd_q_norm_g": 1.0 + nrm(ks[12], (N_ODD, SB_HEAD_DIM), 0.02),
        "odd_k_norm_g": 1.0 + nrm(ks[13], (N_ODD, SB_HEAD_DIM), 0.02),
        "odd_w_o": nrm(ks[14], (N_ODD, D_MODEL, D_MODEL), D_MODEL ** -0.5),
        "mlp_w1": nrm(ks[15], (DEPTH, D_MODEL, D_FF), D_MODEL ** -0.5),
        "mlp_w2": nrm(ks[16], (DEPTH, D_FF, D_MODEL), D_FF ** -0.5),
    }


def reference(x, meta, norm_mix_g, norm_mlp_g, even_w_in, even_ret_gn_g, even_conv_w,
              even_conv_b, even_conv_ln_g, even_conv_ln_b, even_w_out, odd_w_qkv,
              odd_q_norm_g, odd_k_norm_g, odd_w_o, mlp_w1, mlp_w2):
    b = x.shape[0]
    meta_b = jnp.broadcast_to(meta[None].astype(x.dtype), (b, N_META, D_MODEL))
    h = jnp.concatenate([meta_b, x], axis=1)
    for layer in range(DEPTH):
        j = layer // 2
        hn = rmsnorm(h, norm_mix_g[layer])
        if layer % 2 == 0:
            mix = even_mixer(hn, even_w_in[j], even_ret_gn_g[j], even_conv_w[j], even_conv_b[j],
                             even_conv_ln_g[j], even_conv_ln_b[j], even_w_out[j])
        else:
            mix = odd_mixer(hn, odd_w_qkv[j], odd_q_norm_g[j], odd_k_norm_g[j], odd_w_o[j])
        h = h + mix
        h = h + sq_relu_mlp(rmsnorm(h, norm_mlp_g[layer]), mlp_w1[layer], mlp_w2[layer])
    return h[:, N_META:]
```

```python
import numpy as np
import ml_dtypes
from contextlib import ExitStack
import concourse.bass as bass
import concourse.mybir as mybir
from concourse.bass_utils import run_bass_kernel_spmd

F32 = mybir.dt.float32
BF16 = mybir.dt.bfloat16
AF = mybir.ActivationFunctionType
ALU = mybir.AluOpType
AX = mybir.AxisListType

D = 1024
NMETA = 16
PAD = 112
EPS = 1e-6
TCH = 3
TN = 128 * TCH
DFF = 4096
NCORES = 8


class Buf:
    __slots__ = ("w", "r", "name")

    def __init__(self, name=""):
        self.w = None
        self.r = {}
        self.name = name


class Producer:
    def __init__(self, name, unit):
        self.name = name
        self.unit = unit
        self.sem = None
        self.count = 0


class Eng(Producer):
    def __init__(self, name, is_pe=False):
        super().__init__(name, 1)
        self.ops = []
        self.known = {}
        self.is_pe = is_pe
        self.lanes = []
        self.lane_i = 0


class Prog:
    def __init__(self, nc, es):
        self.nc = nc
        self.es = es
        self.pe = Eng("pe", True)
        self.act = Eng("act")
        self.dve = Eng("dve")
        self.pool = Eng("pool")
        self.sp = Eng("sp")
        self.engs = [self.pe, self.act, self.dve, self.pool, self.sp]
        self.prods = list(self.engs)
        for q in (self.sp, self.pool, self.act):
            for i in range(8):
                ln = Producer(f"{q.name}_l{i}", 16)
                q.lanes.append(ln)
                self.prods.append(ln)
        for p in self.prods:
            p.sem = es.enter_context(nc.semaphore("s_" + p.name))

    def op(self, eng, thunk, reads=(), writes=(), lane=None):
        d = {}
        for b in reads:
            if b.w is not None:
                p, i = b.w
                if d.get(p, 0) < i:
                    d[p] = i
        for b in writes:
            if b.w is not None:
                p, i = b.w
                if d.get(p, 0) < i:
                    d[p] = i
            for p, i in b.r.items():
                if d.get(p, 0) < i:
                    d[p] = i
        prod = lane if lane is not None else eng
        if lane is not None and lane.count > 0:
            d[lane] = lane.count
        for p, i in d.items():
            if p is eng and eng.is_pe:
                continue
            if eng.known.get(p, 0) < i:
                eng.ops.append(("w", p, i))
                eng.known[p] = i
        prod.count += 1
        idx = prod.count
        eng.ops.append(("i", thunk, prod))
        for b in reads:
            b.r[prod] = idx
        for b in writes:
            b.w = (prod, idx)
            b.r = {}

    def dma(self, q, out, in_, reads=(), writes=()):
        lane = q.lanes[q.lane_i % len(q.lanes)]
        q.lane_i += 1
        self.op(q, lambda e, o=out, i=in_: e.dma_start(out=o, in_=i), reads, writes, lane=lane)

    def barrier(self):
        for e in self.engs:
            for p in self.prods:
                if p.count > 0 and e.known.get(p, 0) < p.count and not (p is e):
                    e.ops.append(("w", p, p.count))
                    e.known[p] = p.count

    def finish(self):
        for q in (self.sp, self.pool, self.act):
            for ln in q.lanes:
                if ln.count > 0 and q.known.get(ln, 0) < ln.count:
                    q.ops.append(("w", ln, ln.count))
                    q.known[ln] = ln.count
        for p in self.prods:
            if p.count > 0 and p is not self.sp and self.sp.known.get(p, 0) < p.count:
                self.sp.ops.append(("w", p, p.count))
                self.sp.known[p] = p.count

    def emit(self):
        nc = self.nc

        def replay(eng, e):
            for o in eng.ops:
                if o[0] == "w":
                    e.wait_ge(o[1].sem, o[2] * o[1].unit)
                else:
                    ins = o[1](e)
                    ins.then_inc(o[2].sem, o[2].unit)

        with nc.Block() as block:
            @block.sync
            def _(e):
                replay(self.sp, e)

            @block.scalar
            def _(e):
                replay(self.act, e)

            @block.vector
            def _(e):
                replay(self.dve, e)

            @block.gpsimd
            def _(e):
                replay(self.pool, e)

            @block.tensor
            def _(e):
                replay(self.pe, e)


def host_consts(P):
    nch = P // 128
    cb = np.zeros((128, 5, 128), np.float32)
    j = np.arange(128)[:, None]
    k = np.arange(128)[None, :]
    cb[:, 0] = (j == k)
    cb[:, 1] = 1.0
    cb[:, 2] = (j >= k)
    cb[:, 3] = (j < k)
    cb[:, 4] = ((j // 64) == (k // 64))
    cf = np.zeros((128, 3 * 128 + 8), np.float32)
    cf[:, 0:128] = (j == k)
    cf[:, 128:256] = (k >= j)
    cf[:, 256:384] = (j < k)
    idx = np.arange(128, dtype=np.float64)
    for h in range(4):
        lg = np.log1p(-np.exp2(-5.0 - h))
        cf[:, 384 + h] = np.exp((idx - 127.0) * lg)
        cf[:, 388 + h] = np.exp((127.0 - idx) * lg) * (128.0 ** -0.5)
    half = 64
    inv_freq = (10000.0 ** (-np.arange(half, dtype=np.float32) / half)).astype(np.float32)
    ang = np.arange(P, dtype=np.float32)[:, None] * inv_freq[None, :]
    cos = np.cos(ang).astype(np.float32)
    sin = np.sin(ang).astype(np.float32)
    rot = np.zeros((P, 2, 128), np.float32)
    rot[:, 0, :64] = cos
    rot[:, 0, 64:] = cos
    rot[:, 1, :64] = -sin
    rot[:, 1, 64:] = sin
    return cb.astype(ml_dtypes.bfloat16), cf, rot


def build(SEQ, debug=None, stop=None):
    P = PAD + NMETA + SEQ
    NCHK = P // 128
    assert NCHK % TCH == 0
    NT = NCHK // TCH
    nc = bass.Bass("TRN2", target_bir_lowering=False)

    def din(name, shape, dt=F32):
        return nc.dram_tensor(name, list(shape), dt, kind="ExternalInput").ap()

    def dscr(name, shape, dt):
        return nc.dram_tensor(name, list(shape), dt, kind="Internal").ap()

    x = din("x", [SEQ, D])
    meta = din("meta", [NMETA, D])
    norm_mix_g = din("norm_mix_g", [2, D])
    norm_mlp_g = din("norm_mlp_g", [2, D])
    w_in = din("even_w_in", [D, 5120])
    gn_g = din("even_ret_gn_g", [1024])
    conv_w = din("even_conv_w", [31, D])
    conv_b = din("even_conv_b", [D])
    ln_g = din("even_conv_ln_g", [D])
    ln_b = din("even_conv_ln_b", [D])
    w_out = din("even_w_out", [2048, D])
    w_qkv = din("odd_w_qkv", [D, 3072])
    qn_g = din("odd_q_norm_g", [64])
    kn_g = din("odd_k_norm_g", [64])
    w_o = din("odd_w_o", [D, D])
    w1 = [din("mlp_w1_0", [D, DFF]), din("mlp_w1_1", [D, DFF])]
    w2 = [din("mlp_w2_0", [DFF, D]), din("mlp_w2_1", [DFF, D])]
    c_bf = din("c_bf", [128, 5, 128], BF16)
    c_f = din("c_f", [128, 392])
    c_rot = din("c_rot", [P, 2, 128])
    out = nc.dram_tensor("out", [SEQ, D], F32, kind="ExternalOutput").ap()
    dbg = None
    if debug:
        dbg = {k: nc.dram_tensor("dbg_" + k, list(s), F32, kind="ExternalOutput").ap() for k, s in debug.items()}

    s_win = dscr("s_win", [D, 5120], BF16)
    s_wout = dscr("s_wout", [2048, D], BF16)
    s_w1 = [dscr("s_w1_0", [D, DFF], BF16), dscr("s_w1_1", [D, DFF], BF16)]
    s_w2 = [dscr("s_w2_0", [DFF, D], BF16), dscr("s_w2_1", [DFF, D], BF16)]
    s_wqkv = dscr("s_wqkv", [D, 3072], BF16)
    s_wo = dscr("s_wo", [D, D], BF16)
    s_qT = dscr("s_qT", [8, 128, P], BF16)
    s_kT = dscr("s_kT", [8, 128, P], BF16)
    s_v = dscr("s_v", [P, D], BF16)
    s_h = dscr("s_h", [P, D], F32)

    with ExitStack() as es:
        K = Prog(nc, es)
        pe, act, dve, pool, sp = K.pe, K.act, K.dve, K.pool, K.sp

        def sb(stack, name, shape, dt):
            return stack.enter_context(nc.sbuf_tensor(name, list(shape), dt))

        tp = es.enter_context(nc.psum_tensor("tp", [128, 1024], BF16))
        pp = es.enter_context(nc.psum_tensor("pp", [128, 7, 512], F32))
        b_tp = Buf("tp")
        b_pp = [Buf(f"pp{i}") for i in range(7)]

        cbf = sb(es, "cbf", [128, 5, 128], BF16)
        cf = sb(es, "cf", [128, 392], F32)
        b_c = Buf("consts")
        K.dma(sp, cbf[:], c_bf[:, :, :], writes=[b_c])
        K.dma(sp, cf[:], c_f[:, :], writes=[b_c])
        ident = cbf[:, 0, :]
        ones_bf = cbf[:, 1, :]
        tri = cbf[:, 2, :]
        stri = cbf[:, 3, :]
        blk = cbf[:, 4, :]
        ident_f = cf[:, 0:128]
        maskT = cf[:, 128:256]
        mT = cf[:, 256:384]
        dq = cf[:, 384:388]
        dk = cf[:, 388:392]

        vstage = sb(es, "vstage", [128, 128], F32)
        vstage2 = sb(es, "vstage2", [128, 128], F32)
        convw = sb(es, "convw", [128, 248], F32)
        pvec = sb(es, "pvec", [128, 64], F32)
        gqk = sb(es, "gqk", [128, 2], F32)
        gng_b = sb(es, "gng_b", [128, 1024], F32)
        b_vs, b_vs2, b_convw, b_pvec, b_gqk, b_gng = Buf(), Buf(), Buf(), Buf(), Buf(), Buf()
        K.dma(sp, gng_b[:], gn_g.partition_broadcast(128), writes=[b_gng])
        cw = conv_w.rearrange("w (c p) -> (w c) p", p=128)
        K.dma(sp, vstage[0:124, :], cw[0:124, :], writes=[b_vs])
        K.dma(sp, vstage2[0:124, :], cw[124:248, :], writes=[b_vs2])
        ptf = pp[:, 0, :]
        K.op(pe, lambda e: e.transpose(out=ptf[:, 0:124], in_=vstage[0:124, :], identity=ident_f[0:124, 0:124]),
             reads=[b_vs, b_c], writes=[b_pp[0]])
        K.op(pe, lambda e: e.transpose(out=ptf[:, 124:248], in_=vstage2[0:124, :], identity=ident_f[0:124, 0:124]),
             reads=[b_vs2, b_c], writes=[b_pp[0]])
        K.op(dve, lambda e: e.tensor_copy(out=convw[:], in_=ptf[:, 0:248]), reads=[b_pp[0]], writes=[b_convw])
        vecs = [conv_b, ln_g, ln_b, norm_mix_g[0], norm_mlp_g[0], norm_mix_g[1], norm_mlp_g[1]]
        for i, v in enumerate(vecs):
            K.dma(sp, vstage[8 * i:8 * i + 8, :], v.rearrange("(c p) -> c p", p=128), writes=[b_vs])
        K.op(pe, lambda e: e.transpose(out=pp[:, 1, 0:56], in_=vstage[0:56, :], identity=ident_f[0:56, 0:56]),
             reads=[b_vs, b_c], writes=[b_pp[1]])
        K.op(dve, lambda e: e.tensor_copy(out=pvec[:, 0:56], in_=pp[:, 1, 0:56]), reads=[b_pp[1]], writes=[b_pvec])
        for hh in range(2):
            K.dma(sp, gqk[hh * 64:(hh + 1) * 64, 0:1], qn_g.rearrange("(p o) -> p o", o=1), writes=[b_gqk])
            K.dma(sp, gqk[hh * 64:(hh + 1) * 64, 1:2], kn_g.rearrange("(p o) -> p o", o=1), writes=[b_gqk])
        K.op(dve, lambda e: e.tensor_scalar(out=gqk[:, 0:1], in0=gqk[:, 0:1], scalar1=0.125, scalar2=None, op0=ALU.mult),
             reads=[b_gqk], writes=[b_gqk])
        G_MIX0, G_MLP0, G_MIX1, G_MLP1 = 24, 32, 40, 48

        NSLOT = 3
        wslot = [sb(es, f"wslot{i}", [128, 8, 512], BF16) for i in range(NSLOT)]
        b_wslot = [Buf(f"wslot{i}") for i in range(NSLOT)]
        wctr = [0]
        b_scr = {}

        def scr_buf(t):
            return b_scr.setdefault(id(t), Buf("scr"))

        def load_w(ws, k0, c0, nk=8):
            i = wctr[0] % NSLOT
            wctr[0] += 1
            src = ws.rearrange("(kc p) n -> p kc n", p=128)[:, k0:k0 + nk, c0:c0 + 512]
            K.dma(sp, wslot[i][:, 0:nk, :], src, reads=[scr_buf(ws)], writes=[b_wslot[i]])
            return wslot[i], b_wslot[i]

        class _NS:
            pass

        def alloc_tile(stack, tag):
            X = _NS()
            X.a1T = sb(stack, "a1T" + tag, [128, 32, TN], BF16)
            X.b_a1T = Buf("a1T")
            X.hres = sb(stack, "hres" + tag, [128, TCH, D], F32)
            X.b_hres = [Buf(f"hres{c}") for c in range(TCH)]
            X.hnT = sb(stack, "hnT" + tag, [128, 8, TN], BF16)
            X.b_hnT = [Buf(f"hnT{c}") for c in range(TCH)]
            X.hn_bf = [sb(stack, f"hn_bf{i}" + tag, [128, D], BF16) for i in range(2)]
            X.b_hn_bf = [Buf(), Buf()]
            X.junk = sb(stack, "junk" + tag, [128, D], BF16)
            X.b_junk = Buf()
            X.nstat = sb(stack, "nstat" + tag, [128, 8], F32)
            X.b_nstat = Buf()
            X.rr = [sb(stack, f"rr{i}" + tag, [128, TN], F32) for i in range(2)]
            X.b_rr = [Buf(), Buf()]
            return X

        with ExitStack() as es_a:
            def convert(wsrc, wdst):
                Kr, N = wsrc.shape
                rows = 128 * max(1, 1024 // N)
                for r0 in range(0, Kr, rows):
                    lane = pool.lanes[pool.lane_i % len(pool.lanes)]
                    pool.lane_i += 1
                    K.op(pool, lambda e, o=wdst[r0:r0 + rows, :], i=wsrc[r0:r0 + rows, :]: e.dma_start(
                        out=o, in_=i, max_dma_last_dim=4096), writes=[scr_buf(wdst)], lane=lane)

            convert(w_in, s_win)
            convert(w_out, s_wout)
            convert(w1[0], s_w1[0])
            convert(w2[0], s_w2[0])
            convert(w_qkv, s_wqkv)
            convert(w_o, s_wo)
            convert(w1[1], s_w1[1])
            convert(w2[1], s_w2[1])

            X = alloc_tile(es_a, "a")
            hnT, b_hnT = X.hnT, X.b_hnT
            hres_l = [X.hres, sb(es_a, "hresa2", [128, TCH, D], F32)]
            b_hres_l = [X.b_hres, [Buf(f"hresa2_{c}") for c in range(TCH)]]
            mmctr = [0]
            hnctr = [0]

            mm_set = [[0, 1, 2]]

            def mm_bank():
                st = mm_set[0]
                i = st[mmctr[0] % len(st)]
                mmctr[0] += 1
                return pp[:, i, :], b_pp[i]

            def rmsnorm_T(X, gcol):
                hr, b_hr = X.hres, X.b_hres
                for c in range(TCH):
                    K.op(act, lambda e, c=c: e.activation(out=X.junk[:], in_=hr[:, c, :], func=AF.Square,
                                                          accum_out=X.nstat[:, c:c + 1]),
                         reads=[b_hr[c]], writes=[X.b_junk, X.b_nstat])
                K.op(act, lambda e: e.activation(out=X.nstat[:, 4:4 + TCH], in_=X.nstat[:, 0:TCH], func=AF.Sqrt,
                                                 bias=EPS, scale=1.0 / D),
                     reads=[X.b_nstat], writes=[X.b_nstat])
                K.op(dve, lambda e: e.reciprocal(out=X.nstat[:, 0:TCH], in_=X.nstat[:, 4:4 + TCH]),
                     reads=[X.b_nstat], writes=[X.b_nstat])
                for c in range(TCH):
                    i = hnctr[0] % 2
                    hnctr[0] += 1
                    K.op(dve, lambda e, c=c, i=i: e.tensor_scalar(out=X.hn_bf[i][:], in0=hr[:, c, :],
                                                                   scalar1=X.nstat[:, c:c + 1], scalar2=None, op0=ALU.mult),
                         reads=[b_hr[c], X.b_nstat], writes=[X.b_hn_bf[i]])

                    def tr(e, i=i):
                        ins = None
                        for kc in range(8):
                            ins = e.transpose(out=tp[:, kc * 128:(kc + 1) * 128], in_=X.hn_bf[i][:, kc * 128:(kc + 1) * 128],
                                              identity=ident)
                        return ins
                    K.op(pe, tr, reads=[X.b_hn_bf[i], b_c], writes=[b_tp])
                    gbc = pvec[:, gcol:gcol + 8].unsqueeze(2).broadcast_to([128, 8, 128])
                    K.op(dve, lambda e, c=c, gbc=gbc: e.tensor_tensor(
                        out=X.hnT[:, :, c * 128:(c + 1) * 128], in0=tp[:].rearrange("p (k t) -> p k t", k=8), in1=gbc,
                        op=ALU.mult), reads=[b_tp, b_pvec], writes=[X.b_hnT[c]])

            def mlp(X, layer, gcol, aux):
                hr, b_hr = X.hres, X.b_hres
                rmsnorm_T(X, gcol)
                mm_set[0] = [3, 4, 5, 6, 0, 1, 2]
                for cb in range(8):
                    wt, bw = load_w(s_w1[layer], 0, cb * 512)
                    for j in range(4):
                        hb = cb * 4 + j
                        bank, bb = mm_bank()

                        def mmf(e, wt=wt, j=j, bank=bank):
                            ins = None
                            for kc in range(8):
                                ins = e.matmul(bank[:, 0:TN], lhsT=wt[:, kc, j * 128:(j + 1) * 128], rhs=X.hnT[:, kc, :],
                                               start=(kc == 0), stop=(kc == 7))
                            return ins
                        K.op(pe, mmf, reads=[bw] + X.b_hnT, writes=[bb])
                        i = hb % 2
                        K.op(act, lambda e, i=i, bank=bank: e.activation(out=X.rr[i][:], in_=bank[:, 0:TN], func=AF.Relu),
                             reads=[bb], writes=[X.b_rr[i]])
                        K.op(aux, lambda e, i=i, hb=hb: e.tensor_tensor(out=X.a1T[:, hb, :], in0=X.rr[i][:], in1=X.rr[i][:],
                                                                         op=ALU.mult),
                             reads=[X.b_rr[i]], writes=[X.b_a1T])
                mm_set[0] = [0, 1, 2]
                for cb in range(2):
                    for kp in range(4):
                        wt, bw = load_w(s_w2[layer], kp * 8, cb * 512)
                        for c in range(TCH):
                            def mmf(e, wt=wt, c=c, kp=kp):
                                ins = None
                                for kc in range(8):
                                    ins = e.matmul(pp[:, c, :], lhsT=X.a1T[:, kp * 8 + kc, c * 128:(c + 1) * 128],
                                                   rhs=wt[:, kc, :], start=(kp == 0 and kc == 0),
                                                   stop=(kp == 3 and kc == 7))
                                return ins
                            K.op(pe, mmf, reads=[bw, X.b_a1T], writes=[b_pp[c]])
                    for c in range(TCH):
                        K.op(dve, lambda e, c=c, cb=cb: e.tensor_tensor(
                            out=hr[:, c, cb * 512:(cb + 1) * 512], in0=hr[:, c, cb * 512:(cb + 1) * 512],
                            in1=pp[:, c, :], op=ALU.add), reads=[b_pp[c], b_hr[c]], writes=[b_hr[c]])

            with ExitStack() as es0:
                B6 = [sb(es0, f"B6_{i}", [128, TCH * D], BF16) for i in range(3)]
                b_B6 = [Buf(f"B6_{i}") for i in range(3)]
                qkd = B6[0][:].rearrange("p (c n) -> p c n", c=TCH)
                vb = B6[1][:].rearrange("p (c n) -> p c n", c=TCH)
                sgb = B6[2][:].rearrange("p (c n) -> p c n", c=TCH)
                b_qkd, b_vb, b_sg = b_B6
                rot = sb(es0, "rot", [128, TCH, 2, 128], F32)
                b_rot = Buf()
                tA = sb(es0, "tA", [128, 512], F32)
                tB = sb(es0, "tB", [128, 512], F32)
                tR = sb(es0, "tR", [128, 512], F32)
                b_tA, b_tB, b_tR = Buf(), Buf(), Buf()
                qkT = [sb(es0, f"qkT{i}", [128, 8, 128], BF16) for i in range(2)]
                b_qkT = [Buf(), Buf()]
                scT = [sb(es0, f"scT{i}", [128, 4, 128], BF16) for i in range(2)]
                b_scT = [Buf(), Buf()]
                Tst = sb(es0, "Tst", [128, 1024], F32)
                Sb = sb(es0, "Sb", [128, 1024], BF16)
                b_Tst, b_Sb = Buf(), Buf()
                gst = sb(es0, "gst", [128, 4, 6], F32)
                gmv = sb(es0, "gmv", [128, 4, 2], F32)
                grs = sb(es0, "grs", [128, 8], F32)
                b_gst, b_gmv, b_grs = Buf(), Buf(), Buf()
                tmpn = sb(es0, "tmpn", [128, 1024], F32)
                b_tmpn = Buf()
                og = sb(es0, "og", [128, 1024], BF16)
                b_og = Buf()
                ocT = sb(es0, "ocT", [128, 16, TN], BF16)
                b_ocT = [Buf(f"ocT{i}") for i in range(16)]
                hdn = sb(es0, "hdn", [128, 8, 30 + TN], BF16)
                b_hdn = [Buf(f"hdn{i}") for i in range(8)]
                dg2 = [sb(es0, f"dg{i}", [128, 31, 128], BF16) for i in range(2)]
                b_dg2 = [[Buf(), Buf(), Buf()], [Buf(), Buf(), Buf()]]
                ysb = sb(es0, "ysb", [128, 8, TN], F32)
                b_ysb = [Buf() for _ in range(8)]
                ybf = [sb(es0, f"ybf{i}", [128, TN], BF16) for i in range(2)]
                ysq = [sb(es0, f"ysq{i}", [128, TN], BF16) for i in range(2)]
                b_ybf, b_ysq = [Buf(), Buf()], [Buf(), Buf()]
                sig = [sb(es0, f"sig{i}", [128, TN], F32) for i in range(2)]
                b_sig = [Buf(), Buf()]
                lmean = sb(es0, "lmean", [128, TN], F32)
                lrstd = sb(es0, "lrstd", [128, TN], F32)
                lt = sb(es0, "lt", [128, TN], F32)
                b_lmean, b_lrstd, b_lt = Buf(), Buf(), Buf()
                ld = [sb(es0, f"ld{i}", [128, TN], F32) for i in range(2)]
                b_ld = [Buf(), Buf()]
                sqh = [sb(es0, f"sqh{i}", [128, TN], BF16) for i in range(3)]
                b_sqh = [Buf(), Buf(), Buf()]
                rsh = [sb(es0, f"rsh{i}", [128, TN], F32) for i in range(3)]
                b_rsh = [Buf(), Buf(), Buf()]

                K.op(dve, lambda e: e.memset(Tst[:], 0.0), writes=[b_Tst])
                K.op(dve, lambda e: e.memset(Sb[:], 0.0), writes=[b_Sb])
                K.op(dve, lambda e: e.memset(hdn[:, :, 0:30], 0.0), writes=b_hdn)
                cgam = [float(np.exp(128.0 * np.log1p(-np.exp2(-5.0 - h)))) for h in range(4)]

                def load_x(tt_):
                    hrx, b_hrx = hres_l[tt_ % 2], b_hres_l[tt_ % 2]
                    if tt_ == 0:
                        K.op(dve, lambda e, hrx=hrx: e.memset(hrx[:, 0, :], 0.0), writes=[b_hrx[0]])
                        K.dma(sp, hrx[PAD:128, 0, :], meta[:, :], writes=[b_hrx[0]])
                        K.dma(sp, hrx[:, 1:TCH, :], x[0:(TCH - 1) * 128, :].rearrange("(c p) d -> p c d", p=128),
                              writes=b_hrx[1:TCH])
                    else:
                        r0 = tt_ * TN - 128
                        K.dma(sp, hrx[:, :, :], x[r0:r0 + TN, :].rearrange("(c p) d -> p c d", p=128), writes=b_hrx)

                for t in range(NT if stop != 'p0t0' else 1):
                    aux = dve
                    X.hres, X.b_hres = hres_l[t % 2], b_hres_l[t % 2]
                    hr, b_hr = X.hres, X.b_hres
                    p0 = t * TN
                    if t == 0:
                        load_x(0)
                    K.dma(sp, rot[:], c_rot[p0:p0 + TN].rearrange("(c p) a d -> p c a d", p=128), writes=[b_rot])

                    rmsnorm_T(X, G_MIX0)
                    def proj_gen(cbs):
                        for cb in cbs:
                            wt, bw = load_w(s_win, 0, cb * 512)
                            for c in range(TCH):
                                bank, bb = mm_bank()

                                def mmf(e, wt=wt, c=c, bank=bank):
                                    ins = None
                                    for kc in range(8):
                                        ins = e.matmul(bank, lhsT=hnT[:, kc, c * 128:(c + 1) * 128], rhs=wt[:, kc, :],
                                                       start=(kc == 0), stop=(kc == 7))
                                    return ins
                                K.op(pe, mmf, reads=[bw, b_hnT[c]], writes=[bb])
                                if cb < 2:
                                    dec = dq if cb == 0 else dk
                                    b3 = bank.rearrange("p (h d) -> p h d", h=4)
                                    cosb = rot[:, c, 0, :].unsqueeze(1).broadcast_to([128, 4, 128])
                                    K.op(dve, lambda e, b3=b3, cosb=cosb: e.tensor_tensor(
                                        out=tA[:].rearrange("p (h d) -> p h d", h=4), in0=b3, in1=cosb, op=ALU.mult),
                                        reads=[bb, b_rot], writes=[b_tA])
                                    s1 = rot[:, c, 1, 0:64].unsqueeze(1).broadcast_to([128, 4, 64])
                                    s2 = rot[:, c, 1, 64:128].unsqueeze(1).broadcast_to([128, 4, 64])
                                    tB3 = tB[:].rearrange("p (h d) -> p h d", h=4)

                                    def rotB(e, b3=b3, s1=s1, s2=s2, tB3=tB3):
                                        e.tensor_tensor(out=tB3[:, :, 0:64], in0=b3[:, :, 64:128], in1=s1, op=ALU.mult)
                                        return e.tensor_tensor(out=tB3[:, :, 64:128], in0=b3[:, :, 0:64], in1=s2, op=ALU.mult)
                                    K.op(dve, rotB, reads=[bb, b_rot], writes=[b_tB])
                                    K.op(aux, lambda e: e.tensor_tensor(out=tR[:], in0=tA[:], in1=tB[:], op=ALU.add),
                                         reads=[b_tA, b_tB], writes=[b_tR])

                                    def decf(e, c=c, cb=cb, dec=dec):
                                        ins = None
                                        for h in range(4):
                                            ins = e.activation(out=qkd[:, c, cb * 512 + h * 128: cb * 512 + (h + 1) * 128],
                                                               in_=tR[:, h * 128:(h + 1) * 128], func=AF.Copy,
                                                               scale=dec[:, h:h + 1])
                                        return ins
                                    K.op(act, decf, reads=[b_tR, b_c], writes=[b_qkd])
                                elif cb < 4:
                                    K.op(act, lambda e, c=c, cb=cb, bank=bank: e.activation(
                                        out=vb[:, c, (cb - 2) * 512:(cb - 1) * 512], in_=bank, func=AF.Copy),
                                        reads=[bb], writes=[b_vb])
                                else:
                                    K.op(act, lambda e, c=c, cb=cb, bank=bank: e.activation(
                                        out=sgb[:, c, (cb - 4) * 512:(cb - 3) * 512], in_=bank, func=AF.Silu),
                                        reads=[bb], writes=[b_sg])
                                    K.op(dve if t < 2 else pool, lambda e, c=c, cb=cb: e.tensor_tensor(
                                        out=sgb[:, c, (cb - 4) * 512:(cb - 3) * 512],
                                        in0=sgb[:, c, (cb - 4) * 512:(cb - 3) * 512],
                                        in1=gng_b[:, (cb - 4) * 512:(cb - 3) * 512], op=ALU.mult),
                                        reads=[b_sg, b_gng], writes=[b_sg])
                                yield


                    mm_set[0] = [0, 1, 2, 3, 4, 5, 6]
                    for _ in proj_gen(range(4)):
                        pass
                    mm_set[0] = [0, 1, 2]

                    def ret_gen():
                        for c in range(TCH):
                            i2 = c % 2

                            def trq(e, c=c):
                                ins = None
                                for kc in range(8):
                                    ins = e.transpose(out=tp[:, kc * 128:(kc + 1) * 128], in_=qkd[:, c, kc * 128:(kc + 1) * 128],
                                                      identity=ident)
                                return ins
                            K.op(pe, trq, reads=[b_qkd, b_c], writes=[b_tp])
                            K.op(act, lambda e, i2=i2: e.activation(out=qkT[i2][:], in_=tp[:].rearrange("p (k t) -> p k t", k=8),
                                                                    func=AF.Copy), reads=[b_tp], writes=[b_qkT[i2]])
                            bank, bb = mm_bank()

                            def scf(e, i2=i2, bank=bank):
                                ins = None
                                for h in range(4):
                                    ins = e.matmul(bank[:, h * 128:(h + 1) * 128], lhsT=qkT[i2][:, 4 + h, :],
                                                   rhs=qkT[i2][:, h, :], start=True, stop=True)
                                return ins
                            K.op(pe, scf, reads=[b_qkT[i2]], writes=[bb])
                            mb = maskT.unsqueeze(1).broadcast_to([128, 4, 128])
                            K.op(dve, lambda e, i2=i2, bank=bank, mb=mb: e.tensor_tensor(
                                out=scT[i2][:], in0=bank.rearrange("p (h t) -> p h t", h=4), in1=mb, op=ALU.mult),
                                reads=[bb, b_c], writes=[b_scT[i2]])
                            yield
                            o_ps = pp[:, 3:5, :].rearrange("p a b -> p (a b)")
                            kv_ps = pp[:, 5:7, :].rearrange("p a b -> p (a b)")

                            def of(e, i2=i2, c=c):
                                ins = None
                                for h in range(4):
                                    e.matmul(o_ps[:, h * 256:(h + 1) * 256], lhsT=scT[i2][:, h, :],
                                             rhs=vb[:, c, h * 256:(h + 1) * 256], start=True, stop=False)
                                    ins = e.matmul(o_ps[:, h * 256:(h + 1) * 256], lhsT=qkT[i2][:, h, :],
                                                   rhs=Sb[:, h * 256:(h + 1) * 256], start=False, stop=True)
                                return ins
                            K.op(pe, of, reads=[b_scT[i2], b_vb, b_qkT[i2], b_Sb], writes=[b_pp[3], b_pp[4]])

                            def kvf(e, c=c):
                                ins = None
                                for h in range(4):
                                    ins = e.matmul(kv_ps[:, h * 256:(h + 1) * 256],
                                                   lhsT=qkd[:, c, 512 + h * 128: 512 + (h + 1) * 128],
                                                   rhs=vb[:, c, h * 256:(h + 1) * 256], start=True, stop=True)
                                return ins
                            K.op(pe, kvf, reads=[b_qkd, b_vb], writes=[b_pp[5], b_pp[6]])

                            def tupd(e):
                                ins = None
                                for h in range(4):
                                    ins = e.scalar_tensor_tensor(out=Tst[:, h * 256:(h + 1) * 256], in0=Tst[:, h * 256:(h + 1) * 256],
                                                                 scalar=cgam[h], in1=kv_ps[:, h * 256:(h + 1) * 256],
                                                                 op0=ALU.mult, op1=ALU.add)
                                return ins
                            K.op(dve, tupd, reads=[b_pp[5], b_pp[6], b_Tst], writes=[b_Tst])

                            def sbf(e):
                                ins = None
                                for h in range(4):
                                    ins = e.activation(out=Sb[:, h * 256:(h + 1) * 256], in_=Tst[:, h * 256:(h + 1) * 256],
                                                       func=AF.Copy, scale=cgam[h])
                                return ins
                            K.op(act, sbf, reads=[b_Tst], writes=[b_Sb])

                            def gstf(e):
                                ins = None
                                for h in range(4):
                                    ins = e.bn_stats(out=gst[:, h, :], in_=o_ps[:, h * 256:(h + 1) * 256])
                                return ins
                            K.op(dve, gstf, reads=[b_pp[3], b_pp[4]], writes=[b_gst])

                            def gagf(e):
                                ins = None
                                for h in range(4):
                                    ins = e.bn_aggr(out=gmv[:, h, :], in_=gst[:, h, :])
                                return ins
                            K.op(dve, gagf, reads=[b_gst], writes=[b_gmv])
                            K.op(act, lambda e: e.activation(out=grs[:, 4:8], in_=gmv[:, :, 1], func=AF.Sqrt, bias=EPS, scale=1.0),
                                 reads=[b_gmv], writes=[b_grs])
                            K.op(dve, lambda e: e.reciprocal(out=grs[:, 0:4], in_=grs[:, 4:8]), reads=[b_grs], writes=[b_grs])

                            def gnf(e):
                                ins = None
                                for h in range(4):
                                    ins = e.tensor_scalar(out=tmpn[:, h * 256:(h + 1) * 256], in0=o_ps[:, h * 256:(h + 1) * 256],
                                                          scalar1=gmv[:, h, 0:1], scalar2=grs[:, h:h + 1],
                                                          op0=ALU.subtract, op1=ALU.mult)
                                return ins
                            K.op(dve, gnf, reads=[b_pp[3], b_pp[4], b_gmv, b_grs], writes=[b_tmpn])
                            K.op(aux, lambda e, c=c: e.tensor_tensor(out=og[:], in0=tmpn[:], in1=sgb[:, c, :], op=ALU.mult),
                                 reads=[b_tmpn, b_sg], writes=[b_og])
                            yield

                            def tro(e):
                                ins = None
                                for kc in range(8):
                                    ins = e.transpose(out=tp[:, kc * 128:(kc + 1) * 128], in_=og[:, kc * 128:(kc + 1) * 128],
                                                      identity=ident)
                                return ins
                            K.op(pe, tro, reads=[b_og, b_c], writes=[b_tp])
                            K.op(act, lambda e, c=c: e.activation(out=ocT[:, 0:8, c * 128:(c + 1) * 128],
                                                                   in_=tp[:].rearrange("p (k t) -> p k t", k=8), func=AF.Copy),
                                 reads=[b_tp], writes=b_ocT[0:8])

                            yield

                    if t > 0:
                        K.op(aux, lambda e: e.tensor_copy(out=hdn[:, :, 0:30], in_=hdn[:, :, TN:TN + 30]),
                             reads=b_hdn, writes=b_hdn)
                    def ag_gen():
                        for j in range(2):
                            wa, bwa = load_w(s_win, 0, 3072 + j * 512)
                            wg, bwg = load_w(s_win, 0, 4096 + j * 512)
                            for cc in range(4):
                                ch = j * 4 + cc
                                banka, bba = mm_bank()
                                bankg, bbg = mm_bank()

                                def mma(e, w=wa, cc=cc, bank=banka):
                                    ins = None
                                    for kc in range(8):
                                        ins = e.matmul(bank[:, 0:TN], lhsT=w[:, kc, cc * 128:(cc + 1) * 128], rhs=hnT[:, kc, :],
                                                       start=(kc == 0), stop=(kc == 7))
                                    return ins
                                K.op(pe, mma, reads=[bwa] + b_hnT, writes=[bba])

                                def mmg(e, w=wg, cc=cc, bank=bankg):
                                    ins = None
                                    for kc in range(8):
                                        ins = e.matmul(bank[:, 0:TN], lhsT=w[:, kc, cc * 128:(cc + 1) * 128], rhs=hnT[:, kc, :],
                                                       start=(kc == 0), stop=(kc == 7))
                                    return ins
                                K.op(pe, mmg, reads=[bwg] + b_hnT, writes=[bbg])
                                i2 = ch % 2
                                K.op(act, lambda e, i2=i2, bank=bankg: e.activation(out=sig[i2][:], in_=bank[:, 0:TN],
                                                                                    func=AF.Sigmoid),
                                     reads=[bbg], writes=[b_sig[i2]])
                                K.op(dve, lambda e, i2=i2, ch=ch, bank=banka: e.tensor_tensor(
                                    out=hdn[:, ch, 30:30 + TN], in0=bank[:, 0:TN], in1=sig[i2][:], op=ALU.mult),
                                    reads=[bba, b_sig[i2]], writes=[b_hdn[ch]])
                                yield
                    import itertools
                    _rg = ret_gen()
                    _fill = itertools.chain(proj_gen(range(4, 6)), ag_gen())
                    _nf = [0]

                    def _fill1():
                        try:
                            next(_fill)
                            _nf[0] += 1
                            return True
                        except StopIteration:
                            return False
                    for _c in range(TCH):
                        for _part in range(3):
                            if _part == 1:
                                while _nf[0] < 4 + _c:
                                    _fill1()
                            _n = 2 if _part == 2 else (0 if (_part == 1 and _c == 0) else 1)
                            for _ in range(_n):
                                _fill1()
                            next(_rg)
                    while _fill1():
                        pass
                    for _ in _rg:
                        pass
                    sum_ps = pp[:, 3, 0:TN]
                    sq_ps = pp[:, 4, 0:TN]
                    def emit_stats(ch):
                        i2 = ch % 2
                        K.op(pe, lambda e, ch=ch, i2=i2: e.matmul(sum_ps, lhsT=ones_bf, rhs=ybf[i2][:], start=(ch == 0),
                                                                  stop=(ch == 7)),
                             reads=[b_ybf[i2], b_c], writes=[b_pp[3]])
                        K.op(pe, lambda e, ch=ch, i2=i2: e.matmul(sq_ps, lhsT=ones_bf, rhs=ysq[i2][:], start=(ch == 0),
                                                                  stop=(ch == 7)),
                             reads=[b_ysq[i2], b_c], writes=[b_pp[4]])
                    def build_dg(ch):
                        cwv = convw[:].rearrange("p (w c) -> p w c", c=8)[:, :, ch]
                        dg = dg2[ch % 2]
                        b_dga, b_dgb, b_dgc = b_dg2[ch % 2]
                        K.op(dve, lambda e, cwv=cwv, dg=dg: e.tensor_tensor(
                            out=dg[:, 0:10, :], in0=ident.unsqueeze(1).broadcast_to([128, 10, 128]),
                            in1=cwv[:, 0:10].unsqueeze(2).broadcast_to([128, 10, 128]), op=ALU.mult),
                            reads=[b_c, b_convw], writes=[b_dga])
                        dg_pe = dve if t < 2 else pool
                        K.op(dg_pe, lambda e, cwv=cwv, dg=dg: e.tensor_tensor(
                            out=dg[:, 10:20, :], in0=ident.unsqueeze(1).broadcast_to([128, 10, 128]),
                            in1=cwv[:, 10:20].unsqueeze(2).broadcast_to([128, 10, 128]), op=ALU.mult),
                            reads=[b_c, b_convw], writes=[b_dgc])

                        def dgb(e, dg=dg, ch=ch):
                            ins = None
                            for w in range(20, 31):
                                ins = e.activation(out=dg[:, w, :], in_=ident, func=AF.Copy,
                                                   scale=convw[:, w * 8 + ch:w * 8 + ch + 1])
                            return ins
                        K.op(act, dgb, reads=[b_c, b_convw], writes=[b_dgb])

                    dg_pe = dve if t < 2 else pool
                    mm_set[0] = [0, 1, 2, 5, 6]
                    build_dg(0)
                    for ch in range(8):
                        dg = dg2[ch % 2]
                        b_dga, b_dgb, b_dgc = b_dg2[ch % 2]
                        bank, bb = mm_bank()

                        def cvf(e, ch=ch, bank=bank, dg=dg):
                            ins = None
                            for w in range(31):
                                ins = e.matmul(bank[:, 0:TN], lhsT=dg[:, w, :], rhs=hdn[:, ch, w:w + TN],
                                               start=(w == 0), stop=(w == 30))
                            return ins
                        K.op(pe, cvf, reads=[b_dga, b_dgb, b_dgc, b_hdn[ch]], writes=[bb])
                        if ch + 1 < 8:
                            build_dg(ch + 1)
                        if ch > 0:
                            emit_stats(ch - 1)
                        i2 = ch % 2
                        K.op(act, lambda e, ch=ch, bank=bank: e.activation(out=ysb[:, ch, :], in_=bank[:, 0:TN],
                                                                          func=AF.Identity, bias=pvec[:, ch:ch + 1]),
                             reads=[bb, b_pvec], writes=[b_ysb[ch]])
                        K.op(dg_pe, lambda e, ch=ch, i2=i2: e.tensor_copy(out=ybf[i2][:], in_=ysb[:, ch, :]),
                             reads=[b_ysb[ch]], writes=[b_ybf[i2]])
                        K.op(act, lambda e, ch=ch, i2=i2, bank=bank: e.activation(out=ysq[i2][:], in_=bank[:, 0:TN],
                                                                                 func=AF.Square, bias=pvec[:, ch:ch + 1]),
                             reads=[bb, b_pvec], writes=[b_ysq[i2]])
                    emit_stats(7)
                    K.op(act, lambda e: e.activation(out=lmean[:], in_=sum_ps, func=AF.Copy, scale=1.0 / D),
                         reads=[b_pp[3]], writes=[b_lmean])
                    K.op(dve, lambda e: e.tensor_tensor(out=lt[:], in0=lmean[:], in1=lmean[:], op=ALU.mult),
                         reads=[b_lmean], writes=[b_lt])
                    K.op(dve, lambda e: e.scalar_tensor_tensor(out=lt[:], in0=sq_ps, scalar=1.0 / D, in1=lt[:],
                                                               op0=ALU.mult, op1=ALU.subtract),
                         reads=[b_pp[4], b_lt], writes=[b_lt])
                    K.op(act, lambda e: e.activation(out=lt[:], in_=lt[:], func=AF.Sqrt, bias=EPS, scale=1.0),
                         reads=[b_lt], writes=[b_lt])
                    K.op(dve, lambda e: e.reciprocal(out=lrstd[:], in_=lt[:]), reads=[b_lt], writes=[b_lrstd])
                    for ch in range(8):
                        i2 = ch % 2
                        K.op(dve, lambda e, ch=ch, i2=i2: e.tensor_tensor(out=ld[i2][:], in0=ysb[:, ch, :], in1=lmean[:],
                                                                          op=ALU.subtract),
                             reads=[b_ysb[ch], b_lmean], writes=[b_ld[i2]])
                        K.op(aux, lambda e, i2=i2: e.tensor_tensor(out=ld[i2][:], in0=ld[i2][:], in1=lrstd[:], op=ALU.mult),
                             reads=[b_ld[i2], b_lrstd], writes=[b_ld[i2]])
                        K.op(act, lambda e, ch=ch, i2=i2: e.activation(out=ocT[:, 8 + ch, :], in_=ld[i2][:], func=AF.Silu,
                                                                       bias=pvec[:, 16 + ch:17 + ch],
                                                                       scale=pvec[:, 8 + ch:9 + ch]),
                             reads=[b_ld[i2], b_pvec], writes=[b_ocT[8 + ch]])

                    mm_set[0] = [0, 1, 2]
                    wo_banks = [[0, 1, 2], [4, 5, 6]]
                    for kp in range(2):
                        for cb in range(2):
                            wt, bw = load_w(s_wout, kp * 8, cb * 512)
                            for c in range(TCH):
                                bi = wo_banks[cb][c]

                                def mmf(e, wt=wt, c=c, kp=kp, bi=bi):
                                    ins = None
                                    for kc in range(8):
                                        ins = e.matmul(pp[:, bi, :], lhsT=ocT[:, kp * 8 + kc, c * 128:(c + 1) * 128],
                                                       rhs=wt[:, kc, :], start=(kp == 0 and kc == 0),
                                                       stop=(kp == 1 and kc == 7))
                                    return ins
                                K.op(pe, mmf, reads=[bw] + b_ocT[kp * 8:kp * 8 + 8], writes=[b_pp[bi]])
                    for cb in range(2):
                        for c in range(TCH):
                            bi = wo_banks[cb][c]
                            K.op(dve, lambda e, c=c, cb=cb, hr=hr, bi=bi: e.tensor_tensor(
                                out=hr[:, c, cb * 512:(cb + 1) * 512], in0=hr[:, c, cb * 512:(cb + 1) * 512],
                                in1=pp[:, bi, :], op=ALU.add), reads=[b_pp[bi], b_hr[c]], writes=[b_hr[c]])
                    if t == 0:
                        K.op(dve, lambda e, hr=hr: e.memset(hr[0:PAD, 0, :], 0.0), reads=[b_hr[0]], writes=[b_hr[0]])
                    mlp(X, 0, G_MLP0, aux)
                    if t + 1 < (NT if stop != 'p0t0' else 1):
                        load_x(t + 1)
                    if debug and "h0" in debug:
                        K.dma(sp, dbg["h0"][p0:p0 + TN, :].rearrange("(c p) d -> p c d", p=128), hr[:, :, :], reads=b_hr)

                    K.dma(sp, s_h[p0:p0 + TN, :].rearrange("(c p) d -> p c d", p=128), hr[:, :, :], reads=b_hr,
                          writes=[scr_buf(s_h)])
                    rmsnorm_T(X, G_MIX1)
                    qTst = B6[0][:].rearrange("p (h t) -> p h t", h=8)
                    kTst = B6[1][:].rearrange("p (h t) -> p h t", h=8)
                    vst = B6[2][:].rearrange("p (c n) -> p c n", c=TCH)
                    mm_set[0] = [0, 1, 2, 3, 4]
                    blk_ctr = [0]
                    qk_items = [(cb, j) for cb in range(4) for j in range(4)]
                    qk_state = {}

                    def qk_A(n):
                        cb, j = qk_items[n]
                        if j == 0:
                            qk_state["w"] = load_w(s_wqkv, 0, cb * 512)
                        wt, bw = qk_state["w"]
                        hp = (cb % 2) * 4 + j
                        bank, bb = mm_bank()

                        def mmf(e, wt=wt, j=j, bank=bank):
                            ins = None
                            for kc in range(8):
                                ins = e.matmul(bank[:, 0:TN], lhsT=wt[:, kc, j * 128:(j + 1) * 128], rhs=hnT[:, kc, :],
                                               start=(kc == 0), stop=(kc == 7))
                            return ins
                        K.op(pe, mmf, reads=[bw] + b_hnT, writes=[bb])
                        i2 = n % 3
                        K.op(act, lambda e, i2=i2, bank=bank: e.activation(out=sqh[i2][:], in_=bank[:, 0:TN],
                                                                          func=AF.Square),
                             reads=[bb], writes=[b_sqh[i2]])
                        qk_state[n] = (bank, bb, hp, cb < 2)

                    def qk_B(n):
                        bank, bb, hp, isq = qk_state[n]
                        dstT = qTst if isq else kTst
                        b_dst = b_B6[0] if isq else b_B6[1]
                        i2 = n % 3
                        _bi = 5 + (blk_ctr[0] % 2)
                        blk_ctr[0] += 1
                        bank2, bb2 = pp[:, _bi, :], b_pp[_bi]
                        K.op(pe, lambda e, i2=i2, bank2=bank2: e.matmul(bank2[:, 0:TN], lhsT=blk, rhs=sqh[i2][:],
                                                                        start=True, stop=True),
                             reads=[b_sqh[i2], b_c], writes=[bb2])
                        K.op(act, lambda e, i2=i2, bank2=bank2: e.activation(out=rsh[i2][:], in_=bank2[:, 0:TN],
                                                                            func=AF.Sqrt, bias=EPS, scale=1.0 / 64),
                             reads=[bb2], writes=[b_rsh[i2]])
                        K.op(dve, lambda e, i2=i2: e.reciprocal(out=rsh[i2][:], in_=rsh[i2][:]),
                             reads=[b_rsh[i2]], writes=[b_rsh[i2]])
                        gc = 0 if isq else 1
                        K.op(dve, lambda e, i2=i2, bank=bank, hp=hp, dstT=dstT, gc=gc: e.scalar_tensor_tensor(
                            out=dstT[:, hp, :], in0=bank[:, 0:TN], scalar=gqk[:, gc:gc + 1], in1=rsh[i2][:],
                            op0=ALU.mult, op1=ALU.mult), reads=[bb, b_rsh[i2], b_gqk], writes=[b_dst])

                    qk_A(0)
                    qk_A(1)
                    for n in range(16):
                        if n + 2 < 16:
                            qk_A(n + 2)
                        qk_B(n)
                    mm_set[0] = [0, 1, 2]
                    for cb in range(2):
                        wt, bw = load_w(s_wqkv, 0, 2048 + cb * 512)
                        for c in range(TCH):
                            bank, bb = mm_bank()

                            def mmf(e, wt=wt, c=c, bank=bank):
                                ins = None
                                for kc in range(8):
                                    ins = e.matmul(bank, lhsT=hnT[:, kc, c * 128:(c + 1) * 128], rhs=wt[:, kc, :],
                                                   start=(kc == 0), stop=(kc == 7))
                                return ins
                            K.op(pe, mmf, reads=[bw, b_hnT[c]], writes=[bb])
                            K.op(act, lambda e, c=c, cb=cb, bank=bank: e.activation(
                                out=vst[:, c, cb * 512:(cb + 1) * 512], in_=bank, func=AF.Copy),
                                reads=[bb], writes=[b_B6[2]])
                    K.dma(sp, s_qT[:, :, p0:p0 + TN].rearrange("h p t -> p h t"), qTst, reads=[b_B6[0]],
                          writes=[scr_buf(s_qT)])
                    K.dma(sp, s_kT[:, :, p0:p0 + TN].rearrange("h p t -> p h t"), kTst, reads=[b_B6[1]],
                          writes=[scr_buf(s_kT)])
                    K.dma(sp, s_v[p0:p0 + TN, :].rearrange("(c p) d -> p c d", p=128), vst, reads=[b_B6[2]],
                          writes=[scr_buf(s_v)])
            K.barrier()
            es_a.close()

            with ExitStack() as es2:
                oT_all = sb(es2, "oT_all", [128, 8, P], BF16)
                b_oT = [Buf(f"oT{i}") for i in range(8)]
                with ExitStack() as es2b:
                    QT = [sb(es2b, f"QT{i}", [128, P], BF16) for i in range(2)]
                    KT = [sb(es2b, f"KT{i}", [128, P], BF16) for i in range(2)]
                    Vh = [sb(es2b, f"Vh{i}", [128, NCHK, 128], BF16) for i in range(2)]
                    b_QT, b_KT, b_Vh = [Buf(), Buf()], [Buf(), Buf()], [Buf(), Buf()]
                    ee = [sb(es2b, f"ee{i}", [128, 2, TN], F32) for i in range(3)]
                    spb = [sb(es2b, f"spb{i}", [128, 2, TN], BF16) for i in range(3)]
                    tt = [sb(es2b, f"tt{i}", [128, 2, TN], F32) for i in range(2)]
                    ww = [sb(es2b, f"ww{i}", [128, 2, TN], BF16) for i in range(2)]
                    b_ee, b_spb, b_tt, b_ww = [Buf(), Buf(), Buf()], [Buf(), Buf(), Buf()], [Buf(), Buf()], [Buf(), Buf()]
                    zps = [pp[:, 0:2, :], pp[:, 2:4, :]]
                    b_z = [[b_pp[0], b_pp[1]], [b_pp[2], b_pp[3]]]
                    accps = pp[:, 4:6, :]
                    b_acc = [b_pp[4], b_pp[5]]
                    ops_ = pp[:, 6, :]
                    b_o = b_pp[6]
                    NG = NT
                    NHP = 8 if stop not in ('p01', 'p0t0') else 0
                    steps = []
                    for hp in range(NHP):
                        for G in range(NG):
                            kb_hi = TCH * G + TCH - 1
                            for kb in range(kb_hi, -1, -1):
                                steps.append((hp, G, kb, kb_hi))
                    nst = len(steps)

                    def emit_load(hp):
                        s = hp % 2
                        K.dma(sp, QT[s][:], s_qT[hp], reads=[scr_buf(s_qT)], writes=[b_QT[s]])
                        K.dma(sp, KT[s][:], s_kT[hp], reads=[scr_buf(s_kT)], writes=[b_KT[s]])
                        for c0 in range(0, NCHK, 11):
                            c1_ = min(NCHK, c0 + 11)
                            K.dma(sp, Vh[s][:, c0:c1_, :],
                                  s_v.rearrange("(c p) d -> p c d", p=128)[:, c0:c1_, hp * 128:(hp + 1) * 128],
                                  reads=[scr_buf(s_v)], writes=[b_Vh[s]])

                    def geo(j):
                        hp, G, kb, kb_hi = steps[j]
                        r = kb - TCH * G
                        q0 = max(r, 0) * 128
                        return hp, G, kb, kb_hi, r, q0, hp % 2, j % 2, j % 3, G * TN

                    def S_zf(j):
                        hp, G, kb, kb_hi, r, q0, s, i2, i3, g0 = geo(j)
                        z = zps[i2]

                        def zf(e, s=s, kb=kb, z=z, q0=q0, g0=g0):
                            ins = None
                            for h in range(2):
                                ins = e.matmul(z[:, h, q0:TN],
                                               lhsT=KT[s][h * 64:(h + 1) * 64, kb * 128:(kb + 1) * 128],
                                               rhs=QT[s][h * 64:(h + 1) * 64, g0 + q0:g0 + TN],
                                               start=True, stop=True)
                            return ins
                        K.op(pe, zf, reads=[b_KT[s], b_QT[s]], writes=b_z[i2])

                    def S_exp(j):
                        hp, G, kb, kb_hi, r, q0, s, i2, i3, g0 = geo(j)
                        z = zps[i2]
                        K.op(act, lambda e, i3=i3, z=z, q0=q0: e.activation(out=ee[i3][:, :, q0:TN], in_=z[:, :, q0:TN],
                                                                           func=AF.Exp),
                             reads=b_z[i2], writes=[b_ee[i3]])

                    def S_ln(j):
                        hp, G, kb, kb_hi, r, q0, s, i2, i3, g0 = geo(j)
                        K.op(act, lambda e, i3=i3, q0=q0: e.activation(out=spb[i3][:, :, q0:TN], in_=ee[i3][:, :, q0:TN],
                                                                      func=AF.Ln, bias=1.0, scale=1.0),
                             reads=[b_ee[i3]], writes=[b_spb[i3]])
                        if r >= 0:
                            mb = mT.unsqueeze(1).broadcast_to([128, 2, 128])
                            K.op(dve, lambda e, i3=i3, q0=q0, mb=mb: e.tensor_tensor(
                                out=ee[i3][:, :, q0:q0 + 128], in0=ee[i3][:, :, q0:q0 + 128], in1=mb, op=ALU.mult),
                                reads=[b_ee[i3], b_c], writes=[b_ee[i3]])
                            K.op(dve, lambda e, i3=i3, q0=q0, mb=mb: e.tensor_tensor(
                                out=spb[i3][:, :, q0:q0 + 128], in0=spb[i3][:, :, q0:q0 + 128], in1=mb, op=ALU.mult),
                                reads=[b_spb[i3], b_c], writes=[b_spb[i3]])
                        if kb == 0:
                            K.op(dve, lambda e, i3=i3, q0=q0: e.memset(ee[i3][0:PAD, :, q0:TN], 0.0),
                                 reads=[b_ee[i3]], writes=[b_ee[i3]])
                            K.op(dve, lambda e, i3=i3, q0=q0: e.memset(spb[i3][0:PAD, :, q0:TN], 0.0),
                                 reads=[b_spb[i3]], writes=[b_spb[i3]])

                    def S_c1(j):
                        hp, G, kb, kb_hi, r, q0, s, i2, i3, g0 = geo(j)

                        def c1(e, i3=i3, q0=q0, first=(kb == kb_hi)):
                            ins = None
                            for h in range(2):
                                ins = e.matmul(accps[:, h, q0:TN], lhsT=tri, rhs=spb[i3][:, h, q0:TN],
                                               start=first, stop=True, skip_group_check=True)
                            return ins
                        K.op(pe, c1, reads=[b_spb[i3], b_c], writes=b_acc)

                    def S_exp2(j):
                        hp, G, kb, kb_hi, r, q0, s, i2, i3, g0 = geo(j)
                        K.op(act, lambda e, i2=i2, q0=q0: e.activation(out=tt[i2][:, :, q0:TN], in_=accps[:, :, q0:TN],
                                                                      func=AF.Exp, scale=-1.0),
                             reads=b_acc, writes=[b_tt[i2]])

                    def S_c2(j):
                        hp, G, kb, kb_hi, r, q0, s, i2, i3, g0 = geo(j)
                        if kb > 0:
                            def c2(e, i3=i3, q0=q0):
                                ins = None
                                for h in range(2):
                                    ins = e.matmul(accps[:, h, q0:TN], lhsT=stri, rhs=spb[i3][:, h, q0:TN],
                                                   start=False, stop=True, skip_group_check=True)
                                return ins
                            K.op(pe, c2, reads=[b_spb[i3], b_c], writes=b_acc)

                    def S_mult(j):
                        hp, G, kb, kb_hi, r, q0, s, i2, i3, g0 = geo(j)
                        K.op(dve, lambda e, i2=i2, i3=i3, q0=q0: e.tensor_tensor(
                            out=ww[i2][:, :, q0:TN], in0=ee[i3][:, :, q0:TN], in1=tt[i2][:, :, q0:TN], op=ALU.mult),
                            reads=[b_ee[i3], b_tt[i2]], writes=[b_ww[i2]])

                    def S_wv(j):
                        hp, G, kb, kb_hi, r, q0, s, i2, i3, g0 = geo(j)

                        def wv(e, s=s, i2=i2, kb=kb, q0=q0, first=(kb == kb_hi)):
                            ins = None
                            for h in range(2):
                                ins = e.matmul(ops_[h * 64:(h + 1) * 64, q0:TN], lhsT=Vh[s][:, kb, h * 64:(h + 1) * 64],
                                               rhs=ww[i2][:, h, q0:TN], start=first, stop=True, skip_group_check=True)
                            return ins
                        K.op(pe, wv, reads=[b_Vh[s], b_ww[i2]], writes=[b_o])
                        if kb == 0:
                            K.op(dve, lambda e, hp=hp, g0=g0: e.tensor_copy(out=oT_all[:, hp, g0:g0 + TN], in_=ops_[:, 0:TN]),
                                 reads=[b_o], writes=[b_oT[hp]])
                        if (j == 0 or steps[j - 1][0] != hp) and hp + 1 < NHP:
                            emit_load(hp + 1)

                    if nst:
                        emit_load(0)
                        S_zf(0)
                        if nst > 1:
                            S_zf(1)
                        S_exp(0)
                        S_ln(0)
                        if nst > 2:
                            S_zf(2)
                        if nst > 1:
                            S_exp(1)
                            S_ln(1)
                        S_c1(0)
                    for i in range(nst):
                        if i + 2 < nst:
                            S_exp(i + 2)
                        S_exp2(i)
                        S_c2(i)
                        if i + 1 < nst:
                            S_c1(i + 1)
                        S_mult(i)
                        if i >= 1:
                            S_wv(i - 1)
                        if i + 3 < nst:
                            S_zf(i + 3)
                        if i + 2 < nst:
                            S_ln(i + 2)
                    if nst:
                        S_wv(nst - 1)
                K.barrier()

                es3 = es2.enter_context(ExitStack())
                X3 = alloc_tile(es3, "b")
                hres3 = [X3.hres, sb(es3, "hresb2", [128, TCH, D], F32)]
                b_hres3 = [X3.b_hres, [Buf(f"hres2_{c}") for c in range(TCH)]]
                for t in range((1 if stop == 'p3t0' else NT) if stop not in ('p01', 'p0t0', 'p2') else 0):
                    X3.hres, X3.b_hres = hres3[t % 2], b_hres3[t % 2]
                    hr3, b_hr3 = X3.hres, X3.b_hres
                    p0 = t * TN
                    K.dma(sp, hr3[:, :, :], s_h[p0:p0 + TN, :].rearrange("(c p) d -> p c d", p=128),
                          reads=[scr_buf(s_h)], writes=b_hr3)
                    for cb in range(2):
                        wt, bw = load_w(s_wo, 0, cb * 512)
                        for c in range(TCH):
                            def mmf(e, wt=wt, c=c, p0=p0):
                                ins = None
                                for kc in range(8):
                                    ins = e.matmul(pp[:, c, :], lhsT=oT_all[:, kc, p0 + c * 128:p0 + (c + 1) * 128],
                                                   rhs=wt[:, kc, :], start=(kc == 0), stop=(kc == 7))
                                return ins
                            K.op(pe, mmf, reads=[bw] + b_oT, writes=[b_pp[c]])
                        for c in range(TCH):
                            K.op(dve, lambda e, c=c, cb=cb, hr3=hr3: e.tensor_tensor(
                                out=hr3[:, c, cb * 512:(cb + 1) * 512], in0=hr3[:, c, cb * 512:(cb + 1) * 512],
                                in1=pp[:, c, :], op=ALU.add), reads=[b_pp[c], b_hr3[c]], writes=[b_hr3[c]])
                    mlp(X3, 1, G_MLP1, pool)
                    if t == 0:
                        K.dma(sp, out[0:(TCH - 1) * 128, :].rearrange("(c p) d -> p c d", p=128), hr3[:, 1:TCH, :],
                              reads=b_hr3[1:TCH])
                    else:
                        r0 = p0 - 128
                        K.dma(sp, out[r0:r0 + TN, :].rearrange("(c p) d -> p c d", p=128), hr3[:, :, :], reads=b_hr3)
            K.finish()
            K.emit()
    return nc


_CACHE = {}


def _get_nc(SEQ, debug=None):
    key = (SEQ, None if debug is None else tuple(sorted(debug)))
    if key not in _CACHE:
        _CACHE[key] = build(SEQ, debug)
    return _CACHE[key]


def make_in_maps(inputs, SEQ, ncores):
    P = PAD + NMETA + SEQ
    cb, cf, rot = host_consts(P)
    f = lambda a: np.ascontiguousarray(np.asarray(a, dtype=np.float32))
    shared = {
        "meta": f(inputs["meta"]),
        "norm_mix_g": f(inputs["norm_mix_g"]),
        "norm_mlp_g": f(inputs["norm_mlp_g"]),
        "even_w_in": f(inputs["even_w_in"][0]),
        "even_ret_gn_g": f(inputs["even_ret_gn_g"][0]).reshape(1024),
        "even_conv_w": f(inputs["even_conv_w"][0]),
        "even_conv_b": f(inputs["even_conv_b"][0]),
        "even_conv_ln_g": f(inputs["even_conv_ln_g"][0]),
        "even_conv_ln_b": f(inputs["even_conv_ln_b"][0]),
        "even_w_out": f(inputs["even_w_out"][0]),
        "odd_w_qkv": f(inputs["odd_w_qkv"][0]),
        "odd_q_norm_g": f(inputs["odd_q_norm_g"][0]),
        "odd_k_norm_g": f(inputs["odd_k_norm_g"][0]),
        "odd_w_o": f(inputs["odd_w_o"][0]),
        "mlp_w1_0": f(inputs["mlp_w1"][0]),
        "mlp_w1_1": f(inputs["mlp_w1"][1]),
        "mlp_w2_0": f(inputs["mlp_w2"][0]),
        "mlp_w2_1": f(inputs["mlp_w2"][1]),
        "c_bf": cb,
        "c_f": cf,
        "c_rot": rot,
    }
    xs = np.asarray(inputs["x"], dtype=np.float32)
    maps = []
    for b in range(ncores):
        m = dict(shared)
        m["x"] = np.ascontiguousarray(xs[b])
        maps.append(m)
    return maps


def kernel(**inputs):
    x = np.asarray(inputs["x"])
    B, SEQ, _ = x.shape
    nc = _get_nc(SEQ)
    maps = make_in_maps(inputs, SEQ, B)
    res = run_bass_kernel_spmd(nc, maps, core_ids=list(range(B)))
    return np.stack([np.asarray(r["out"], dtype=np.float32) for r in res.results], axis=0)
```

```python
import numpy as np
import ml_dtypes
from contextlib import ExitStack
import concourse.bass as bass
import concourse.mybir as mybir
from concourse.bass_utils import run_bass_kernel_spmd

F32 = mybir.dt.float32
BF16 = mybir.dt.bfloat16
AF = mybir.ActivationFunctionType
ALU = mybir.AluOpType
AX = mybir.AxisListType

D = 1024
NMETA = 16
PAD = 112
EPS = 1e-6
TCH = 3
TN = 128 * TCH
DFF = 4096
NCORES = 8


class Buf:
    __slots__ = ("w", "r", "name")

    def __init__(self, name=""):
        self.w = None
        self.r = {}
        self.name = name


class Producer:
    def __init__(self, name, unit):
        self.name = name
        self.unit = unit
        self.sem = None
        self.count = 0


class Eng(Producer):
    def __init__(self, name, is_pe=False):
        super().__init__(name, 1)
        self.ops = []
        self.known = {}
        self.is_pe = is_pe
        self.lanes = []
        self.lane_i = 0


class Prog:
    def __init__(self, nc, es):
        self.nc = nc
        self.es = es
        self.pe = Eng("pe", True)
        self.act = Eng("act")
        self.dve = Eng("dve")
        self.pool = Eng("pool")
        self.sp = Eng("sp")
        self.engs = [self.pe, self.act, self.dve, self.pool, self.sp]
        self.prods = list(self.engs)
        for q in (self.sp, self.pool, self.act):
            for i in range(8):
                ln = Producer(f"{q.name}_l{i}", 16)
                q.lanes.append(ln)
                self.prods.append(ln)
        for p in self.prods:
            p.sem = es.enter_context(nc.semaphore("s_" + p.name))

    def op(self, eng, thunk, reads=(), writes=(), lane=None):
        d = {}
        for b in reads:
            if b.w is not None:
                p, i = b.w
                if d.get(p, 0) < i:
                    d[p] = i
        for b in writes:
            if b.w is not None:
                p, i = b.w
                if d.get(p, 0) < i:
                    d[p] = i
            for p, i in b.r.items():
                if d.get(p, 0) < i:
                    d[p] = i
        prod = lane if lane is not None else eng
        if lane is not None and lane.count > 0:
            d[lane] = lane.count
        for p, i in d.items():
            if p is eng and eng.is_pe:
                continue
            if eng.known.get(p, 0) < i:
                eng.ops.append(("w", p, i))
                eng.known[p] = i
        prod.count += 1
        idx = prod.count
        eng.ops.append(("i", thunk, prod))
        for b in reads:
            b.r[prod] = idx
        for b in writes:
            b.w = (prod, idx)
            b.r = {}

    def dma(self, q, out, in_, reads=(), writes=()):
        lane = q.lanes[q.lane_i % len(q.lanes)]
        q.lane_i += 1
        self.op(q, lambda e, o=out, i=in_: e.dma_start(out=o, in_=i), reads, writes, lane=lane)

    def barrier(self):
        for e in self.engs:
            for p in self.prods:
                if p.count > 0 and e.known.get(p, 0) < p.count and not (p is e):
                    e.ops.append(("w", p, p.count))
                    e.known[p] = p.count

    def finish(self):
        for q in (self.sp, self.pool, self.act):
            for ln in q.lanes:
                if ln.count > 0 and q.known.get(ln, 0) < ln.count:
                    q.ops.append(("w", ln, ln.count))
                    q.known[ln] = ln.count
        for p in self.prods:
            if p.count > 0 and p is not self.sp and self.sp.known.get(p, 0) < p.count:
                self.sp.ops.append(("w", p, p.count))
                self.sp.known[p] = p.count

    def emit(self):
        nc = self.nc

        def replay(eng, e):
            for o in eng.ops:
                if o[0] == "w":
                    e.wait_ge(o[1].sem, o[2] * o[1].unit)
                else:
                    ins = o[1](e)
                    ins.then_inc(o[2].sem, o[2].unit)

        with nc.Block() as block:
            @block.sync
            def _(e):
                replay(self.sp, e)

            @block.scalar
            def _(e):
                replay(self.act, e)

            @block.vector
            def _(e):
                replay(self.dve, e)

            @block.gpsimd
            def _(e):
                replay(self.pool, e)

            @block.tensor
            def _(e):
                replay(self.pe, e)


def host_consts(P):
    nch = P // 128
    cb = np.zeros((128, 5, 128), np.float32)
    j = np.arange(128)[:, None]
    k = np.arange(128)[None, :]
    cb[:, 0] = (j == k)
    cb[:, 1] = 1.0
    cb[:, 2] = (j >= k)
    cb[:, 3] = (j < k)
    cb[:, 4] = ((j // 64) == (k // 64))
    cf = np.zeros((128, 3 * 128 + 8), np.float32)
    cf[:, 0:128] = (j == k)
    cf[:, 128:256] = (k >= j)
    cf[:, 256:384] = (j < k)
    idx = np.arange(128, dtype=np.float64)
    for h in range(4):
        lg = np.log1p(-np.exp2(-5.0 - h))
        cf[:, 384 + h] = np.exp((idx - 127.0) * lg)
        cf[:, 388 + h] = np.exp((127.0 - idx) * lg) * (128.0 ** -0.5)
    half = 64
    inv_freq = (10000.0 ** (-np.arange(half, dtype=np.float32) / half)).astype(np.float32)
    ang = np.arange(P, dtype=np.float32)[:, None] * inv_freq[None, :]
    cos = np.cos(ang).astype(np.float32)
    sin = np.sin(ang).astype(np.float32)
    rot = np.zeros((P, 2, 128), np.float32)
    rot[:, 0, :64] = cos
    rot[:, 0, 64:] = cos
    rot[:, 1, :64] = -sin
    rot[:, 1, 64:] = sin
    return cb.astype(ml_dtypes.bfloat16), cf, rot


def build(SEQ, debug=None, stop=None):
    P = PAD + NMETA + SEQ
    NCHK = P // 128
    assert NCHK % TCH == 0
    NT = NCHK // TCH
    nc = bass.Bass("TRN2", target_bir_lowering=False)

    def din(name, shape, dt=F32):
        return nc.dram_tensor(name, list(shape), dt, kind="ExternalInput").ap()

    def dscr(name, shape, dt):
        return nc.dram_tensor(name, list(shape), dt, kind="Internal").ap()

    x = din("x", [SEQ, D])
    meta = din("meta", [NMETA, D])
    norm_mix_g = din("norm_mix_g", [2, D])
    norm_mlp_g = din("norm_mlp_g", [2, D])
    w_in = din("even_w_in", [D, 5120])
    gn_g = din("even_ret_gn_g", [1024])
    conv_w = din("even_conv_w", [31, D])
    conv_b = din("even_conv_b", [D])
    ln_g = din("even_conv_ln_g", [D])
    ln_b = din("even_conv_ln_b", [D])
    w_out = din("even_w_out", [2048, D])
    w_qkv = din("odd_w_qkv", [D, 3072])
    qn_g = din("odd_q_norm_g", [64])
    kn_g = din("odd_k_norm_g", [64])
    w_o = din("odd_w_o", [D, D])
    w1 = [din("mlp_w1_0", [D, DFF]), din("mlp_w1_1", [D, DFF])]
    w2 = [din("mlp_w2_0", [DFF, D]), din("mlp_w2_1", [DFF, D])]
    c_bf = din("c_bf", [128, 5, 128], BF16)
    c_f = din("c_f", [128, 392])
    c_rot = din("c_rot", [P, 2, 128])
    out = nc.dram_tensor("out", [SEQ, D], F32, kind="ExternalOutput").ap()
    dbg = None
    if debug:
        dbg = {k: nc.dram_tensor("dbg_" + k, list(s), F32, kind="ExternalOutput").ap() for k, s in debug.items()}

    s_win = dscr("s_win", [D, 5120], BF16)
    s_wout = dscr("s_wout", [2048, D], BF16)
    s_w1 = [dscr("s_w1_0", [D, DFF], BF16), dscr("s_w1_1", [D, DFF], BF16)]
    s_w2 = [dscr("s_w2_0", [DFF, D], BF16), dscr("s_w2_1", [DFF, D], BF16)]
    s_wqkv = dscr("s_wqkv", [D, 3072], BF16)
    s_wo = dscr("s_wo", [D, D], BF16)
    s_qT = dscr("s_qT", [8, 128, P], BF16)
    s_kT = dscr("s_kT", [8, 128, P], BF16)
    s_v = dscr("s_v", [P, D], BF16)
    s_h = dscr("s_h", [P, D], F32)

    with ExitStack() as es:
        K = Prog(nc, es)
        pe, act, dve, pool, sp = K.pe, K.act, K.dve, K.pool, K.sp

        def sb(stack, name, shape, dt):
            return stack.enter_context(nc.sbuf_tensor(name, list(shape), dt))

        pp8 = es.enter_context(nc.psum_tensor("pp8", [128, 8, 512], F32))
        pp = pp8[:, 0:7, :]
        tp = pp8[:, 7, :].bitcast(BF16)
        b_tp = Buf("tp")
        b_pp = [Buf(f"pp{i}") for i in range(7)]

        cbf = sb(es, "cbf", [128, 5, 128], BF16)
        cf = sb(es, "cf", [128, 392], F32)
        b_c = Buf("consts")
        K.dma(sp, cbf[:], c_bf[:, :, :], writes=[b_c])
        K.dma(sp, cf[:], c_f[:, :], writes=[b_c])
        ident = cbf[:, 0, :]
        ones_bf = cbf[:, 1, :]
        tri = cbf[:, 2, :]
        stri = cbf[:, 3, :]
        blk = cbf[:, 4, :]
        ident_f = cf[:, 0:128]
        maskT = cf[:, 128:256]
        mT = cf[:, 256:384]
        dq = cf[:, 384:388]
        dk = cf[:, 388:392]

        vstage = sb(es, "vstage", [128, 128], F32)
        vstage2 = sb(es, "vstage2", [128, 128], F32)
        convw = sb(es, "convw", [128, 248], F32)
        pvec = sb(es, "pvec", [128, 64], F32)
        gqk = sb(es, "gqk", [128, 2], F32)
        gng_b = sb(es, "gng_b", [128, 1024], F32)
        b_vs, b_vs2, b_convw, b_pvec, b_gqk, b_gng = Buf(), Buf(), Buf(), Buf(), Buf(), Buf()
        K.dma(sp, gng_b[:], gn_g.partition_broadcast(128), writes=[b_gng])
        cw = conv_w.rearrange("w (c p) -> (w c) p", p=128)
        K.dma(sp, vstage[0:124, :], cw[0:124, :], writes=[b_vs])
        K.dma(sp, vstage2[0:124, :], cw[124:248, :], writes=[b_vs2])
        ptf = pp[:, 0, :]
        K.op(pe, lambda e: e.transpose(out=ptf[:, 0:124], in_=vstage[0:124, :], identity=ident_f[0:124, 0:124]),
             reads=[b_vs, b_c], writes=[b_pp[0]])
        K.op(pe, lambda e: e.transpose(out=ptf[:, 124:248], in_=vstage2[0:124, :], identity=ident_f[0:124, 0:124]),
             reads=[b_vs2, b_c], writes=[b_pp[0]])
        K.op(dve, lambda e: e.tensor_copy(out=convw[:], in_=ptf[:, 0:248]), reads=[b_pp[0]], writes=[b_convw])
        vecs = [conv_b, ln_g, ln_b, norm_mix_g[0], norm_mlp_g[0], norm_mix_g[1], norm_mlp_g[1]]
        for i, v in enumerate(vecs):
            K.dma(sp, vstage[8 * i:8 * i + 8, :], v.rearrange("(c p) -> c p", p=128), writes=[b_vs])
        K.op(pe, lambda e: e.transpose(out=pp[:, 1, 0:56], in_=vstage[0:56, :], identity=ident_f[0:56, 0:56]),
             reads=[b_vs, b_c], writes=[b_pp[1]])
        K.op(dve, lambda e: e.tensor_copy(out=pvec[:, 0:56], in_=pp[:, 1, 0:56]), reads=[b_pp[1]], writes=[b_pvec])
        for hh in range(2):
            K.dma(sp, gqk[hh * 64:(hh + 1) * 64, 0:1], qn_g.rearrange("(p o) -> p o", o=1), writes=[b_gqk])
            K.dma(sp, gqk[hh * 64:(hh + 1) * 64, 1:2], kn_g.rearrange("(p o) -> p o", o=1), writes=[b_gqk])
        K.op(dve, lambda e: e.tensor_scalar(out=gqk[:, 0:1], in0=gqk[:, 0:1], scalar1=0.125, scalar2=None, op0=ALU.mult),
             reads=[b_gqk], writes=[b_gqk])
        G_MIX0, G_MLP0, G_MIX1, G_MLP1 = 24, 32, 40, 48

        NSLOT = 3
        wslot = [sb(es, f"wslot{i}", [128, 8, 512], BF16) for i in range(NSLOT)]
        b_wslot = [Buf(f"wslot{i}") for i in range(NSLOT)]
        wctr = [0]
        b_scr = {}

        def scr_buf(t):
            return b_scr.setdefault(id(t), Buf("scr"))

        def load_w(ws, k0, c0, nk=8):
            i = wctr[0] % NSLOT
            wctr[0] += 1
            src = ws.rearrange("(kc p) n -> p kc n", p=128)[:, k0:k0 + nk, c0:c0 + 512]
            K.dma(sp, wslot[i][:, 0:nk, :], src, reads=[scr_buf(ws)], writes=[b_wslot[i]])
            return wslot[i], b_wslot[i]

        class _NS:
            pass

        def alloc_tile(stack, tag):
            X = _NS()
            X.a1T = sb(stack, "a1T" + tag, [128, 32, TN], BF16)
            X.b_a1T = Buf("a1T")
            X.hres = sb(stack, "hres" + tag, [128, TCH, D], F32)
            X.b_hres = [Buf(f"hres{c}") for c in range(TCH)]
            X.hnT = sb(stack, "hnT" + tag, [128, 8, TN], BF16)
            X.b_hnT = [Buf(f"hnT{c}") for c in range(TCH)]
            X.hn_bf = [sb(stack, f"hn_bf{i}" + tag, [128, D], BF16) for i in range(2)]
            X.b_hn_bf = [Buf(), Buf()]
            X.junk = sb(stack, "junk" + tag, [128, D], BF16)
            X.b_junk = Buf()
            X.nstat = sb(stack, "nstat" + tag, [128, 8], F32)
            X.b_nstat = Buf()
            X.rr = [sb(stack, f"rr{i}" + tag, [128, TN], F32) for i in range(2)]
            X.b_rr = [Buf(), Buf()]
            X.tp, X.b_tp = tp, b_tp
            return X

        with ExitStack() as es_a:
            def convert(wsrc, wdst):
                Kr, N = wsrc.shape
                rows = 128 * max(1, 1024 // N)
                for r0 in range(0, Kr, rows):
                    lane = pool.lanes[pool.lane_i % len(pool.lanes)]
                    pool.lane_i += 1
                    K.op(pool, lambda e, o=wdst[r0:r0 + rows, :], i=wsrc[r0:r0 + rows, :]: e.dma_start(
                        out=o, in_=i, max_dma_last_dim=4096), writes=[scr_buf(wdst)], lane=lane)

            convert(w_in, s_win)
            convert(w_out, s_wout)
            convert(w1[0], s_w1[0])
            convert(w2[0], s_w2[0])
            convert(w_qkv, s_wqkv)
            convert(w_o, s_wo)
            convert(w1[1], s_w1[1])
            convert(w2[1], s_w2[1])

            X = alloc_tile(es_a, "a")
            hnT, b_hnT = X.hnT, X.b_hnT
            hres_l = [X.hres, sb(es_a, "hresa2", [128, TCH, D], F32)]
            b_hres_l = [X.b_hres, [Buf(f"hresa2_{c}") for c in range(TCH)]]
            mmctr = [0]
            hnctr = [0]

            mm_set = [[0, 1, 2]]

            def mm_bank():
                st = mm_set[0]
                i = st[mmctr[0] % len(st)]
                mmctr[0] += 1
                return pp[:, i, :], b_pp[i]

            def rmsnorm_T_gen(X, gcol):
                hr, b_hr = X.hres, X.b_hres
                for c in range(TCH):
                    K.op(act, lambda e, c=c: e.activation(out=X.junk[:], in_=hr[:, c, :], func=AF.Square,
                                                          accum_out=X.nstat[:, c:c + 1]),
                         reads=[b_hr[c]], writes=[X.b_junk, X.b_nstat])
                K.op(act, lambda e: e.activation(out=X.nstat[:, 4:4 + TCH], in_=X.nstat[:, 0:TCH], func=AF.Sqrt,
                                                 bias=EPS, scale=1.0 / D),
                     reads=[X.b_nstat], writes=[X.b_nstat])
                K.op(dve, lambda e: e.reciprocal(out=X.nstat[:, 0:TCH], in_=X.nstat[:, 4:4 + TCH]),
                     reads=[X.b_nstat], writes=[X.b_nstat])
                for c in range(TCH):
                    i = hnctr[0] % 2
                    hnctr[0] += 1
                    K.op(dve, lambda e, c=c, i=i: e.tensor_scalar(out=X.hn_bf[i][:], in0=hr[:, c, :],
                                                                   scalar1=X.nstat[:, c:c + 1], scalar2=None, op0=ALU.mult),
                         reads=[b_hr[c], X.b_nstat], writes=[X.b_hn_bf[i]])

                    def tr(e, i=i):
                        ins = None
                        for kc in range(8):
                            ins = e.transpose(out=X.tp[:, kc * 128:(kc + 1) * 128], in_=X.hn_bf[i][:, kc * 128:(kc + 1) * 128],
                                              identity=ident)
                        return ins
                    K.op(pe, tr, reads=[X.b_hn_bf[i], b_c], writes=[X.b_tp])
                    gbc = pvec[:, gcol:gcol + 8].unsqueeze(2).broadcast_to([128, 8, 128])
                    K.op(dve, lambda e, c=c, gbc=gbc: e.tensor_tensor(
                        out=X.hnT[:, :, c * 128:(c + 1) * 128], in0=X.tp.rearrange("p (k t) -> p k t", k=8), in1=gbc,
                        op=ALU.mult), reads=[X.b_tp, b_pvec], writes=[X.b_hnT[c]])
                    yield

            def rmsnorm_T(X, gcol):
                for _ in rmsnorm_T_gen(X, gcol):
                    pass

            def mlp(X, layer, gcol, aux):
                for _ in mlp_gen(X, layer, gcol, aux):
                    pass

            def mlp_gen(X, layer, gcol, aux, w1_banks=(3, 4, 5, 6, 0, 1, 2)):
                hr, b_hr = X.hres, X.b_hres
                yield from rmsnorm_T_gen(X, gcol)
                mm_set[0] = list(w1_banks)
                for cb in range(8):
                    wt, bw = load_w(s_w1[layer], 0, cb * 512)
                    for j in range(4):
                        hb = cb * 4 + j
                        bank, bb = mm_bank()

                        def mmf(e, wt=wt, j=j, bank=bank):
                            ins = None
                            for kc in range(8):
                                ins = e.matmul(bank[:, 0:TN], lhsT=wt[:, kc, j * 128:(j + 1) * 128], rhs=X.hnT[:, kc, :],
                                               start=(kc == 0), stop=(kc == 7))
                            return ins
                        K.op(pe, mmf, reads=[bw] + X.b_hnT, writes=[bb])
                        i = hb % 2
                        K.op(act, lambda e, i=i, bank=bank: e.activation(out=X.rr[i][:], in_=bank[:, 0:TN], func=AF.Relu),
                             reads=[bb], writes=[X.b_rr[i]])
                        K.op(aux, lambda e, i=i, hb=hb: e.tensor_tensor(out=X.a1T[:, hb, :], in0=X.rr[i][:], in1=X.rr[i][:],
                                                                         op=ALU.mult),
                             reads=[X.b_rr[i]], writes=[X.b_a1T])
                        yield
                mm_set[0] = [0, 1, 2]
                for cb in range(2):
                    for kp in range(4):
                        wt, bw = load_w(s_w2[layer], kp * 8, cb * 512)
                        for c in range(TCH):
                            def mmf(e, wt=wt, c=c, kp=kp):
                                ins = None
                                for kc in range(8):
                                    ins = e.matmul(pp[:, c, :], lhsT=X.a1T[:, kp * 8 + kc, c * 128:(c + 1) * 128],
                                                   rhs=wt[:, kc, :], start=(kp == 0 and kc == 0),
                                                   stop=(kp == 3 and kc == 7))
                                return ins
                            K.op(pe, mmf, reads=[bw, X.b_a1T], writes=[b_pp[c]])
                            yield
                    for c in range(TCH):
                        K.op(dve, lambda e, c=c, cb=cb: e.tensor_tensor(
                            out=hr[:, c, cb * 512:(cb + 1) * 512], in0=hr[:, c, cb * 512:(cb + 1) * 512],
                            in1=pp[:, c, :], op=ALU.add), reads=[b_pp[c], b_hr[c]], writes=[b_hr[c]])

            with ExitStack() as es0:
                B6 = [sb(es0, f"B6_{i}", [128, TCH * D], BF16) for i in range(3)]
                b_B6 = [Buf(f"B6_{i}") for i in range(3)]
                qkd = B6[0][:].rearrange("p (c n) -> p c n", c=TCH)
                vb = B6[1][:].rearrange("p (c n) -> p c n", c=TCH)
                sgb = B6[2][:].rearrange("p (c n) -> p c n", c=TCH)
                b_qkd, b_vb, b_sg = b_B6
                rot = sb(es0, "rot", [128, TCH, 2, 128], F32)
                b_rot = Buf()
                tA = sb(es0, "tA", [128, 512], F32)
                tB = sb(es0, "tB", [128, 512], F32)
                tR = sb(es0, "tR", [128, 512], F32)
                b_tA, b_tB, b_tR = Buf(), Buf(), Buf()
                qkT = [sb(es0, f"qkT{i}", [128, 8, 128], BF16) for i in range(2)]
                b_qkT = [Buf(), Buf()]
                scT = [sb(es0, f"scT{i}", [128, 4, 128], BF16) for i in range(2)]
                b_scT = [Buf(), Buf()]
                Tst = sb(es0, "Tst", [128, 1024], F32)
                Sb = sb(es0, "Sb", [128, 1024], BF16)
                b_Tst, b_Sb = Buf(), Buf()
                gst = sb(es0, "gst", [128, 4, 6], F32)
                gmv = sb(es0, "gmv", [128, 4, 2], F32)
                grs = sb(es0, "grs", [128, 8], F32)
                b_gst, b_gmv, b_grs = Buf(), Buf(), Buf()
                tmpn = sb(es0, "tmpn", [128, 1024], F32)
                b_tmpn = Buf()
                og = sb(es0, "og", [128, 1024], BF16)
                b_og = Buf()
                ocT = sb(es0, "ocT", [128, 16, TN], BF16)
                b_ocT = [Buf(f"ocT{i}") for i in range(16)]
                hdn = sb(es0, "hdn", [128, 8, 30 + TN], BF16)
                b_hdn = [Buf(f"hdn{i}") for i in range(8)]
                dg2 = [sb(es0, f"dg{i}", [128, 31, 128], BF16) for i in range(2)]
                b_dg2 = [[Buf(), Buf(), Buf()], [Buf(), Buf(), Buf()]]
                ysb = sb(es0, "ysb", [128, 8, TN], F32)
                b_ysb = [Buf() for _ in range(8)]
                ybf = [sb(es0, f"ybf{i}", [128, TN], BF16) for i in range(2)]
                ysq = [sb(es0, f"ysq{i}", [128, TN], BF16) for i in range(2)]
                b_ybf, b_ysq = [Buf(), Buf()], [Buf(), Buf()]
                sig = [sb(es0, f"sig{i}", [128, TN], F32) for i in range(2)]
                b_sig = [Buf(), Buf()]
                lmean = sb(es0, "lmean", [128, TN], F32)
                lrstd = sb(es0, "lrstd", [128, TN], F32)
                lt = sb(es0, "lt", [128, TN], F32)
                b_lmean, b_lrstd, b_lt = Buf(), Buf(), Buf()
                ld = [sb(es0, f"ld{i}", [128, TN], F32) for i in range(2)]
                b_ld = [Buf(), Buf()]
                sqh = [sb(es0, f"sqh{i}", [128, TN], BF16) for i in range(3)]
                b_sqh = [Buf(), Buf(), Buf()]
                rsh = [sb(es0, f"rsh{i}", [128, TN], F32) for i in range(3)]
                b_rsh = [Buf(), Buf(), Buf()]

                K.op(dve, lambda e: e.memset(Tst[:], 0.0), writes=[b_Tst])
                K.op(dve, lambda e: e.memset(Sb[:], 0.0), writes=[b_Sb])
                K.op(dve, lambda e: e.memset(hdn[:, :, 0:30], 0.0), writes=b_hdn)
                cgam = [float(np.exp(128.0 * np.log1p(-np.exp2(-5.0 - h)))) for h in range(4)]

                def load_x(tt_):
                    hrx, b_hrx = hres_l[tt_ % 2], b_hres_l[tt_ % 2]
                    if tt_ == 0:
                        K.op(dve, lambda e, hrx=hrx: e.memset(hrx[:, 0, :], 0.0), writes=[b_hrx[0]])
                        K.dma(sp, hrx[PAD:128, 0, :], meta[:, :], writes=[b_hrx[0]])
                        K.dma(sp, hrx[:, 1:TCH, :], x[0:(TCH - 1) * 128, :].rearrange("(c p) d -> p c d", p=128),
                              writes=b_hrx[1:TCH])
                    else:
                        r0 = tt_ * TN - 128
                        K.dma(sp, hrx[:, :, :], x[r0:r0 + TN, :].rearrange("(c p) d -> p c d", p=128), writes=b_hrx)

                for t in range(NT if stop != 'p0t0' else 1):
                    aux = dve
                    X.hres, X.b_hres = hres_l[t % 2], b_hres_l[t % 2]
                    hr, b_hr = X.hres, X.b_hres
                    p0 = t * TN
                    if t == 0:
                        load_x(0)
                    K.dma(sp, rot[:], c_rot[p0:p0 + TN].rearrange("(c p) a d -> p c a d", p=128), writes=[b_rot])

                    rmsnorm_T(X, G_MIX0)
                    def proj_gen(cbs):
                        for cb in cbs:
                            wt, bw = load_w(s_win, 0, cb * 512)
                            for c in range(TCH):
                                bank, bb = mm_bank()

                                def mmf(e, wt=wt, c=c, bank=bank):
                                    ins = None
                                    for kc in range(8):
                                        ins = e.matmul(bank, lhsT=hnT[:, kc, c * 128:(c + 1) * 128], rhs=wt[:, kc, :],
                                                       start=(kc == 0), stop=(kc == 7))
                                    return ins
                                K.op(pe, mmf, reads=[bw, b_hnT[c]], writes=[bb])
                                if cb < 2:
                                    dec = dq if cb == 0 else dk
                                    b3 = bank.rearrange("p (h d) -> p h d", h=4)
                                    cosb = rot[:, c, 0, :].unsqueeze(1).broadcast_to([128, 4, 128])
                                    K.op(dve, lambda e, b3=b3, cosb=cosb: e.tensor_tensor(
                                        out=tA[:].rearrange("p (h d) -> p h d", h=4), in0=b3, in1=cosb, op=ALU.mult),
                                        reads=[bb, b_rot], writes=[b_tA])
                                    s1 = rot[:, c, 1, 0:64].unsqueeze(1).broadcast_to([128, 4, 64])
                                    s2 = rot[:, c, 1, 64:128].unsqueeze(1).broadcast_to([128, 4, 64])
                                    tB3 = tB[:].rearrange("p (h d) -> p h d", h=4)

                                    def rotB(e, b3=b3, s1=s1, s2=s2, tB3=tB3):
                                        e.tensor_tensor(out=tB3[:, :, 0:64], in0=b3[:, :, 64:128], in1=s1, op=ALU.mult)
                                        return e.tensor_tensor(out=tB3[:, :, 64:128], in0=b3[:, :, 0:64], in1=s2, op=ALU.mult)
                                    K.op(dve, rotB, reads=[bb, b_rot], writes=[b_tB])
                                    K.op(aux, lambda e: e.tensor_tensor(out=tR[:], in0=tA[:], in1=tB[:], op=ALU.add),
                                         reads=[b_tA, b_tB], writes=[b_tR])

                                    def decf(e, c=c, cb=cb, dec=dec):
                                        ins = None
                                        for h in range(4):
                                            ins = e.activation(out=qkd[:, c, cb * 512 + h * 128: cb * 512 + (h + 1) * 128],
                                                               in_=tR[:, h * 128:(h + 1) * 128], func=AF.Copy,
                                                               scale=dec[:, h:h + 1])
                                        return ins
                                    K.op(act, decf, reads=[b_tR, b_c], writes=[b_qkd])
                                elif cb < 4:
                                    K.op(act, lambda e, c=c, cb=cb, bank=bank: e.activation(
                                        out=vb[:, c, (cb - 2) * 512:(cb - 1) * 512], in_=bank, func=AF.Copy),
                                        reads=[bb], writes=[b_vb])
                                else:
                                    K.op(act, lambda e, c=c, cb=cb, bank=bank: e.activation(
                                        out=sgb[:, c, (cb - 4) * 512:(cb - 3) * 512], in_=bank, func=AF.Silu),
                                        reads=[bb], writes=[b_sg])
                                    K.op(dve if t < 2 else pool, lambda e, c=c, cb=cb: e.tensor_tensor(
                                        out=sgb[:, c, (cb - 4) * 512:(cb - 3) * 512],
                                        in0=sgb[:, c, (cb - 4) * 512:(cb - 3) * 512],
                                        in1=gng_b[:, (cb - 4) * 512:(cb - 3) * 512], op=ALU.mult),
                                        reads=[b_sg, b_gng], writes=[b_sg])
                                yield


                    mm_set[0] = [0, 1, 2, 3, 4, 5, 6]
                    for _ in proj_gen(range(4)):
                        pass
                    mm_set[0] = [0, 1, 2]

                    def ret_gen():
                        for c in range(TCH):
                            i2 = c % 2

                            def trq(e, c=c):
                                ins = None
                                for kc in range(8):
                                    ins = e.transpose(out=tp[:, kc * 128:(kc + 1) * 128], in_=qkd[:, c, kc * 128:(kc + 1) * 128],
                                                      identity=ident)
                                return ins
                            K.op(pe, trq, reads=[b_qkd, b_c], writes=[b_tp])
                            K.op(act, lambda e, i2=i2: e.activation(out=qkT[i2][:], in_=tp[:].rearrange("p (k t) -> p k t", k=8),
                                                                    func=AF.Copy), reads=[b_tp], writes=[b_qkT[i2]])
                            bank, bb = mm_bank()

                            def scf(e, i2=i2, bank=bank):
                                ins = None
                                for h in range(4):
                                    ins = e.matmul(bank[:, h * 128:(h + 1) * 128], lhsT=qkT[i2][:, 4 + h, :],
                                                   rhs=qkT[i2][:, h, :], start=True, stop=True)
                                return ins
                            K.op(pe, scf, reads=[b_qkT[i2]], writes=[bb])
                            mb = maskT.unsqueeze(1).broadcast_to([128, 4, 128])
                            K.op(dve, lambda e, i2=i2, bank=bank, mb=mb: e.tensor_tensor(
                                out=scT[i2][:], in0=bank.rearrange("p (h t) -> p h t", h=4), in1=mb, op=ALU.mult),
                                reads=[bb, b_c], writes=[b_scT[i2]])
                            yield
                            o_ps = pp[:, 3:5, :].rearrange("p a b -> p (a b)")
                            kv_ps = pp[:, 5:7, :].rearrange("p a b -> p (a b)")

                            def of(e, i2=i2, c=c):
                                ins = None
                                for h in range(4):
                                    e.matmul(o_ps[:, h * 256:(h + 1) * 256], lhsT=scT[i2][:, h, :],
                                             rhs=vb[:, c, h * 256:(h + 1) * 256], start=True, stop=False)
                                    ins = e.matmul(o_ps[:, h * 256:(h + 1) * 256], lhsT=qkT[i2][:, h, :],
                                                   rhs=Sb[:, h * 256:(h + 1) * 256], start=False, stop=True)
                                return ins
                            K.op(pe, of, reads=[b_scT[i2], b_vb, b_qkT[i2], b_Sb], writes=[b_pp[3], b_pp[4]])

                            def kvf(e, c=c):
                                ins = None
                                for h in range(4):
                                    ins = e.matmul(kv_ps[:, h * 256:(h + 1) * 256],
                                                   lhsT=qkd[:, c, 512 + h * 128: 512 + (h + 1) * 128],
                                                   rhs=vb[:, c, h * 256:(h + 1) * 256], start=True, stop=True)
                                return ins
                            K.op(pe, kvf, reads=[b_qkd, b_vb], writes=[b_pp[5], b_pp[6]])

                            def tupd(e):
                                ins = None
                                for h in range(4):
                                    ins = e.scalar_tensor_tensor(out=Tst[:, h * 256:(h + 1) * 256], in0=Tst[:, h * 256:(h + 1) * 256],
                                                                 scalar=cgam[h], in1=kv_ps[:, h * 256:(h + 1) * 256],
                                                                 op0=ALU.mult, op1=ALU.add)
                                return ins
                            K.op(dve, tupd, reads=[b_pp[5], b_pp[6], b_Tst], writes=[b_Tst])

                            def sbf(e):
                                ins = None
                                for h in range(4):
                                    ins = e.activation(out=Sb[:, h * 256:(h + 1) * 256], in_=Tst[:, h * 256:(h + 1) * 256],
                                                       func=AF.Copy, scale=cgam[h])
                                return ins
                            K.op(act, sbf, reads=[b_Tst], writes=[b_Sb])

                            def gstf(e):
                                ins = None
                                for h in range(4):
                                    ins = e.bn_stats(out=gst[:, h, :], in_=o_ps[:, h * 256:(h + 1) * 256])
                                return ins
                            K.op(dve, gstf, reads=[b_pp[3], b_pp[4]], writes=[b_gst])

                            def gagf(e):
                                ins = None
                                for h in range(4):
                                    ins = e.bn_aggr(out=gmv[:, h, :], in_=gst[:, h, :])
                                return ins
                            K.op(dve, gagf, reads=[b_gst], writes=[b_gmv])
                            K.op(act, lambda e: e.activation(out=grs[:, 4:8], in_=gmv[:, :, 1], func=AF.Sqrt, bias=EPS, scale=1.0),
                                 reads=[b_gmv], writes=[b_grs])
                            K.op(dve, lambda e: e.reciprocal(out=grs[:, 0:4], in_=grs[:, 4:8]), reads=[b_grs], writes=[b_grs])

                            def gnf(e):
                                ins = None
                                for h in range(4):
                                    ins = e.tensor_scalar(out=tmpn[:, h * 256:(h + 1) * 256], in0=o_ps[:, h * 256:(h + 1) * 256],
                                                          scalar1=gmv[:, h, 0:1], scalar2=grs[:, h:h + 1],
                                                          op0=ALU.subtract, op1=ALU.mult)
                                return ins
                            K.op(dve, gnf, reads=[b_pp[3], b_pp[4], b_gmv, b_grs], writes=[b_tmpn])
                            K.op(aux, lambda e, c=c: e.tensor_tensor(out=og[:], in0=tmpn[:], in1=sgb[:, c, :], op=ALU.mult),
                                 reads=[b_tmpn, b_sg], writes=[b_og])
                            yield

                            def tro(e):
                                ins = None
                                for kc in range(8):
                                    ins = e.transpose(out=tp[:, kc * 128:(kc + 1) * 128], in_=og[:, kc * 128:(kc + 1) * 128],
                                                      identity=ident)
                                return ins
                            K.op(pe, tro, reads=[b_og, b_c], writes=[b_tp])
                            K.op(act, lambda e, c=c: e.activation(out=ocT[:, 0:8, c * 128:(c + 1) * 128],
                                                                   in_=tp[:].rearrange("p (k t) -> p k t", k=8), func=AF.Copy),
                                 reads=[b_tp], writes=b_ocT[0:8])

                            yield

                    if t > 0:
                        K.op(aux, lambda e: e.tensor_copy(out=hdn[:, :, 0:30], in_=hdn[:, :, TN:TN + 30]),
                             reads=b_hdn, writes=b_hdn)
                    def ag_gen():
                        for j in range(2):
                            wa, bwa = load_w(s_win, 0, 3072 + j * 512)
                            wg, bwg = load_w(s_win, 0, 4096 + j * 512)
                            for cc in range(4):
                                ch = j * 4 + cc
                                banka, bba = mm_bank()
                                bankg, bbg = mm_bank()

                                def mma(e, w=wa, cc=cc, bank=banka):
                                    ins = None
                                    for kc in range(8):
                                        ins = e.matmul(bank[:, 0:TN], lhsT=w[:, kc, cc * 128:(cc + 1) * 128], rhs=hnT[:, kc, :],
                                                       start=(kc == 0), stop=(kc == 7))
                                    return ins
                                K.op(pe, mma, reads=[bwa] + b_hnT, writes=[bba])

                                def mmg(e, w=wg, cc=cc, bank=bankg):
                                    ins = None
                                    for kc in range(8):
                                        ins = e.matmul(bank[:, 0:TN], lhsT=w[:, kc, cc * 128:(cc + 1) * 128], rhs=hnT[:, kc, :],
                                                       start=(kc == 0), stop=(kc == 7))
                                    return ins
                                K.op(pe, mmg, reads=[bwg] + b_hnT, writes=[bbg])
                                i2 = ch % 2
                                K.op(act, lambda e, i2=i2, bank=bankg: e.activation(out=sig[i2][:], in_=bank[:, 0:TN],
                                                                                    func=AF.Sigmoid),
                                     reads=[bbg], writes=[b_sig[i2]])
                                K.op(dve, lambda e, i2=i2, ch=ch, bank=banka: e.tensor_tensor(
                                    out=hdn[:, ch, 30:30 + TN], in0=bank[:, 0:TN], in1=sig[i2][:], op=ALU.mult),
                                    reads=[bba, b_sig[i2]], writes=[b_hdn[ch]])
                                yield
                    import itertools
                    _rg = ret_gen()
                    _fill = itertools.chain(proj_gen(range(4, 6)), ag_gen())
                    _nf = [0]

                    def _fill1():
                        try:
                            next(_fill)
                            _nf[0] += 1
                            return True
                        except StopIteration:
                            return False
                    for _c in range(TCH):
                        for _part in range(3):
                            if _part == 1:
                                while _nf[0] < 4 + _c:
                                    _fill1()
                            _fill1()
                            next(_rg)
                    while _fill1():
                        pass
                    for _ in _rg:
                        pass
                    sum_ps = pp[:, 3, 0:TN]
                    sq_ps = pp[:, 4, 0:TN]
                    def emit_stats(ch):
                        i2 = ch % 2
                        K.op(pe, lambda e, ch=ch, i2=i2: e.matmul(sum_ps, lhsT=ones_bf, rhs=ybf[i2][:], start=(ch == 0),
                                                                  stop=(ch == 7)),
                             reads=[b_ybf[i2], b_c], writes=[b_pp[3]])
                        K.op(pe, lambda e, ch=ch, i2=i2: e.matmul(sq_ps, lhsT=ones_bf, rhs=ysq[i2][:], start=(ch == 0),
                                                                  stop=(ch == 7)),
                             reads=[b_ysq[i2], b_c], writes=[b_pp[4]])
                    def build_dg(ch):
                        cwv = convw[:].rearrange("p (w c) -> p w c", c=8)[:, :, ch]
                        dg = dg2[ch % 2]
                        b_dga, b_dgb, b_dgc = b_dg2[ch % 2]
                        K.op(dve, lambda e, cwv=cwv, dg=dg: e.tensor_tensor(
                            out=dg[:, 0:10, :], in0=ident.unsqueeze(1).broadcast_to([128, 10, 128]),
                            in1=cwv[:, 0:10].unsqueeze(2).broadcast_to([128, 10, 128]), op=ALU.mult),
                            reads=[b_c, b_convw], writes=[b_dga])
                        dg_pe = dve if t < 2 else pool
                        K.op(dg_pe, lambda e, cwv=cwv, dg=dg: e.tensor_tensor(
                            out=dg[:, 10:20, :], in0=ident.unsqueeze(1).broadcast_to([128, 10, 128]),
                            in1=cwv[:, 10:20].unsqueeze(2).broadcast_to([128, 10, 128]), op=ALU.mult),
                            reads=[b_c, b_convw], writes=[b_dgc])

                        def dgb(e, dg=dg, ch=ch):
                            ins = None
                            for w in range(20, 31):
                                ins = e.activation(out=dg[:, w, :], in_=ident, func=AF.Copy,
                                                   scale=convw[:, w * 8 + ch:w * 8 + ch + 1])
                            return ins
                        K.op(act, dgb, reads=[b_c, b_convw], writes=[b_dgb])

                    dg_pe = dve if t < 2 else pool
                    mm_set[0] = [0, 1, 2, 5, 6]
                    build_dg(0)
                    for ch in range(8):
                        dg = dg2[ch % 2]
                        b_dga, b_dgb, b_dgc = b_dg2[ch % 2]
                        bank, bb = mm_bank()

                        def cvf(e, ch=ch, bank=bank, dg=dg):
                            ins = None
                            for w in range(31):
                                ins = e.matmul(bank[:, 0:TN], lhsT=dg[:, w, :], rhs=hdn[:, ch, w:w + TN],
                                               start=(w == 0), stop=(w == 30))
                            return ins
                        K.op(pe, cvf, reads=[b_dga, b_dgb, b_dgc, b_hdn[ch]], writes=[bb])
                        if ch + 1 < 8:
                            build_dg(ch + 1)
                        if ch > 0:
                            emit_stats(ch - 1)
                        i2 = ch % 2
                        K.op(act, lambda e, ch=ch, bank=bank: e.activation(out=ysb[:, ch, :], in_=bank[:, 0:TN],
                                                                          func=AF.Identity, bias=pvec[:, ch:ch + 1]),
                             reads=[bb, b_pvec], writes=[b_ysb[ch]])
                        K.op(dg_pe, lambda e, ch=ch, i2=i2: e.tensor_copy(out=ybf[i2][:], in_=ysb[:, ch, :]),
                             reads=[b_ysb[ch]], writes=[b_ybf[i2]])
                        K.op(act, lambda e, ch=ch, i2=i2, bank=bank: e.activation(out=ysq[i2][:], in_=bank[:, 0:TN],
                                                                                 func=AF.Square, bias=pvec[:, ch:ch + 1]),
                             reads=[bb, b_pvec], writes=[b_ysq[i2]])
                    emit_stats(7)
                    K.op(act, lambda e: e.activation(out=lmean[:], in_=sum_ps, func=AF.Copy, scale=1.0 / D),
                         reads=[b_pp[3]], writes=[b_lmean])
                    K.op(dve, lambda e: e.tensor_tensor(out=lt[:], in0=lmean[:], in1=lmean[:], op=ALU.mult),
                         reads=[b_lmean], writes=[b_lt])
                    K.op(dve, lambda e: e.scalar_tensor_tensor(out=lt[:], in0=sq_ps, scalar=1.0 / D, in1=lt[:],
                                                               op0=ALU.mult, op1=ALU.subtract),
                         reads=[b_pp[4], b_lt], writes=[b_lt])
                    K.op(act, lambda e: e.activation(out=lt[:], in_=lt[:], func=AF.Sqrt, bias=EPS, scale=1.0),
                         reads=[b_lt], writes=[b_lt])
                    K.op(dve, lambda e: e.reciprocal(out=lrstd[:], in_=lt[:]), reads=[b_lt], writes=[b_lrstd])
                    for ch in range(8):
                        i2 = ch % 2
                        K.op(dve, lambda e, ch=ch, i2=i2: e.tensor_tensor(out=ld[i2][:], in0=ysb[:, ch, :], in1=lmean[:],
                                                                          op=ALU.subtract),
                             reads=[b_ysb[ch], b_lmean], writes=[b_ld[i2]])
                        K.op(aux, lambda e, i2=i2: e.tensor_tensor(out=ld[i2][:], in0=ld[i2][:], in1=lrstd[:], op=ALU.mult),
                             reads=[b_ld[i2], b_lrstd], writes=[b_ld[i2]])
                        K.op(act, lambda e, ch=ch, i2=i2: e.activation(out=ocT[:, 8 + ch, :], in_=ld[i2][:], func=AF.Silu,
                                                                       bias=pvec[:, 16 + ch:17 + ch],
                                                                       scale=pvec[:, 8 + ch:9 + ch]),
                             reads=[b_ld[i2], b_pvec], writes=[b_ocT[8 + ch]])

                    mm_set[0] = [0, 1, 2]
                    wo_banks = [[0, 1, 2], [4, 5, 6]]
                    for kp in range(2):
                        for cb in range(2):
                            wt, bw = load_w(s_wout, kp * 8, cb * 512)
                            for c in range(TCH):
                                bi = wo_banks[cb][c]

                                def mmf(e, wt=wt, c=c, kp=kp, bi=bi):
                                    ins = None
                                    for kc in range(8):
                                        ins = e.matmul(pp[:, bi, :], lhsT=ocT[:, kp * 8 + kc, c * 128:(c + 1) * 128],
                                                       rhs=wt[:, kc, :], start=(kp == 0 and kc == 0),
                                                       stop=(kp == 1 and kc == 7))
                                    return ins
                                K.op(pe, mmf, reads=[bw] + b_ocT[kp * 8:kp * 8 + 8], writes=[b_pp[bi]])
                    for cb in range(2):
                        for c in range(TCH):
                            bi = wo_banks[cb][c]
                            K.op(dve, lambda e, c=c, cb=cb, hr=hr, bi=bi: e.tensor_tensor(
                                out=hr[:, c, cb * 512:(cb + 1) * 512], in0=hr[:, c, cb * 512:(cb + 1) * 512],
                                in1=pp[:, bi, :], op=ALU.add), reads=[b_pp[bi], b_hr[c]], writes=[b_hr[c]])
                    if t == 0:
                        K.op(dve, lambda e, hr=hr: e.memset(hr[0:PAD, 0, :], 0.0), reads=[b_hr[0]], writes=[b_hr[0]])
                    mlp(X, 0, G_MLP0, aux)
                    if t + 1 < (NT if stop != 'p0t0' else 1):
                        load_x(t + 1)
                    if debug and "h0" in debug:
                        K.dma(sp, dbg["h0"][p0:p0 + TN, :].rearrange("(c p) d -> p c d", p=128), hr[:, :, :], reads=b_hr)

                    K.dma(sp, s_h[p0:p0 + TN, :].rearrange("(c p) d -> p c d", p=128), hr[:, :, :], reads=b_hr,
                          writes=[scr_buf(s_h)])
                    rmsnorm_T(X, G_MIX1)
                    qTst = B6[0][:].rearrange("p (h t) -> p h t", h=8)
                    kTst = B6[1][:].rearrange("p (h t) -> p h t", h=8)
                    vst = B6[2][:].rearrange("p (c n) -> p c n", c=TCH)
                    mm_set[0] = [0, 1, 2, 3, 4]
                    blk_ctr = [0]
                    qk_items = [(cb, j) for cb in range(4) for j in range(4)]
                    qk_state = {}

                    def qk_A(n):
                        cb, j = qk_items[n]
                        if j == 0:
                            qk_state["w"] = load_w(s_wqkv, 0, cb * 512)
                        wt, bw = qk_state["w"]
                        hp = (cb % 2) * 4 + j
                        bank, bb = mm_bank()

                        def mmf(e, wt=wt, j=j, bank=bank):
                            ins = None
                            for kc in range(8):
                                ins = e.matmul(bank[:, 0:TN], lhsT=wt[:, kc, j * 128:(j + 1) * 128], rhs=hnT[:, kc, :],
                                               start=(kc == 0), stop=(kc == 7))
                            return ins
                        K.op(pe, mmf, reads=[bw] + b_hnT, writes=[bb])
                        i2 = n % 3
                        K.op(act, lambda e, i2=i2, bank=bank: e.activation(out=sqh[i2][:], in_=bank[:, 0:TN],
                                                                          func=AF.Square),
                             reads=[bb], writes=[b_sqh[i2]])
                        qk_state[n] = (bank, bb, hp, cb < 2)

                    def qk_B(n):
                        bank, bb, hp, isq = qk_state[n]
                        dstT = qTst if isq else kTst
                        b_dst = b_B6[0] if isq else b_B6[1]
                        i2 = n % 3
                        _bi = 5 + (blk_ctr[0] % 2)
                        blk_ctr[0] += 1
                        bank2, bb2 = pp[:, _bi, :], b_pp[_bi]
                        K.op(pe, lambda e, i2=i2, bank2=bank2: e.matmul(bank2[:, 0:TN], lhsT=blk, rhs=sqh[i2][:],
                                                                        start=True, stop=True),
                             reads=[b_sqh[i2], b_c], writes=[bb2])
                        K.op(act, lambda e, i2=i2, bank2=bank2: e.activation(out=rsh[i2][:], in_=bank2[:, 0:TN],
                                                                            func=AF.Sqrt, bias=EPS, scale=1.0 / 64),
                             reads=[bb2], writes=[b_rsh[i2]])
                        K.op(dve, lambda e, i2=i2: e.reciprocal(out=rsh[i2][:], in_=rsh[i2][:]),
                             reads=[b_rsh[i2]], writes=[b_rsh[i2]])
                        gc = 0 if isq else 1
                        K.op(dve, lambda e, i2=i2, bank=bank, hp=hp, dstT=dstT, gc=gc: e.scalar_tensor_tensor(
                            out=dstT[:, hp, :], in0=bank[:, 0:TN], scalar=gqk[:, gc:gc + 1], in1=rsh[i2][:],
                            op0=ALU.mult, op1=ALU.mult), reads=[bb, b_rsh[i2], b_gqk], writes=[b_dst])

                    qk_A(0)
                    qk_A(1)
                    for n in range(16):
                        if n + 2 < 16:
                            qk_A(n + 2)
                        qk_B(n)
                    mm_set[0] = [0, 1, 2]
                    for cb in range(2):
                        wt, bw = load_w(s_wqkv, 0, 2048 + cb * 512)
                        for c in range(TCH):
                            bank, bb = mm_bank()

                            def mmf(e, wt=wt, c=c, bank=bank):
                                ins = None
                                for kc in range(8):
                                    ins = e.matmul(bank, lhsT=hnT[:, kc, c * 128:(c + 1) * 128], rhs=wt[:, kc, :],
                                                   start=(kc == 0), stop=(kc == 7))
                                return ins
                            K.op(pe, mmf, reads=[bw, b_hnT[c]], writes=[bb])
                            K.op(act, lambda e, c=c, cb=cb, bank=bank: e.activation(
                                out=vst[:, c, cb * 512:(cb + 1) * 512], in_=bank, func=AF.Copy),
                                reads=[bb], writes=[b_B6[2]])
                    K.dma(sp, s_qT[:, :, p0:p0 + TN].rearrange("h p t -> p h t"), qTst, reads=[b_B6[0]],
                          writes=[scr_buf(s_qT)])
                    K.dma(sp, s_kT[:, :, p0:p0 + TN].rearrange("h p t -> p h t"), kTst, reads=[b_B6[1]],
                          writes=[scr_buf(s_kT)])
                    K.dma(sp, s_v[p0:p0 + TN, :].rearrange("(c p) d -> p c d", p=128), vst, reads=[b_B6[2]],
                          writes=[scr_buf(s_v)])
            K.barrier()
            es_a.close()

            with ExitStack() as es2:
                QTt = [sb(es2, f"QTt{i}", [128, TN], BF16) for i in range(3)]
                KT = [sb(es2, f"KT{i}", [128, P], BF16) for i in range(3)]
                Vh = [sb(es2, f"Vh{i}", [128, NCHK, 128], BF16) for i in range(3)]
                b_QT, b_KT, b_Vh = [Buf(), Buf(), Buf()], [Buf(), Buf(), Buf()], [Buf(), Buf(), Buf()]
                ee = [sb(es2, f"ee{i}", [128, 2, TN], F32) for i in range(3)]
                spb = [sb(es2, f"spb{i}", [128, 2, TN], BF16) for i in range(3)]
                tt = [sb(es2, f"tt{i}", [128, 2, TN], F32) for i in range(2)]
                ww = [sb(es2, f"ww{i}", [128, 2, TN], BF16) for i in range(2)]
                b_ee, b_spb, b_tt, b_ww = [Buf(), Buf(), Buf()], [Buf(), Buf(), Buf()], [Buf(), Buf()], [Buf(), Buf()]
                oTt = [sb(es2, f"oTt{i}", [128, 8, TN], BF16) for i in range(2)]
                b_oTt = [[Buf() for _ in range(8)] for _ in range(2)]
                zps = pp8[:, 3:5, :]
                b_z = [b_pp[3], b_pp[4]]
                accps = pp8[:, 5:7, :]
                b_acc = [b_pp[5], b_pp[6]]
                ops_ = pp8[:, 7, :]
                b_o = b_tp
                X3 = alloc_tile(es2, "b")
                X3.tp, X3.b_tp = pp8[:, 0, :].bitcast(BF16), b_pp[0]
                hres3 = [X3.hres, sb(es2, "hresb2", [128, TCH, D], F32)]
                b_hres3 = [X3.b_hres, [Buf(f"hres2_{c}") for c in range(TCH)]]
                mm_set[0] = [0, 1, 2]
                NG = NT
                NHP = 8 if stop not in ('p01', 'p0t0') else 0
                steps = []
                for G in range(NG if NHP else 0):
                    for hp in range(NHP):
                        kb_hi = TCH * G + TCH - 1
                        for kb in range(kb_hi, -1, -1):
                            steps.append((G * NHP + hp, G, hp, kb, kb_hi))
                nst = len(steps)
                NU = NG * NHP

                def emit_load(u):
                    G, hp = divmod(u, NHP)
                    s = u % 3
                    g0 = G * TN
                    nk = TCH * G + TCH
                    K.dma(sp, QTt[s][:], s_qT[hp][:, g0:g0 + TN], reads=[scr_buf(s_qT)], writes=[b_QT[s]])
                    K.dma(sp, KT[s][:, 0:nk * 128], s_kT[hp][:, 0:nk * 128], reads=[scr_buf(s_kT)], writes=[b_KT[s]])
                    for c0 in range(0, nk, 11):
                        c1_ = min(nk, c0 + 11)
                        K.dma(sp, Vh[s][:, c0:c1_, :],
                              s_v.rearrange("(c p) d -> p c d", p=128)[:, c0:c1_, hp * 128:(hp + 1) * 128],
                              reads=[scr_buf(s_v)], writes=[b_Vh[s]])

                def geo(j):
                    u, G, hp, kb, kb_hi = steps[j]
                    r = kb - TCH * G
                    q0 = max(r, 0) * 128
                    return u, G, hp, kb, kb_hi, r, q0, u % 3, j % 2, j % 3

                def S_zf(j):
                    u, G, hp, kb, kb_hi, r, q0, s, i2, i3 = geo(j)

                    def zf(e, s=s, kb=kb, q0=q0):
                        ins = None
                        for h in range(2):
                            ins = e.matmul(zps[:, h, q0:TN],
                                           lhsT=KT[s][h * 64:(h + 1) * 64, kb * 128:(kb + 1) * 128],
                                           rhs=QTt[s][h * 64:(h + 1) * 64, q0:TN],
                                           start=True, stop=True)
                        return ins
                    K.op(pe, zf, reads=[b_KT[s], b_QT[s]], writes=b_z)

                def S_exp(j):
                    u, G, hp, kb, kb_hi, r, q0, s, i2, i3 = geo(j)
                    K.op(act, lambda e, i3=i3, q0=q0: e.activation(out=ee[i3][:, :, q0:TN], in_=zps[:, :, q0:TN],
                                                                  func=AF.Exp),
                         reads=b_z, writes=[b_ee[i3]])

                def S_ln(j):
                    u, G, hp, kb, kb_hi, r, q0, s, i2, i3 = geo(j)
                    K.op(act, lambda e, i3=i3, q0=q0: e.activation(out=spb[i3][:, :, q0:TN], in_=ee[i3][:, :, q0:TN],
                                                                  func=AF.Ln, bias=1.0, scale=1.0),
                         reads=[b_ee[i3]], writes=[b_spb[i3]])
                    if r >= 0:
                        mb = mT.unsqueeze(1).broadcast_to([128, 2, 128])
                        K.op(dve, lambda e, i3=i3, q0=q0, mb=mb: e.tensor_tensor(
                            out=ee[i3][:, :, q0:q0 + 128], in0=ee[i3][:, :, q0:q0 + 128], in1=mb, op=ALU.mult),
                            reads=[b_ee[i3], b_c], writes=[b_ee[i3]])
                        K.op(dve, lambda e, i3=i3, q0=q0, mb=mb: e.tensor_tensor(
                            out=spb[i3][:, :, q0:q0 + 128], in0=spb[i3][:, :, q0:q0 + 128], in1=mb, op=ALU.mult),
                            reads=[b_spb[i3], b_c], writes=[b_spb[i3]])
                    if kb == 0:
                        K.op(dve, lambda e, i3=i3, q0=q0: e.memset(ee[i3][0:PAD, :, q0:TN], 0.0),
                             reads=[b_ee[i3]], writes=[b_ee[i3]])
                        K.op(dve, lambda e, i3=i3, q0=q0: e.memset(spb[i3][0:PAD, :, q0:TN], 0.0),
                             reads=[b_spb[i3]], writes=[b_spb[i3]])

                def S_c1(j):
                    u, G, hp, kb, kb_hi, r, q0, s, i2, i3 = geo(j)

                    def c1(e, i3=i3, q0=q0, first=(kb == kb_hi)):
                        ins = None
                        for h in range(2):
                            ins = e.matmul(accps[:, h, q0:TN], lhsT=tri, rhs=spb[i3][:, h, q0:TN],
                                           start=first, stop=True, skip_group_check=True)
                        return ins
                    K.op(pe, c1, reads=[b_spb[i3], b_c], writes=b_acc)

                def S_exp2(j):
                    u, G, hp, kb, kb_hi, r, q0, s, i2, i3 = geo(j)
                    K.op(act, lambda e, i2=i2, q0=q0: e.activation(out=tt[i2][:, :, q0:TN], in_=accps[:, :, q0:TN],
                                                                  func=AF.Exp, scale=-1.0),
                         reads=b_acc, writes=[b_tt[i2]])

                def S_c2(j):
                    u, G, hp, kb, kb_hi, r, q0, s, i2, i3 = geo(j)
                    if kb > 0:
                        def c2(e, i3=i3, q0=q0):
                            ins = None
                            for h in range(2):
                                ins = e.matmul(accps[:, h, q0:TN], lhsT=stri, rhs=spb[i3][:, h, q0:TN],
                                               start=False, stop=True, skip_group_check=True)
                            return ins
                        K.op(pe, c2, reads=[b_spb[i3], b_c], writes=b_acc)

                def S_mult(j):
                    u, G, hp, kb, kb_hi, r, q0, s, i2, i3 = geo(j)
                    K.op(dve, lambda e, i2=i2, i3=i3, q0=q0: e.tensor_tensor(
                        out=ww[i2][:, :, q0:TN], in0=ee[i3][:, :, q0:TN], in1=tt[i2][:, :, q0:TN], op=ALU.mult),
                        reads=[b_ee[i3], b_tt[i2]], writes=[b_ww[i2]])

                def S_wv(j):
                    u, G, hp, kb, kb_hi, r, q0, s, i2, i3 = geo(j)

                    def wv(e, s=s, i2=i2, kb=kb, q0=q0, first=(kb == kb_hi)):
                        ins = None
                        for h in range(2):
                            ins = e.matmul(ops_[h * 64:(h + 1) * 64, q0:TN], lhsT=Vh[s][:, kb, h * 64:(h + 1) * 64],
                                           rhs=ww[i2][:, h, q0:TN], start=first, stop=True, skip_group_check=True)
                        return ins
                    K.op(pe, wv, reads=[b_Vh[s], b_ww[i2]], writes=[b_o])
                    if kb == 0:
                        K.op(dve, lambda e, hp=hp, G=G: e.tensor_copy(out=oTt[G % 2][:, hp, :], in_=ops_[:, 0:TN]),
                             reads=[b_o], writes=[b_oTt[G % 2][hp]])
                    if (j + 1 == nst or steps[j + 1][0] != u) and u + 3 < NU:
                        emit_load(u + 3)

                def p3_gen(t):
                    X3.hres, X3.b_hres = hres3[t % 2], b_hres3[t % 2]
                    hr3, b_hr3 = X3.hres, X3.b_hres
                    oT = oTt[t % 2]
                    b_oT = b_oTt[t % 2]
                    p0 = t * TN
                    K.dma(sp, hr3[:, :, :], s_h[p0:p0 + TN, :].rearrange("(c p) d -> p c d", p=128),
                          reads=[scr_buf(s_h)], writes=b_hr3)
                    yield
                    for cb in range(2):
                        wt, bw = load_w(s_wo, 0, cb * 512)
                        for c in range(TCH):
                            def mmf(e, wt=wt, c=c, oT=oT):
                                ins = None
                                for kc in range(8):
                                    ins = e.matmul(pp[:, c, :], lhsT=oT[:, kc, c * 128:(c + 1) * 128],
                                                   rhs=wt[:, kc, :], start=(kc == 0), stop=(kc == 7))
                                return ins
                            K.op(pe, mmf, reads=[bw] + b_oT, writes=[b_pp[c]])
                            yield
                        for c in range(TCH):
                            K.op(dve, lambda e, c=c, cb=cb, hr3=hr3: e.tensor_tensor(
                                out=hr3[:, c, cb * 512:(cb + 1) * 512], in0=hr3[:, c, cb * 512:(cb + 1) * 512],
                                in1=pp[:, c, :], op=ALU.add), reads=[b_pp[c], b_hr3[c]], writes=[b_hr3[c]])
                    yield from mlp_gen(X3, 1, G_MLP1, pool, w1_banks=(0, 1, 2))
                    if t == 0:
                        K.dma(sp, out[0:(TCH - 1) * 128, :].rearrange("(c p) d -> p c d", p=128), hr3[:, 1:TCH, :],
                              reads=b_hr3[1:TCH])
                    else:
                        r0 = p0 - 128
                        K.dma(sp, out[r0:r0 + TN, :].rearrange("(c p) d -> p c d", p=128), hr3[:, :, :], reads=b_hr3)

                p3 = [None]
                p3_acc = [0.0]
                p3_rate = [0.0]
                P3_ITEMS = 75.0
                P3_CAP = 0.33

                def p3_advance(n):
                    for _ in range(n):
                        if p3[0] is None:
                            return
                        try:
                            next(p3[0])
                        except StopIteration:
                            p3[0] = None

                def p3_drain():
                    while p3[0] is not None:
                        p3_advance(1)

                if nst:
                    for _u in range(min(3, NU)):
                        emit_load(_u)
                    S_zf(0)
                    S_exp(0)
                    if nst > 1:
                        S_zf(1)
                    S_ln(0)
                    if nst > 1:
                        S_exp(1)
                    if nst > 2:
                        S_zf(2)
                    if nst > 1:
                        S_ln(1)
                    S_c1(0)
                for i in range(nst):
                    S_exp2(i)
                    S_c2(i)
                    if i + 1 < nst:
                        S_c1(i + 1)
                    S_mult(i)
                    if i >= 1:
                        S_wv(i - 1)
                        Gp = steps[i - 1][1]
                        if steps[i][1] != Gp:
                            p3_drain()
                            p3[0] = p3_gen(Gp)
                            n_next = NHP * (TCH * (Gp + 1) + TCH)
                            p3_rate[0] = min(P3_CAP, P3_ITEMS / (0.8 * n_next))
                            p3_acc[0] = 0.0
                    if i + 2 < nst:
                        S_exp(i + 2)
                    if i + 3 < nst:
                        S_zf(i + 3)
                    if p3[0] is not None:
                        p3_acc[0] += p3_rate[0]
                        k = int(p3_acc[0])
                        if k:
                            p3_acc[0] -= k
                            p3_advance(k)
                    if i + 2 < nst:
                        S_ln(i + 2)
                if nst:
                    S_wv(nst - 1)
                    p3_drain()
                    p3[0] = p3_gen(steps[nst - 1][1])
                    p3_drain()
            K.finish()
            K.emit()
    return nc


_CACHE = {}


def _get_nc(SEQ, debug=None):
    key = (SEQ, None if debug is None else tuple(sorted(debug)))
    if key not in _CACHE:
        _CACHE[key] = build(SEQ, debug)
    return _CACHE[key]


def make_in_maps(inputs, SEQ, ncores):
    P = PAD + NMETA + SEQ
    cb, cf, rot = host_consts(P)
    f = lambda a: np.ascontiguousarray(np.asarray(a, dtype=np.float32))
    shared = {
        "meta": f(inputs["meta"]),
        "norm_mix_g": f(inputs["norm_mix_g"]),
        "norm_mlp_g": f(inputs["norm_mlp_g"]),
        "even_w_in": f(inputs["even_w_in"][0]),
        "even_ret_gn_g": f(inputs["even_ret_gn_g"][0]).reshape(1024),
        "even_conv_w": f(inputs["even_conv_w"][0]),
        "even_conv_b": f(inputs["even_conv_b"][0]),
        "even_conv_ln_g": f(inputs["even_conv_ln_g"][0]),
        "even_conv_ln_b": f(inputs["even_conv_ln_b"][0]),
        "even_w_out": f(inputs["even_w_out"][0]),
        "odd_w_qkv": f(inputs["odd_w_qkv"][0]),
        "odd_q_norm_g": f(inputs["odd_q_norm_g"][0]),
        "odd_k_norm_g": f(inputs["odd_k_norm_g"][0]),
        "odd_w_o": f(inputs["odd_w_o"][0]),
        "mlp_w1_0": f(inputs["mlp_w1"][0]),
        "mlp_w1_1": f(inputs["mlp_w1"][1]),
        "mlp_w2_0": f(inputs["mlp_w2"][0]),
        "mlp_w2_1": f(inputs["mlp_w2"][1]),
        "c_bf": cb,
        "c_f": cf,
        "c_rot": rot,
    }
    xs = np.asarray(inputs["x"], dtype=np.float32)
    maps = []
    for b in range(ncores):
        m = dict(shared)
        m["x"] = np.ascontiguousarray(xs[b])
        maps.append(m)
    return maps


def kernel(**inputs):
    x = np.asarray(inputs["x"])
    B, SEQ, _ = x.shape
    nc = _get_nc(SEQ)
    maps = make_in_maps(inputs, SEQ, B)
    res = run_bass_kernel_spmd(nc, maps, core_ids=list(range(B)))
    return np.stack([np.asarray(r["out"], dtype=np.float32) for r in res.results], axis=0)
```

```python
import numpy as np
import ml_dtypes
from contextlib import ExitStack
import concourse.bass as bass
import concourse.mybir as mybir
from concourse.bass_utils import run_bass_kernel_spmd

F32 = mybir.dt.float32
BF16 = mybir.dt.bfloat16
AF = mybir.ActivationFunctionType
ALU = mybir.AluOpType
AX = mybir.AxisListType

D = 1024
NMETA = 16
PAD = 112
EPS = 1e-6
TCH = 3
TN = 128 * TCH
DFF = 4096
NCORES = 8


class Buf:
    __slots__ = ("w", "r", "name")

    def __init__(self, name=""):
        self.w = None
        self.r = {}
        self.name = name


class Producer:
    def __init__(self, name, unit):
        self.name = name
        self.unit = unit
        self.sem = None
        self.count = 0


class Eng(Producer):
    def __init__(self, name, is_pe=False):
        super().__init__(name, 1)
        self.ops = []
        self.known = {}
        self.is_pe = is_pe
        self.lanes = []
        self.lane_i = 0


class Prog:
    def __init__(self, nc, es):
        self.nc = nc
        self.es = es
        self.pe = Eng("pe", True)
        self.act = Eng("act")
        self.dve = Eng("dve")
        self.pool = Eng("pool")
        self.sp = Eng("sp")
        self.engs = [self.pe, self.act, self.dve, self.pool, self.sp]
        self.prods = list(self.engs)
        for q in (self.sp, self.pool, self.act):
            for i in range(8):
                ln = Producer(f"{q.name}_l{i}", 16)
                q.lanes.append(ln)
                self.prods.append(ln)
        for p in self.prods:
            p.sem = es.enter_context(nc.semaphore("s_" + p.name))

    def op(self, eng, thunk, reads=(), writes=(), lane=None):
        d = {}
        for b in reads:
            if b.w is not None:
                p, i = b.w
                if d.get(p, 0) < i:
                    d[p] = i
        for b in writes:
            if b.w is not None:
                p, i = b.w
                if d.get(p, 0) < i:
                    d[p] = i
            for p, i in b.r.items():
                if d.get(p, 0) < i:
                    d[p] = i
        prod = lane if lane is not None else eng
        if lane is not None and lane.count > 0:
            d[lane] = lane.count
        for p, i in d.items():
            if p is eng and eng.is_pe:
                continue
            if eng.known.get(p, 0) < i:
                eng.ops.append(("w", p, i))
                eng.known[p] = i
        prod.count += 1
        idx = prod.count
        eng.ops.append(("i", thunk, prod))
        for b in reads:
            b.r[prod] = idx
        for b in writes:
            b.w = (prod, idx)
            b.r = {}

    def dma(self, q, out, in_, reads=(), writes=()):
        lane = q.lanes[q.lane_i % len(q.lanes)]
        q.lane_i += 1
        self.op(q, lambda e, o=out, i=in_: e.dma_start(out=o, in_=i), reads, writes, lane=lane)

    def barrier(self):
        for e in self.engs:
            for p in self.prods:
                if p.count > 0 and e.known.get(p, 0) < p.count and not (p is e):
                    e.ops.append(("w", p, p.count))
                    e.known[p] = p.count

    def finish(self):
        for q in (self.sp, self.pool, self.act):
            for ln in q.lanes:
                if ln.count > 0 and q.known.get(ln, 0) < ln.count:
                    q.ops.append(("w", ln, ln.count))
                    q.known[ln] = ln.count
        for p in self.prods:
            if p.count > 0 and p is not self.sp and self.sp.known.get(p, 0) < p.count:
                self.sp.ops.append(("w", p, p.count))
                self.sp.known[p] = p.count

    def emit(self):
        nc = self.nc

        def replay(eng, e):
            for o in eng.ops:
                if o[0] == "w":
                    e.wait_ge(o[1].sem, o[2] * o[1].unit)
                else:
                    ins = o[1](e)
                    ins.then_inc(o[2].sem, o[2].unit)

        with nc.Block() as block:
            @block.sync
            def _(e):
                replay(self.sp, e)

            @block.scalar
            def _(e):
                replay(self.act, e)

            @block.vector
            def _(e):
                replay(self.dve, e)

            @block.gpsimd
            def _(e):
                replay(self.pool, e)

            @block.tensor
            def _(e):
                replay(self.pe, e)


def host_consts(P):
    nch = P // 128
    cb = np.zeros((128, 5, 128), np.float32)
    j = np.arange(128)[:, None]
    k = np.arange(128)[None, :]
    cb[:, 0] = (j == k)
    cb[:, 1] = 1.0
    cb[:, 2] = (j >= k)
    cb[:, 3] = (j < k)
    cb[:, 4] = ((j // 64) == (k // 64))
    cf = np.zeros((128, 3 * 128 + 8), np.float32)
    cf[:, 0:128] = (j == k)
    cf[:, 128:256] = (k >= j)
    cf[:, 256:384] = (j < k)
    idx = np.arange(128, dtype=np.float64)
    for h in range(4):
        lg = np.log1p(-np.exp2(-5.0 - h))
        cf[:, 384 + h] = np.exp((idx - 127.0) * lg)
        cf[:, 388 + h] = np.exp((127.0 - idx) * lg) * (128.0 ** -0.5)
    half = 64
    inv_freq = (10000.0 ** (-np.arange(half, dtype=np.float32) / half)).astype(np.float32)
    ang = np.arange(P, dtype=np.float32)[:, None] * inv_freq[None, :]
    cos = np.cos(ang).astype(np.float32)
    sin = np.sin(ang).astype(np.float32)
    rot = np.zeros((P, 2, 128), np.float32)
    rot[:, 0, :64] = cos
    rot[:, 0, 64:] = cos
    rot[:, 1, :64] = -sin
    rot[:, 1, 64:] = sin
    return cb.astype(ml_dtypes.bfloat16), cf, rot


def build(SEQ, debug=None, stop=None):
    P = PAD + NMETA + SEQ
    NCHK = P // 128
    assert NCHK % TCH == 0
    NT = NCHK // TCH
    nc = bass.Bass("TRN2", target_bir_lowering=False)

    def din(name, shape, dt=F32):
        return nc.dram_tensor(name, list(shape), dt, kind="ExternalInput").ap()

    def dscr(name, shape, dt):
        return nc.dram_tensor(name, list(shape), dt, kind="Internal").ap()

    x = din("x", [SEQ, D])
    meta = din("meta", [NMETA, D])
    norm_mix_g = din("norm_mix_g", [2, D])
    norm_mlp_g = din("norm_mlp_g", [2, D])
    w_in = din("even_w_in", [D, 5120])
    gn_g = din("even_ret_gn_g", [1024])
    conv_w = din("even_conv_w", [31, D])
    conv_b = din("even_conv_b", [D])
    ln_g = din("even_conv_ln_g", [D])
    ln_b = din("even_conv_ln_b", [D])
    w_out = din("even_w_out", [2048, D])
    w_qkv = din("odd_w_qkv", [D, 3072])
    qn_g = din("odd_q_norm_g", [64])
    kn_g = din("odd_k_norm_g", [64])
    w_o = din("odd_w_o", [D, D])
    w1 = [din("mlp_w1_0", [D, DFF]), din("mlp_w1_1", [D, DFF])]
    w2 = [din("mlp_w2_0", [DFF, D]), din("mlp_w2_1", [DFF, D])]
    c_bf = din("c_bf", [128, 5, 128], BF16)
    c_f = din("c_f", [128, 392])
    c_rot = din("c_rot", [P, 2, 128])
    out = nc.dram_tensor("out", [SEQ, D], F32, kind="ExternalOutput").ap()
    dbg = None
    if debug:
        dbg = {k: nc.dram_tensor("dbg_" + k, list(s), F32, kind="ExternalOutput").ap() for k, s in debug.items()}

    s_win = dscr("s_win", [D, 5120], BF16)
    s_wout = dscr("s_wout", [2048, D], BF16)
    s_w1 = [dscr("s_w1_0", [D, DFF], BF16), dscr("s_w1_1", [D, DFF], BF16)]
    s_w2 = [dscr("s_w2_0", [DFF, D], BF16), dscr("s_w2_1", [DFF, D], BF16)]
    s_wqkv = dscr("s_wqkv", [D, 3072], BF16)
    s_wo = dscr("s_wo", [D, D], BF16)
    s_qT = dscr("s_qT", [8, 128, P], BF16)
    s_kT = dscr("s_kT", [8, 128, P], BF16)
    s_v = dscr("s_v", [P, D], BF16)
    s_h = dscr("s_h", [P, D], F32)

    with ExitStack() as es:
        K = Prog(nc, es)
        pe, act, dve, pool, sp = K.pe, K.act, K.dve, K.pool, K.sp

        def sb(stack, name, shape, dt):
            return stack.enter_context(nc.sbuf_tensor(name, list(shape), dt))

        tp = es.enter_context(nc.psum_tensor("tp", [128, 1024], BF16))
        pp = es.enter_context(nc.psum_tensor("pp", [128, 7, 512], F32))
        b_tp = Buf("tp")
        b_pp = [Buf(f"pp{i}") for i in range(7)]

        cbf = sb(es, "cbf", [128, 5, 128], BF16)
        cf = sb(es, "cf", [128, 392], F32)
        b_c = Buf("consts")
        K.dma(sp, cbf[:], c_bf[:, :, :], writes=[b_c])
        K.dma(sp, cf[:], c_f[:, :], writes=[b_c])
        ident = cbf[:, 0, :]
        ones_bf = cbf[:, 1, :]
        tri = cbf[:, 2, :]
        stri = cbf[:, 3, :]
        blk = cbf[:, 4, :]
        ident_f = cf[:, 0:128]
        maskT = cf[:, 128:256]
        mT = cf[:, 256:384]
        dq = cf[:, 384:388]
        dk = cf[:, 388:392]

        vstage = sb(es, "vstage", [128, 128], F32)
        vstage2 = sb(es, "vstage2", [128, 128], F32)
        convw = sb(es, "convw", [128, 248], F32)
        pvec = sb(es, "pvec", [128, 64], F32)
        gqk = sb(es, "gqk", [128, 2], F32)
        gng_b = sb(es, "gng_b", [128, 1024], F32)
        b_vs, b_vs2, b_convw, b_pvec, b_gqk, b_gng = Buf(), Buf(), Buf(), Buf(), Buf(), Buf()
        K.dma(sp, gng_b[:], gn_g.partition_broadcast(128), writes=[b_gng])
        cw = conv_w.rearrange("w (c p) -> (w c) p", p=128)
        K.dma(sp, vstage[0:124, :], cw[0:124, :], writes=[b_vs])
        K.dma(sp, vstage2[0:124, :], cw[124:248, :], writes=[b_vs2])
        ptf = pp[:, 0, :]
        K.op(pe, lambda e: e.transpose(out=ptf[:, 0:124], in_=vstage[0:124, :], identity=ident_f[0:124, 0:124]),
             reads=[b_vs, b_c], writes=[b_pp[0]])
        K.op(pe, lambda e: e.transpose(out=ptf[:, 124:248], in_=vstage2[0:124, :], identity=ident_f[0:124, 0:124]),
             reads=[b_vs2, b_c], writes=[b_pp[0]])
        K.op(dve, lambda e: e.tensor_copy(out=convw[:], in_=ptf[:, 0:248]), reads=[b_pp[0]], writes=[b_convw])
        vecs = [conv_b, ln_g, ln_b, norm_mix_g[0], norm_mlp_g[0], norm_mix_g[1], norm_mlp_g[1]]
        for i, v in enumerate(vecs):
            K.dma(sp, vstage[8 * i:8 * i + 8, :], v.rearrange("(c p) -> c p", p=128), writes=[b_vs])
        K.op(pe, lambda e: e.transpose(out=pp[:, 1, 0:56], in_=vstage[0:56, :], identity=ident_f[0:56, 0:56]),
             reads=[b_vs, b_c], writes=[b_pp[1]])
        K.op(dve, lambda e: e.tensor_copy(out=pvec[:, 0:56], in_=pp[:, 1, 0:56]), reads=[b_pp[1]], writes=[b_pvec])
        for hh in range(2):
            K.dma(sp, gqk[hh * 64:(hh + 1) * 64, 0:1], qn_g.rearrange("(p o) -> p o", o=1), writes=[b_gqk])
            K.dma(sp, gqk[hh * 64:(hh + 1) * 64, 1:2], kn_g.rearrange("(p o) -> p o", o=1), writes=[b_gqk])
        K.op(dve, lambda e: e.tensor_scalar(out=gqk[:, 0:1], in0=gqk[:, 0:1], scalar1=0.125, scalar2=None, op0=ALU.mult),
             reads=[b_gqk], writes=[b_gqk])
        G_MIX0, G_MLP0, G_MIX1, G_MLP1 = 24, 32, 40, 48

        NSLOT = 3
        wslot = [sb(es, f"wslot{i}", [128, 8, 512], BF16) for i in range(NSLOT)]
        b_wslot = [Buf(f"wslot{i}") for i in range(NSLOT)]
        wctr = [0]
        b_scr = {}

        def scr_buf(t):
            return b_scr.setdefault(id(t), Buf("scr"))

        def load_w(ws, k0, c0, nk=8):
            i = wctr[0] % NSLOT
            wctr[0] += 1
            src = ws.rearrange("(kc p) n -> p kc n", p=128)[:, k0:k0 + nk, c0:c0 + 512]
            K.dma(sp, wslot[i][:, 0:nk, :], src, reads=[scr_buf(ws)], writes=[b_wslot[i]])
            return wslot[i], b_wslot[i]

        class _NS:
            pass

        def alloc_tile(stack, tag):
            X = _NS()
            X.a1T = sb(stack, "a1T" + tag, [128, 32, TN], BF16)
            X.b_a1T = Buf("a1T")
            X.hres = sb(stack, "hres" + tag, [128, TCH, D], F32)
            X.b_hres = [Buf(f"hres{c}") for c in range(TCH)]
            X.hnT = sb(stack, "hnT" + tag, [128, 8, TN], BF16)
            X.b_hnT = [Buf(f"hnT{c}") for c in range(TCH)]
            X.hn_bf = [sb(stack, f"hn_bf{i}" + tag, [128, D], BF16) for i in range(2)]
            X.b_hn_bf = [Buf(), Buf()]
            X.junk = sb(stack, "junk" + tag, [128, D], BF16)
            X.b_junk = Buf()
            X.nstat = sb(stack, "nstat" + tag, [128, 8], F32)
            X.b_nstat = Buf()
            X.rr = [sb(stack, f"rr{i}" + tag, [128, TN], F32) for i in range(2)]
            X.b_rr = [Buf(), Buf()]
            return X

        with ExitStack() as es_a:
            def convert(wsrc, wdst):
                Kr, N = wsrc.shape
                rows = 128 * max(1, 1024 // N)
                for r0 in range(0, Kr, rows):
                    lane = pool.lanes[pool.lane_i % len(pool.lanes)]
                    pool.lane_i += 1
                    K.op(pool, lambda e, o=wdst[r0:r0 + rows, :], i=wsrc[r0:r0 + rows, :]: e.dma_start(
                        out=o, in_=i, max_dma_last_dim=4096), writes=[scr_buf(wdst)], lane=lane)

            convert(w_in, s_win)
            convert(w_out, s_wout)
            convert(w1[0], s_w1[0])
            convert(w2[0], s_w2[0])
            convert(w_qkv, s_wqkv)
            convert(w_o, s_wo)
            convert(w1[1], s_w1[1])
            convert(w2[1], s_w2[1])

            X = alloc_tile(es_a, "a")
            hnT, b_hnT = X.hnT, X.b_hnT
            hres_l = [X.hres, sb(es_a, "hresa2", [128, TCH, D], F32)]
            b_hres_l = [X.b_hres, [Buf(f"hresa2_{c}") for c in range(TCH)]]
            mmctr = [0]
            hnctr = [0]

            mm_set = [[0, 1, 2]]

            def mm_bank():
                st = mm_set[0]
                i = st[mmctr[0] % len(st)]
                mmctr[0] += 1
                return pp[:, i, :], b_pp[i]

            def rmsnorm_T(X, gcol):
                hr, b_hr = X.hres, X.b_hres
                for c in range(TCH):
                    K.op(act, lambda e, c=c: e.activation(out=X.junk[:], in_=hr[:, c, :], func=AF.Square,
                                                          accum_out=X.nstat[:, c:c + 1]),
                         reads=[b_hr[c]], writes=[X.b_junk, X.b_nstat])
                K.op(act, lambda e: e.activation(out=X.nstat[:, 4:4 + TCH], in_=X.nstat[:, 0:TCH], func=AF.Sqrt,
                                                 bias=EPS, scale=1.0 / D),
                     reads=[X.b_nstat], writes=[X.b_nstat])
                K.op(dve, lambda e: e.reciprocal(out=X.nstat[:, 0:TCH], in_=X.nstat[:, 4:4 + TCH]),
                     reads=[X.b_nstat], writes=[X.b_nstat])
                for c in range(TCH):
                    i = hnctr[0] % 2
                    hnctr[0] += 1
                    K.op(dve, lambda e, c=c, i=i: e.tensor_scalar(out=X.hn_bf[i][:], in0=hr[:, c, :],
                                                                   scalar1=X.nstat[:, c:c + 1], scalar2=None, op0=ALU.mult),
                         reads=[b_hr[c], X.b_nstat], writes=[X.b_hn_bf[i]])

                    def tr(e, i=i):
                        ins = None
                        for kc in range(8):
                            ins = e.transpose(out=tp[:, kc * 128:(kc + 1) * 128], in_=X.hn_bf[i][:, kc * 128:(kc + 1) * 128],
                                              identity=ident)
                        return ins
                    K.op(pe, tr, reads=[X.b_hn_bf[i], b_c], writes=[b_tp])
                    gbc = pvec[:, gcol:gcol + 8].unsqueeze(2).broadcast_to([128, 8, 128])
                    K.op(dve, lambda e, c=c, gbc=gbc: e.tensor_tensor(
                        out=X.hnT[:, :, c * 128:(c + 1) * 128], in0=tp[:].rearrange("p (k t) -> p k t", k=8), in1=gbc,
                        op=ALU.mult), reads=[b_tp, b_pvec], writes=[X.b_hnT[c]])

            def mlp(X, layer, gcol, aux):
                hr, b_hr = X.hres, X.b_hres
                rmsnorm_T(X, gcol)
                mm_set[0] = [3, 4, 5, 6, 0, 1, 2]
                for cb in range(8):
                    wt, bw = load_w(s_w1[layer], 0, cb * 512)
                    for j in range(4):
                        hb = cb * 4 + j
                        bank, bb = mm_bank()

                        def mmf(e, wt=wt, j=j, bank=bank):
                            ins = None
                            for kc in range(8):
                                ins = e.matmul(bank[:, 0:TN], lhsT=wt[:, kc, j * 128:(j + 1) * 128], rhs=X.hnT[:, kc, :],
                                               start=(kc == 0), stop=(kc == 7))
                            return ins
                        K.op(pe, mmf, reads=[bw] + X.b_hnT, writes=[bb])
                        i = hb % 2
                        K.op(act, lambda e, i=i, bank=bank: e.activation(out=X.rr[i][:], in_=bank[:, 0:TN], func=AF.Relu),
                             reads=[bb], writes=[X.b_rr[i]])
                        K.op(aux, lambda e, i=i, hb=hb: e.tensor_tensor(out=X.a1T[:, hb, :], in0=X.rr[i][:], in1=X.rr[i][:],
                                                                         op=ALU.mult),
                             reads=[X.b_rr[i]], writes=[X.b_a1T])
                mm_set[0] = [0, 1, 2]
                for cb in range(2):
                    for kp in range(4):
                        wt, bw = load_w(s_w2[layer], kp * 8, cb * 512)
                        for c in range(TCH):
                            def mmf(e, wt=wt, c=c, kp=kp):
                                ins = None
                                for kc in range(8):
                                    ins = e.matmul(pp[:, c, :], lhsT=X.a1T[:, kp * 8 + kc, c * 128:(c + 1) * 128],
                                                   rhs=wt[:, kc, :], start=(kp == 0 and kc == 0),
                                                   stop=(kp == 3 and kc == 7))
                                return ins
                            K.op(pe, mmf, reads=[bw, X.b_a1T], writes=[b_pp[c]])
                    for c in range(TCH):
                        K.op(dve, lambda e, c=c, cb=cb: e.tensor_tensor(
                            out=hr[:, c, cb * 512:(cb + 1) * 512], in0=hr[:, c, cb * 512:(cb + 1) * 512],
                            in1=pp[:, c, :], op=ALU.add), reads=[b_pp[c], b_hr[c]], writes=[b_hr[c]])

            with ExitStack() as es0:
                B6 = [sb(es0, f"B6_{i}", [128, TCH * D], BF16) for i in range(3)]
                b_B6 = [Buf(f"B6_{i}") for i in range(3)]
                qkd = B6[0][:].rearrange("p (c n) -> p c n", c=TCH)
                vb = B6[1][:].rearrange("p (c n) -> p c n", c=TCH)
                sgb = B6[2][:].rearrange("p (c n) -> p c n", c=TCH)
                b_qkd, b_vb, b_sg = b_B6
                rot = sb(es0, "rot", [128, TCH, 2, 128], F32)
                b_rot = Buf()
                tA = sb(es0, "tA", [128, 512], F32)
                tB = sb(es0, "tB", [128, 512], F32)
                tR = sb(es0, "tR", [128, 512], F32)
                b_tA, b_tB, b_tR = Buf(), Buf(), Buf()
                qkT = [sb(es0, f"qkT{i}", [128, 8, 128], BF16) for i in range(2)]
                b_qkT = [Buf(), Buf()]
                scT = [sb(es0, f"scT{i}", [128, 4, 128], BF16) for i in range(2)]
                b_scT = [Buf(), Buf()]
                Tst = sb(es0, "Tst", [128, 1024], F32)
                Sb = sb(es0, "Sb", [128, 1024], BF16)
                b_Tst, b_Sb = Buf(), Buf()
                gst = sb(es0, "gst", [128, 4, 6], F32)
                gmv = sb(es0, "gmv", [128, 4, 2], F32)
                grs = sb(es0, "grs", [128, 8], F32)
                b_gst, b_gmv, b_grs = Buf(), Buf(), Buf()
                tmpn = sb(es0, "tmpn", [128, 1024], F32)
                b_tmpn = Buf()
                og = sb(es0, "og", [128, 1024], BF16)
                b_og = Buf()
                ocT = sb(es0, "ocT", [128, 16, TN], BF16)
                b_ocT = [Buf(f"ocT{i}") for i in range(16)]
                hdn = sb(es0, "hdn", [128, 8, 30 + TN], BF16)
                b_hdn = [Buf(f"hdn{i}") for i in range(8)]
                dg2 = [sb(es0, f"dg{i}", [128, 31, 128], BF16) for i in range(2)]
                b_dg2 = [[Buf(), Buf(), Buf()], [Buf(), Buf(), Buf()]]
                ysb = sb(es0, "ysb", [128, 8, TN], F32)
                b_ysb = [Buf() for _ in range(8)]
                ybf = [sb(es0, f"ybf{i}", [128, TN], BF16) for i in range(2)]
                ysq = [sb(es0, f"ysq{i}", [128, TN], BF16) for i in range(2)]
                b_ybf, b_ysq = [Buf(), Buf()], [Buf(), Buf()]
                sig = [sb(es0, f"sig{i}", [128, TN], F32) for i in range(2)]
                b_sig = [Buf(), Buf()]
                lmean = sb(es0, "lmean", [128, TN], F32)
                lrstd = sb(es0, "lrstd", [128, TN], F32)
                lt = sb(es0, "lt", [128, TN], F32)
                b_lmean, b_lrstd, b_lt = Buf(), Buf(), Buf()
                ld = [sb(es0, f"ld{i}", [128, TN], F32) for i in range(2)]
                b_ld = [Buf(), Buf()]
                sqh = [sb(es0, f"sqh{i}", [128, TN], BF16) for i in range(3)]
                b_sqh = [Buf(), Buf(), Buf()]
                rsh = [sb(es0, f"rsh{i}", [128, TN], F32) for i in range(3)]
                b_rsh = [Buf(), Buf(), Buf()]

                K.op(dve, lambda e: e.memset(Tst[:], 0.0), writes=[b_Tst])
                K.op(dve, lambda e: e.memset(Sb[:], 0.0), writes=[b_Sb])
                K.op(dve, lambda e: e.memset(hdn[:, :, 0:30], 0.0), writes=b_hdn)
                cgam = [float(np.exp(128.0 * np.log1p(-np.exp2(-5.0 - h)))) for h in range(4)]

                def load_x(tt_):
                    hrx, b_hrx = hres_l[tt_ % 2], b_hres_l[tt_ % 2]
                    if tt_ == 0:
                        K.op(dve, lambda e, hrx=hrx: e.memset(hrx[:, 0, :], 0.0), writes=[b_hrx[0]])
                        K.dma(sp, hrx[PAD:128, 0, :], meta[:, :], writes=[b_hrx[0]])
                        K.dma(sp, hrx[:, 1:TCH, :], x[0:(TCH - 1) * 128, :].rearrange("(c p) d -> p c d", p=128),
                              writes=b_hrx[1:TCH])
                    else:
                        r0 = tt_ * TN - 128
                        K.dma(sp, hrx[:, :, :], x[r0:r0 + TN, :].rearrange("(c p) d -> p c d", p=128), writes=b_hrx)

                for t in range(NT if stop != 'p0t0' else 1):
                    aux = dve
                    X.hres, X.b_hres = hres_l[t % 2], b_hres_l[t % 2]
                    hr, b_hr = X.hres, X.b_hres
                    p0 = t * TN
                    if t == 0:
                        load_x(0)
                    K.dma(sp, rot[:], c_rot[p0:p0 + TN].rearrange("(c p) a d -> p c a d", p=128), writes=[b_rot])

                    rmsnorm_T(X, G_MIX0)
                    def proj_gen(cbs):
                        for cb in cbs:
                            wt, bw = load_w(s_win, 0, cb * 512)
                            for c in range(TCH):
                                bank, bb = mm_bank()

                                def mmf(e, wt=wt, c=c, bank=bank):
                                    ins = None
                                    for kc in range(8):
                                        ins = e.matmul(bank, lhsT=hnT[:, kc, c * 128:(c + 1) * 128], rhs=wt[:, kc, :],
                                                       start=(kc == 0), stop=(kc == 7))
                                    return ins
                                K.op(pe, mmf, reads=[bw, b_hnT[c]], writes=[bb])
                                if cb < 2:
                                    dec = dq if cb == 0 else dk
                                    b3 = bank.rearrange("p (h d) -> p h d", h=4)
                                    cosb = rot[:, c, 0, :].unsqueeze(1).broadcast_to([128, 4, 128])
                                    K.op(dve, lambda e, b3=b3, cosb=cosb: e.tensor_tensor(
                                        out=tA[:].rearrange("p (h d) -> p h d", h=4), in0=b3, in1=cosb, op=ALU.mult),
                                        reads=[bb, b_rot], writes=[b_tA])
                                    s1 = rot[:, c, 1, 0:64].unsqueeze(1).broadcast_to([128, 4, 64])
                                    s2 = rot[:, c, 1, 64:128].unsqueeze(1).broadcast_to([128, 4, 64])
                                    tB3 = tB[:].rearrange("p (h d) -> p h d", h=4)

                                    def rotB(e, b3=b3, s1=s1, s2=s2, tB3=tB3):
                                        e.tensor_tensor(out=tB3[:, :, 0:64], in0=b3[:, :, 64:128], in1=s1, op=ALU.mult)
                                        return e.tensor_tensor(out=tB3[:, :, 64:128], in0=b3[:, :, 0:64], in1=s2, op=ALU.mult)
                                    K.op(dve, rotB, reads=[bb, b_rot], writes=[b_tB])
                                    K.op(aux, lambda e: e.tensor_tensor(out=tR[:], in0=tA[:], in1=tB[:], op=ALU.add),
                                         reads=[b_tA, b_tB], writes=[b_tR])

                                    def decf(e, c=c, cb=cb, dec=dec):
                                        ins = None
                                        for h in range(4):
                                            ins = e.activation(out=qkd[:, c, cb * 512 + h * 128: cb * 512 + (h + 1) * 128],
                                                               in_=tR[:, h * 128:(h + 1) * 128], func=AF.Copy,
                                                               scale=dec[:, h:h + 1])
                                        return ins
                                    K.op(act, decf, reads=[b_tR, b_c], writes=[b_qkd])
                                elif cb < 4:
                                    K.op(act, lambda e, c=c, cb=cb, bank=bank: e.activation(
                                        out=vb[:, c, (cb - 2) * 512:(cb - 1) * 512], in_=bank, func=AF.Copy),
                                        reads=[bb], writes=[b_vb])
                                else:
                                    K.op(act, lambda e, c=c, cb=cb, bank=bank: e.activation(
                                        out=sgb[:, c, (cb - 4) * 512:(cb - 3) * 512], in_=bank, func=AF.Silu),
                                        reads=[bb], writes=[b_sg])
                                    K.op(dve if t < 2 else pool, lambda e, c=c, cb=cb: e.tensor_tensor(
                                        out=sgb[:, c, (cb - 4) * 512:(cb - 3) * 512],
                                        in0=sgb[:, c, (cb - 4) * 512:(cb - 3) * 512],
                                        in1=gng_b[:, (cb - 4) * 512:(cb - 3) * 512], op=ALU.mult),
                                        reads=[b_sg, b_gng], writes=[b_sg])
                                yield


                    mm_set[0] = [0, 1, 2, 3, 4, 5, 6]
                    for _ in proj_gen(range(4)):
                        pass
                    mm_set[0] = [0, 1, 2]

                    def ret_gen():
                        for c in range(TCH):
                            i2 = c % 2

                            def trq(e, c=c):
                                ins = None
                                for kc in range(8):
                                    ins = e.transpose(out=tp[:, kc * 128:(kc + 1) * 128], in_=qkd[:, c, kc * 128:(kc + 1) * 128],
                                                      identity=ident)
                                return ins
                            K.op(pe, trq, reads=[b_qkd, b_c], writes=[b_tp])
                            K.op(act, lambda e, i2=i2: e.activation(out=qkT[i2][:], in_=tp[:].rearrange("p (k t) -> p k t", k=8),
                                                                    func=AF.Copy), reads=[b_tp], writes=[b_qkT[i2]])
                            bank, bb = mm_bank()

                            def scf(e, i2=i2, bank=bank):
                                ins = None
                                for h in range(4):
                                    ins = e.matmul(bank[:, h * 128:(h + 1) * 128], lhsT=qkT[i2][:, 4 + h, :],
                                                   rhs=qkT[i2][:, h, :], start=True, stop=True)
                                return ins
                            K.op(pe, scf, reads=[b_qkT[i2]], writes=[bb])
                            mb = maskT.unsqueeze(1).broadcast_to([128, 4, 128])
                            K.op(dve, lambda e, i2=i2, bank=bank, mb=mb: e.tensor_tensor(
                                out=scT[i2][:], in0=bank.rearrange("p (h t) -> p h t", h=4), in1=mb, op=ALU.mult),
                                reads=[bb, b_c], writes=[b_scT[i2]])
                            yield
                            o_ps = pp[:, 3:5, :].rearrange("p a b -> p (a b)")
                            kv_ps = pp[:, 5:7, :].rearrange("p a b -> p (a b)")

                            def of(e, i2=i2, c=c):
                                ins = None
                                for h in range(4):
                                    e.matmul(o_ps[:, h * 256:(h + 1) * 256], lhsT=scT[i2][:, h, :],
                                             rhs=vb[:, c, h * 256:(h + 1) * 256], start=True, stop=False)
                                    ins = e.matmul(o_ps[:, h * 256:(h + 1) * 256], lhsT=qkT[i2][:, h, :],
                                                   rhs=Sb[:, h * 256:(h + 1) * 256], start=False, stop=True)
                                return ins
                            K.op(pe, of, reads=[b_scT[i2], b_vb, b_qkT[i2], b_Sb], writes=[b_pp[3], b_pp[4]])

                            def kvf(e, c=c):
                                ins = None
                                for h in range(4):
                                    ins = e.matmul(kv_ps[:, h * 256:(h + 1) * 256],
                                                   lhsT=qkd[:, c, 512 + h * 128: 512 + (h + 1) * 128],
                                                   rhs=vb[:, c, h * 256:(h + 1) * 256], start=True, stop=True)
                                return ins
                            K.op(pe, kvf, reads=[b_qkd, b_vb], writes=[b_pp[5], b_pp[6]])

                            def tupd(e):
                                ins = None
                                for h in range(4):
                                    ins = e.scalar_tensor_tensor(out=Tst[:, h * 256:(h + 1) * 256], in0=Tst[:, h * 256:(h + 1) * 256],
                                                                 scalar=cgam[h], in1=kv_ps[:, h * 256:(h + 1) * 256],
                                                                 op0=ALU.mult, op1=ALU.add)
                                return ins
                            K.op(dve, tupd, reads=[b_pp[5], b_pp[6], b_Tst], writes=[b_Tst])

                            def sbf(e):
                                ins = None
                                for h in range(4):
                                    ins = e.activation(out=Sb[:, h * 256:(h + 1) * 256], in_=Tst[:, h * 256:(h + 1) * 256],
                                                       func=AF.Copy, scale=cgam[h])
                                return ins
                            K.op(act, sbf, reads=[b_Tst], writes=[b_Sb])

                            def gstf(e):
                                ins = None
                                for h in range(4):
                                    ins = e.bn_stats(out=gst[:, h, :], in_=o_ps[:, h * 256:(h + 1) * 256])
                                return ins
                            K.op(dve, gstf, reads=[b_pp[3], b_pp[4]], writes=[b_gst])

                            def gagf(e):
                                ins = None
                                for h in range(4):
                                    ins = e.bn_aggr(out=gmv[:, h, :], in_=gst[:, h, :])
                                return ins
                            K.op(dve, gagf, reads=[b_gst], writes=[b_gmv])
                            K.op(act, lambda e: e.activation(out=grs[:, 4:8], in_=gmv[:, :, 1], func=AF.Sqrt, bias=EPS, scale=1.0),
                                 reads=[b_gmv], writes=[b_grs])
                            K.op(dve, lambda e: e.reciprocal(out=grs[:, 0:4], in_=grs[:, 4:8]), reads=[b_grs], writes=[b_grs])

                            def gnf(e):
                                ins = None
                                for h in range(4):
                                    ins = e.tensor_scalar(out=tmpn[:, h * 256:(h + 1) * 256], in0=o_ps[:, h * 256:(h + 1) * 256],
                                                          scalar1=gmv[:, h, 0:1], scalar2=grs[:, h:h + 1],
                                                          op0=ALU.subtract, op1=ALU.mult)
                                return ins
                            K.op(dve, gnf, reads=[b_pp[3], b_pp[4], b_gmv, b_grs], writes=[b_tmpn])
                            K.op(aux, lambda e, c=c: e.tensor_tensor(out=og[:], in0=tmpn[:], in1=sgb[:, c, :], op=ALU.mult),
                                 reads=[b_tmpn, b_sg], writes=[b_og])
                            yield

                            def tro(e):
                                ins = None
                                for kc in range(8):
                                    ins = e.transpose(out=tp[:, kc * 128:(kc + 1) * 128], in_=og[:, kc * 128:(kc + 1) * 128],
                                                      identity=ident)
                                return ins
                            K.op(pe, tro, reads=[b_og, b_c], writes=[b_tp])
                            K.op(act, lambda e, c=c: e.activation(out=ocT[:, 0:8, c * 128:(c + 1) * 128],
                                                                   in_=tp[:].rearrange("p (k t) -> p k t", k=8), func=AF.Copy),
                                 reads=[b_tp], writes=b_ocT[0:8])

                            yield

                    if t > 0:
                        K.op(aux, lambda e: e.tensor_copy(out=hdn[:, :, 0:30], in_=hdn[:, :, TN:TN + 30]),
                             reads=b_hdn, writes=b_hdn)
                    def ag_gen():
                        for j in range(2):
                            wa, bwa = load_w(s_win, 0, 3072 + j * 512)
                            wg, bwg = load_w(s_win, 0, 4096 + j * 512)
                            for cc in range(4):
                                ch = j * 4 + cc
                                banka, bba = mm_bank()
                                bankg, bbg = mm_bank()

                                def mma(e, w=wa, cc=cc, bank=banka):
                                    ins = None
                                    for kc in range(8):
                                        ins = e.matmul(bank[:, 0:TN], lhsT=w[:, kc, cc * 128:(cc + 1) * 128], rhs=hnT[:, kc, :],
                                                       start=(kc == 0), stop=(kc == 7))
                                    return ins
                                K.op(pe, mma, reads=[bwa] + b_hnT, writes=[bba])

                                def mmg(e, w=wg, cc=cc, bank=bankg):
                                    ins = None
                                    for kc in range(8):
                                        ins = e.matmul(bank[:, 0:TN], lhsT=w[:, kc, cc * 128:(cc + 1) * 128], rhs=hnT[:, kc, :],
                                                       start=(kc == 0), stop=(kc == 7))
                                    return ins
                                K.op(pe, mmg, reads=[bwg] + b_hnT, writes=[bbg])
                                i2 = ch % 2
                                K.op(act, lambda e, i2=i2, bank=bankg: e.activation(out=sig[i2][:], in_=bank[:, 0:TN],
                                                                                    func=AF.Sigmoid),
                                     reads=[bbg], writes=[b_sig[i2]])
                                K.op(dve, lambda e, i2=i2, ch=ch, bank=banka: e.tensor_tensor(
                                    out=hdn[:, ch, 30:30 + TN], in0=bank[:, 0:TN], in1=sig[i2][:], op=ALU.mult),
                                    reads=[bba, b_sig[i2]], writes=[b_hdn[ch]])
                                yield
                    import itertools
                    _rg = ret_gen()
                    _fill = itertools.chain(proj_gen(range(4, 6)), ag_gen())
                    _nf = [0]

                    def _fill1():
                        try:
                            next(_fill)
                            _nf[0] += 1
                            return True
                        except StopIteration:
                            return False
                    for _c in range(TCH):
                        for _part in range(3):
                            if _part == 1:
                                while _nf[0] < 4 + _c:
                                    _fill1()
                            _n = 2 if _part == 2 else (0 if (_part == 1 and _c == 0) else 1)
                            for _ in range(_n):
                                _fill1()
                            next(_rg)
                    while _fill1():
                        pass
                    for _ in _rg:
                        pass
                    sum_ps = pp[:, 3, 0:TN]
                    sq_ps = pp[:, 4, 0:TN]
                    def emit_stats(ch):
                        i2 = ch % 2
                        K.op(pe, lambda e, ch=ch, i2=i2: e.matmul(sum_ps, lhsT=ones_bf, rhs=ybf[i2][:], start=(ch == 0),
                                                                  stop=(ch == 7)),
                             reads=[b_ybf[i2], b_c], writes=[b_pp[3]])
                        K.op(pe, lambda e, ch=ch, i2=i2: e.matmul(sq_ps, lhsT=ones_bf, rhs=ysq[i2][:], start=(ch == 0),
                                                                  stop=(ch == 7)),
                             reads=[b_ysq[i2], b_c], writes=[b_pp[4]])
                    def build_dg(ch):
                        cwv = convw[:].rearrange("p (w c) -> p w c", c=8)[:, :, ch]
                        dg = dg2[ch % 2]
                        b_dga, b_dgb, b_dgc = b_dg2[ch % 2]
                        K.op(dve, lambda e, cwv=cwv, dg=dg: e.tensor_tensor(
                            out=dg[:, 0:10, :], in0=ident.unsqueeze(1).broadcast_to([128, 10, 128]),
                            in1=cwv[:, 0:10].unsqueeze(2).broadcast_to([128, 10, 128]), op=ALU.mult),
                            reads=[b_c, b_convw], writes=[b_dga])
                        dg_pe = dve if t < 2 else pool
                        K.op(dg_pe, lambda e, cwv=cwv, dg=dg: e.tensor_tensor(
                            out=dg[:, 10:20, :], in0=ident.unsqueeze(1).broadcast_to([128, 10, 128]),
                            in1=cwv[:, 10:20].unsqueeze(2).broadcast_to([128, 10, 128]), op=ALU.mult),
                            reads=[b_c, b_convw], writes=[b_dgc])

                        def dgb(e, dg=dg, ch=ch):
                            ins = None
                            for w in range(20, 31):
                                ins = e.activation(out=dg[:, w, :], in_=ident, func=AF.Copy,
                                                   scale=convw[:, w * 8 + ch:w * 8 + ch + 1])
                            return ins
                        K.op(act, dgb, reads=[b_c, b_convw], writes=[b_dgb])

                    dg_pe = dve if t < 2 else pool
                    mm_set[0] = [0, 1, 2, 5, 6]
                    build_dg(0)
                    for ch in range(8):
                        dg = dg2[ch % 2]
                        b_dga, b_dgb, b_dgc = b_dg2[ch % 2]
                        bank, bb = mm_bank()

                        def cvf(e, ch=ch, bank=bank, dg=dg):
                            ins = None
                            for w in range(31):
                                ins = e.matmul(bank[:, 0:TN], lhsT=dg[:, w, :], rhs=hdn[:, ch, w:w + TN],
                                               start=(w == 0), stop=(w == 30))
                            return ins
                        K.op(pe, cvf, reads=[b_dga, b_dgb, b_dgc, b_hdn[ch]], writes=[bb])
                        if ch + 1 < 8:
                            build_dg(ch + 1)
                        if ch > 0:
                            emit_stats(ch - 1)
                        i2 = ch % 2
                        K.op(act, lambda e, ch=ch, bank=bank: e.activation(out=ysb[:, ch, :], in_=bank[:, 0:TN],
                                                                          func=AF.Identity, bias=pvec[:, ch:ch + 1]),
                             reads=[bb, b_pvec], writes=[b_ysb[ch]])
                        K.op(dg_pe, lambda e, ch=ch, i2=i2: e.tensor_copy(out=ybf[i2][:], in_=ysb[:, ch, :]),
                             reads=[b_ysb[ch]], writes=[b_ybf[i2]])
                        K.op(act, lambda e, ch=ch, i2=i2, bank=bank: e.activation(out=ysq[i2][:], in_=bank[:, 0:TN],
                                                                                 func=AF.Square, bias=pvec[:, ch:ch + 1]),
                             reads=[bb, b_pvec], writes=[b_ysq[i2]])
                    emit_stats(7)
                    K.op(act, lambda e: e.activation(out=lmean[:], in_=sum_ps, func=AF.Copy, scale=1.0 / D),
                         reads=[b_pp[3]], writes=[b_lmean])
                    K.op(dve, lambda e: e.tensor_tensor(out=lt[:], in0=lmean[:], in1=lmean[:], op=ALU.mult),
                         reads=[b_lmean], writes=[b_lt])
                    K.op(dve, lambda e: e.scalar_tensor_tensor(out=lt[:], in0=sq_ps, scalar=1.0 / D, in1=lt[:],
                                                               op0=ALU.mult, op1=ALU.subtract),
                         reads=[b_pp[4], b_lt], writes=[b_lt])
                    K.op(act, lambda e: e.activation(out=lt[:], in_=lt[:], func=AF.Sqrt, bias=EPS, scale=1.0),
                         reads=[b_lt], writes=[b_lt])
                    K.op(dve, lambda e: e.reciprocal(out=lrstd[:], in_=lt[:]), reads=[b_lt], writes=[b_lrstd])
                    for ch in range(8):
                        i2 = ch % 2
                        K.op(dve, lambda e, ch=ch, i2=i2: e.tensor_tensor(out=ld[i2][:], in0=ysb[:, ch, :], in1=lmean[:],
                                                                          op=ALU.subtract),
                             reads=[b_ysb[ch], b_lmean], writes=[b_ld[i2]])
                        K.op(aux, lambda e, i2=i2: e.tensor_tensor(out=ld[i2][:], in0=ld[i2][:], in1=lrstd[:], op=ALU.mult),
                             reads=[b_ld[i2], b_lrstd], writes=[b_ld[i2]])
                        K.op(act, lambda e, ch=ch, i2=i2: e.activation(out=ocT[:, 8 + ch, :], in_=ld[i2][:], func=AF.Silu,
                                                                       bias=pvec[:, 16 + ch:17 + ch],
                                                                       scale=pvec[:, 8 + ch:9 + ch]),
                             reads=[b_ld[i2], b_pvec], writes=[b_ocT[8 + ch]])

                    mm_set[0] = [0, 1, 2]
                    wo_banks = [[0, 1, 2], [4, 5, 6]]
                    for kp in range(2):
                        for cb in range(2):
                            wt, bw = load_w(s_wout, kp * 8, cb * 512)
                            for c in range(TCH):
                                bi = wo_banks[cb][c]

                                def mmf(e, wt=wt, c=c, kp=kp, bi=bi):
                                    ins = None
                                    for kc in range(8):
                                        ins = e.matmul(pp[:, bi, :], lhsT=ocT[:, kp * 8 + kc, c * 128:(c + 1) * 128],
                                                       rhs=wt[:, kc, :], start=(kp == 0 and kc == 0),
                                                       stop=(kp == 1 and kc == 7))
                                    return ins
                                K.op(pe, mmf, reads=[bw] + b_ocT[kp * 8:kp * 8 + 8], writes=[b_pp[bi]])
                    for cb in range(2):
                        for c in range(TCH):
                            bi = wo_banks[cb][c]
                            K.op(dve, lambda e, c=c, cb=cb, hr=hr, bi=bi: e.tensor_tensor(
                                out=hr[:, c, cb * 512:(cb + 1) * 512], in0=hr[:, c, cb * 512:(cb + 1) * 512],
                                in1=pp[:, bi, :], op=ALU.add), reads=[b_pp[bi], b_hr[c]], writes=[b_hr[c]])
                    if t == 0:
                        K.op(dve, lambda e, hr=hr: e.memset(hr[0:PAD, 0, :], 0.0), reads=[b_hr[0]], writes=[b_hr[0]])
                    mlp(X, 0, G_MLP0, aux)
                    if t + 1 < (NT if stop != 'p0t0' else 1):
                        load_x(t + 1)
                    if debug and "h0" in debug:
                        K.dma(sp, dbg["h0"][p0:p0 + TN, :].rearrange("(c p) d -> p c d", p=128), hr[:, :, :], reads=b_hr)

                    K.dma(sp, s_h[p0:p0 + TN, :].rearrange("(c p) d -> p c d", p=128), hr[:, :, :], reads=b_hr,
                          writes=[scr_buf(s_h)])
                    rmsnorm_T(X, G_MIX1)
                    qTst = B6[0][:].rearrange("p (h t) -> p h t", h=8)
                    kTst = B6[1][:].rearrange("p (h t) -> p h t", h=8)
                    vst = B6[2][:].rearrange("p (c n) -> p c n", c=TCH)
                    mm_set[0] = [0, 1, 2, 3, 4]
                    blk_ctr = [0]
                    qk_items = [(cb, j) for cb in range(4) for j in range(4)]
                    qk_state = {}

                    def qk_A(n):
                        cb, j = qk_items[n]
                        if j == 0:
                            qk_state["w"] = load_w(s_wqkv, 0, cb * 512)
                        wt, bw = qk_state["w"]
                        hp = (cb % 2) * 4 + j
                        bank, bb = mm_bank()

                        def mmf(e, wt=wt, j=j, bank=bank):
                            ins = None
                            for kc in range(8):
                                ins = e.matmul(bank[:, 0:TN], lhsT=wt[:, kc, j * 128:(j + 1) * 128], rhs=hnT[:, kc, :],
                                               start=(kc == 0), stop=(kc == 7))
                            return ins
                        K.op(pe, mmf, reads=[bw] + b_hnT, writes=[bb])
                        i2 = n % 3
                        K.op(act, lambda e, i2=i2, bank=bank: e.activation(out=sqh[i2][:], in_=bank[:, 0:TN],
                                                                          func=AF.Square),
                             reads=[bb], writes=[b_sqh[i2]])
                        qk_state[n] = (bank, bb, hp, cb < 2)

                    def qk_B(n):
                        bank, bb, hp, isq = qk_state[n]
                        dstT = qTst if isq else kTst
                        b_dst = b_B6[0] if isq else b_B6[1]
                        i2 = n % 3
                        _bi = 5 + (blk_ctr[0] % 2)
                        blk_ctr[0] += 1
                        bank2, bb2 = pp[:, _bi, :], b_pp[_bi]
                        K.op(pe, lambda e, i2=i2, bank2=bank2: e.matmul(bank2[:, 0:TN], lhsT=blk, rhs=sqh[i2][:],
                                                                        start=True, stop=True),
                             reads=[b_sqh[i2], b_c], writes=[bb2])
                        K.op(act, lambda e, i2=i2, bank2=bank2: e.activation(out=rsh[i2][:], in_=bank2[:, 0:TN],
                                                                            func=AF.Sqrt, bias=EPS, scale=1.0 / 64),
                             reads=[bb2], writes=[b_rsh[i2]])
                        K.op(dve, lambda e, i2=i2: e.reciprocal(out=rsh[i2][:], in_=rsh[i2][:]),
                             reads=[b_rsh[i2]], writes=[b_rsh[i2]])
                        gc = 0 if isq else 1
                        K.op(dve, lambda e, i2=i2, bank=bank, hp=hp, dstT=dstT, gc=gc: e.scalar_tensor_tensor(
                            out=dstT[:, hp, :], in0=bank[:, 0:TN], scalar=gqk[:, gc:gc + 1], in1=rsh[i2][:],
                            op0=ALU.mult, op1=ALU.mult), reads=[bb, b_rsh[i2], b_gqk], writes=[b_dst])

                    qk_A(0)
                    qk_A(1)
                    for n in range(16):
                        if n + 2 < 16:
                            qk_A(n + 2)
                        qk_B(n)
                    mm_set[0] = [0, 1, 2]
                    for cb in range(2):
                        wt, bw = load_w(s_wqkv, 0, 2048 + cb * 512)
                        for c in range(TCH):
                            bank, bb = mm_bank()

                            def mmf(e, wt=wt, c=c, bank=bank):
                                ins = None
                                for kc in range(8):
                                    ins = e.matmul(bank, lhsT=hnT[:, kc, c * 128:(c + 1) * 128], rhs=wt[:, kc, :],
                                                   start=(kc == 0), stop=(kc == 7))
                                return ins
                            K.op(pe, mmf, reads=[bw, b_hnT[c]], writes=[bb])
                            K.op(act, lambda e, c=c, cb=cb, bank=bank: e.activation(
                                out=vst[:, c, cb * 512:(cb + 1) * 512], in_=bank, func=AF.Copy),
                                reads=[bb], writes=[b_B6[2]])
                    K.dma(sp, s_qT[:, :, p0:p0 + TN].rearrange("h p t -> p h t"), qTst, reads=[b_B6[0]],
                          writes=[scr_buf(s_qT)])
                    K.dma(sp, s_kT[:, :, p0:p0 + TN].rearrange("h p t -> p h t"), kTst, reads=[b_B6[1]],
                          writes=[scr_buf(s_kT)])
                    K.dma(sp, s_v[p0:p0 + TN, :].rearrange("(c p) d -> p c d", p=128), vst, reads=[b_B6[2]],
                          writes=[scr_buf(s_v)])
            K.barrier()
            es_a.close()

            with ExitStack() as es2:
                oT_all = sb(es2, "oT_all", [128, 8, P], BF16)
                b_oT = [Buf(f"oT{i}") for i in range(8)]
                with ExitStack() as es2b:
                    QT = [sb(es2b, f"QT{i}", [128, P], BF16) for i in range(2)]
                    KT = [sb(es2b, f"KT{i}", [128, P], BF16) for i in range(2)]
                    Vh = [sb(es2b, f"Vh{i}", [128, NCHK, 128], BF16) for i in range(2)]
                    b_QT, b_KT, b_Vh = [Buf(), Buf()], [Buf(), Buf()], [Buf(), Buf()]
                    ee = [sb(es2b, f"ee{i}", [128, 2, TN], F32) for i in range(3)]
                    spb = [sb(es2b, f"spb{i}", [128, 2, TN], BF16) for i in range(3)]
                    tt = [sb(es2b, f"tt{i}", [128, 2, TN], F32) for i in range(2)]
                    ww = [sb(es2b, f"ww{i}", [128, 2, TN], BF16) for i in range(2)]
                    b_ee, b_spb, b_tt, b_ww = [Buf(), Buf(), Buf()], [Buf(), Buf(), Buf()], [Buf(), Buf()], [Buf(), Buf()]
                    zps = [pp[:, 0:2, :], pp[:, 2:4, :]]
                    b_z = [[b_pp[0], b_pp[1]], [b_pp[2], b_pp[3]]]
                    accps = pp[:, 4:6, :]
                    b_acc = [b_pp[4], b_pp[5]]
                    ops_ = pp[:, 6, :]
                    b_o = b_pp[6]
                    NG = NT
                    NHP = 8 if stop not in ('p01', 'p0t0') else 0
                    steps = []
                    for hp in range(NHP):
                        for G in range(NG):
                            kb_hi = TCH * G + TCH - 1
                            for kb in range(kb_hi, -1, -1):
                                steps.append((hp, G, kb, kb_hi))
                    nst = len(steps)

                    def emit_load(hp):
                        s = hp % 2
                        K.dma(sp, QT[s][:], s_qT[hp], reads=[scr_buf(s_qT)], writes=[b_QT[s]])
                        K.dma(sp, KT[s][:], s_kT[hp], reads=[scr_buf(s_kT)], writes=[b_KT[s]])
                        for c0 in range(0, NCHK, 11):
                            c1_ = min(NCHK, c0 + 11)
                            K.dma(sp, Vh[s][:, c0:c1_, :],
                                  s_v.rearrange("(c p) d -> p c d", p=128)[:, c0:c1_, hp * 128:(hp + 1) * 128],
                                  reads=[scr_buf(s_v)], writes=[b_Vh[s]])

                    def geo(j):
                        hp, G, kb, kb_hi = steps[j]
                        r = kb - TCH * G
                        q0 = max(r, 0) * 128
                        return hp, G, kb, kb_hi, r, q0, hp % 2, j % 2, j % 3, G * TN

                    def S_zf(j):
                        hp, G, kb, kb_hi, r, q0, s, i2, i3, g0 = geo(j)
                        z = zps[i2]

                        def zf(e, s=s, kb=kb, z=z, q0=q0, g0=g0):
                            ins = None
                            for h in range(2):
                                ins = e.matmul(z[:, h, q0:TN],
                                               lhsT=KT[s][h * 64:(h + 1) * 64, kb * 128:(kb + 1) * 128],
                                               rhs=QT[s][h * 64:(h + 1) * 64, g0 + q0:g0 + TN],
                                               start=True, stop=True)
                            return ins
                        K.op(pe, zf, reads=[b_KT[s], b_QT[s]], writes=b_z[i2])

                    def S_expln(j):
                        hp, G, kb, kb_hi, r, q0, s, i2, i3, g0 = geo(j)
                        z = zps[i2]
                        K.op(act, lambda e, i3=i3, z=z, q0=q0: e.activation(out=ee[i3][:, :, q0:TN], in_=z[:, :, q0:TN],
                                                                           func=AF.Exp),
                             reads=b_z[i2], writes=[b_ee[i3]])
                        K.op(act, lambda e, i3=i3, q0=q0: e.activation(out=spb[i3][:, :, q0:TN], in_=ee[i3][:, :, q0:TN],
                                                                      func=AF.Ln, bias=1.0, scale=1.0),
                             reads=[b_ee[i3]], writes=[b_spb[i3]])
                        if r >= 0:
                            mb = mT.unsqueeze(1).broadcast_to([128, 2, 128])
                            K.op(dve, lambda e, i3=i3, q0=q0, mb=mb: e.tensor_tensor(
                                out=ee[i3][:, :, q0:q0 + 128], in0=ee[i3][:, :, q0:q0 + 128], in1=mb, op=ALU.mult),
                                reads=[b_ee[i3], b_c], writes=[b_ee[i3]])
                            K.op(dve, lambda e, i3=i3, q0=q0, mb=mb: e.tensor_tensor(
                                out=spb[i3][:, :, q0:q0 + 128], in0=spb[i3][:, :, q0:q0 + 128], in1=mb, op=ALU.mult),
                                reads=[b_spb[i3], b_c], writes=[b_spb[i3]])
                        if kb == 0:
                            K.op(dve, lambda e, i3=i3, q0=q0: e.memset(ee[i3][0:PAD, :, q0:TN], 0.0),
                                 reads=[b_ee[i3]], writes=[b_ee[i3]])
                            K.op(dve, lambda e, i3=i3, q0=q0: e.memset(spb[i3][0:PAD, :, q0:TN], 0.0),
                                 reads=[b_spb[i3]], writes=[b_spb[i3]])

                    def S_c1(j):
                        hp, G, kb, kb_hi, r, q0, s, i2, i3, g0 = geo(j)

                        def c1(e, i3=i3, q0=q0, first=(kb == kb_hi)):
                            ins = None
                            for h in range(2):
                                ins = e.matmul(accps[:, h, q0:TN], lhsT=tri, rhs=spb[i3][:, h, q0:TN],
                                               start=first, stop=True, skip_group_check=True)
                            return ins
                        K.op(pe, c1, reads=[b_spb[i3], b_c], writes=b_acc)

                    def S_exp2(j):
                        hp, G, kb, kb_hi, r, q0, s, i2, i3, g0 = geo(j)
                        K.op(act, lambda e, i2=i2, q0=q0: e.activation(out=tt[i2][:, :, q0:TN], in_=accps[:, :, q0:TN],
                                                                      func=AF.Exp, scale=-1.0),
                             reads=b_acc, writes=[b_tt[i2]])

                    def S_c2(j):
                        hp, G, kb, kb_hi, r, q0, s, i2, i3, g0 = geo(j)
                        if kb > 0:
                            def c2(e, i3=i3, q0=q0):
                                ins = None
                                for h in range(2):
                                    ins = e.matmul(accps[:, h, q0:TN], lhsT=stri, rhs=spb[i3][:, h, q0:TN],
                                                   start=False, stop=True, skip_group_check=True)
                                return ins
                            K.op(pe, c2, reads=[b_spb[i3], b_c], writes=b_acc)

                    def S_mult(j):
                        hp, G, kb, kb_hi, r, q0, s, i2, i3, g0 = geo(j)
                        K.op(dve, lambda e, i2=i2, i3=i3, q0=q0: e.tensor_tensor(
                            out=ww[i2][:, :, q0:TN], in0=ee[i3][:, :, q0:TN], in1=tt[i2][:, :, q0:TN], op=ALU.mult),
                            reads=[b_ee[i3], b_tt[i2]], writes=[b_ww[i2]])

                    def S_wv(j):
                        hp, G, kb, kb_hi, r, q0, s, i2, i3, g0 = geo(j)

                        def wv(e, s=s, i2=i2, kb=kb, q0=q0, first=(kb == kb_hi)):
                            ins = None
                            for h in range(2):
                                ins = e.matmul(ops_[h * 64:(h + 1) * 64, q0:TN], lhsT=Vh[s][:, kb, h * 64:(h + 1) * 64],
                                               rhs=ww[i2][:, h, q0:TN], start=first, stop=True, skip_group_check=True)
                            return ins
                        K.op(pe, wv, reads=[b_Vh[s], b_ww[i2]], writes=[b_o])
                        if kb == 0:
                            K.op(dve, lambda e, hp=hp, g0=g0: e.tensor_copy(out=oT_all[:, hp, g0:g0 + TN], in_=ops_[:, 0:TN]),
                                 reads=[b_o], writes=[b_oT[hp]])
                        if (j == 0 or steps[j - 1][0] != hp) and hp + 1 < NHP:
                            emit_load(hp + 1)

                    if nst:
                        emit_load(0)
                        S_zf(0)
                        if nst > 1:
                            S_zf(1)
                        S_expln(0)
                        if nst > 2:
                            S_zf(2)
                        if nst > 1:
                            S_expln(1)
                        S_c1(0)
                    for i in range(nst):
                        S_exp2(i)
                        S_c2(i)
                        if i + 1 < nst:
                            S_c1(i + 1)
                        S_mult(i)
                        if i >= 1:
                            S_wv(i - 1)
                        if i + 3 < nst:
                            S_zf(i + 3)
                        if i + 2 < nst:
                            S_expln(i + 2)
                    if nst:
                        S_wv(nst - 1)
                K.barrier()

                es3 = es2.enter_context(ExitStack())
                X3 = alloc_tile(es3, "b")
                hres3 = [X3.hres, sb(es3, "hresb2", [128, TCH, D], F32)]
                b_hres3 = [X3.b_hres, [Buf(f"hres2_{c}") for c in range(TCH)]]
                for t in range((1 if stop == 'p3t0' else NT) if stop not in ('p01', 'p0t0', 'p2') else 0):
                    X3.hres, X3.b_hres = hres3[t % 2], b_hres3[t % 2]
                    hr3, b_hr3 = X3.hres, X3.b_hres
                    p0 = t * TN
                    K.dma(sp, hr3[:, :, :], s_h[p0:p0 + TN, :].rearrange("(c p) d -> p c d", p=128),
                          reads=[scr_buf(s_h)], writes=b_hr3)
                    for cb in range(2):
                        wt, bw = load_w(s_wo, 0, cb * 512)
                        for c in range(TCH):
                            def mmf(e, wt=wt, c=c, p0=p0):
                                ins = None
                                for kc in range(8):
                                    ins = e.matmul(pp[:, c, :], lhsT=oT_all[:, kc, p0 + c * 128:p0 + (c + 1) * 128],
                                                   rhs=wt[:, kc, :], start=(kc == 0), stop=(kc == 7))
                                return ins
                            K.op(pe, mmf, reads=[bw] + b_oT, writes=[b_pp[c]])
                        for c in range(TCH):
                            K.op(dve, lambda e, c=c, cb=cb, hr3=hr3: e.tensor_tensor(
                                out=hr3[:, c, cb * 512:(cb + 1) * 512], in0=hr3[:, c, cb * 512:(cb + 1) * 512],
                                in1=pp[:, c, :], op=ALU.add), reads=[b_pp[c], b_hr3[c]], writes=[b_hr3[c]])
                    mlp(X3, 1, G_MLP1, pool)
                    if t == 0:
                        K.dma(sp, out[0:(TCH - 1) * 128, :].rearrange("(c p) d -> p c d", p=128), hr3[:, 1:TCH, :],
                              reads=b_hr3[1:TCH])
                    else:
                        r0 = p0 - 128
                        K.dma(sp, out[r0:r0 + TN, :].rearrange("(c p) d -> p c d", p=128), hr3[:, :, :], reads=b_hr3)
            K.finish()
            K.emit()
    return nc


_CACHE = {}


def _get_nc(SEQ, debug=None):
    key = (SEQ, None if debug is None else tuple(sorted(debug)))
    if key not in _CACHE:
        _CACHE[key] = build(SEQ, debug)
    return _CACHE[key]


def make_in_maps(inputs, SEQ, ncores):
    P = PAD + NMETA + SEQ
    cb, cf, rot = host_consts(P)
    f = lambda a: np.ascontiguousarray(np.asarray(a, dtype=np.float32))
    shared = {
        "meta": f(inputs["meta"]),
        "norm_mix_g": f(inputs["norm_mix_g"]),
        "norm_mlp_g": f(inputs["norm_mlp_g"]),
        "even_w_in": f(inputs["even_w_in"][0]),
        "even_ret_gn_g": f(inputs["even_ret_gn_g"][0]).reshape(1024),
        "even_conv_w": f(inputs["even_conv_w"][0]),
        "even_conv_b": f(inputs["even_conv_b"][0]),
        "even_conv_ln_g": f(inputs["even_conv_ln_g"][0]),
        "even_conv_ln_b": f(inputs["even_conv_ln_b"][0]),
        "even_w_out": f(inputs["even_w_out"][0]),
        "odd_w_qkv": f(inputs["odd_w_qkv"][0]),
        "odd_q_norm_g": f(inputs["odd_q_norm_g"][0]),
        "odd_k_norm_g": f(inputs["odd_k_norm_g"][0]),
        "odd_w_o": f(inputs["odd_w_o"][0]),
        "mlp_w1_0": f(inputs["mlp_w1"][0]),
        "mlp_w1_1": f(inputs["mlp_w1"][1]),
        "mlp_w2_0": f(inputs["mlp_w2"][0]),
        "mlp_w2_1": f(inputs["mlp_w2"][1]),
        "c_bf": cb,
        "c_f": cf,
        "c_rot": rot,
    }
    xs = np.asarray(inputs["x"], dtype=np.float32)
    maps = []
    for b in range(ncores):
        m = dict(shared)
        m["x"] = np.ascontiguousarray(xs[b])
        maps.append(m)
    return maps


def kernel(**inputs):
    x = np.asarray(inputs["x"])
    B, SEQ, _ = x.shape
    nc = _get_nc(SEQ)
    maps = make_in_maps(inputs, SEQ, B)
    res = run_bass_kernel_spmd(nc, maps, core_ids=list(range(B)))
    return np.stack([np.asarray(r["out"], dtype=np.float32) for r in res.results], axis=0)
```

```python
import numpy as np
import ml_dtypes
from contextlib import ExitStack
import concourse.bass as bass
import concourse.mybir as mybir
from concourse.bass_utils import run_bass_kernel_spmd

F32 = mybir.dt.float32
BF16 = mybir.dt.bfloat16
AF = mybir.ActivationFunctionType
ALU = mybir.AluOpType
AX = mybir.AxisListType

D = 1024
NMETA = 16
PAD = 112
EPS = 1e-6
TCH = 3
TN = 128 * TCH
DFF = 4096
NCORES = 8


class Buf:
    __slots__ = ("w", "r", "name")

    def __init__(self, name=""):
        self.w = None
        self.r = {}
        self.name = name


class Producer:
    def __init__(self, name, unit):
        self.name = name
        self.unit = unit
        self.sem = None
        self.count = 0


class Eng(Producer):
    def __init__(self, name, is_pe=False):
        super().__init__(name, 1)
        self.ops = []
        self.known = {}
        self.is_pe = is_pe
        self.lanes = []
        self.lane_i = 0


class Prog:
    def __init__(self, nc, es):
        self.nc = nc
        self.es = es
        self.pe = Eng("pe", True)
        self.act = Eng("act")
        self.dve = Eng("dve")
        self.pool = Eng("pool")
        self.sp = Eng("sp")
        self.engs = [self.pe, self.act, self.dve, self.pool, self.sp]
        self.prods = list(self.engs)
        for q in (self.sp, self.pool, self.act):
            for i in range(8):
                ln = Producer(f"{q.name}_l{i}", 16)
                q.lanes.append(ln)
                self.prods.append(ln)
        for p in self.prods:
            p.sem = es.enter_context(nc.semaphore("s_" + p.name))

    def op(self, eng, thunk, reads=(), writes=(), lane=None):
        d = {}
        for b in reads:
            if b.w is not None:
                p, i = b.w
                if d.get(p, 0) < i:
                    d[p] = i
        for b in writes:
            if b.w is not None:
                p, i = b.w
                if d.get(p, 0) < i:
                    d[p] = i
            for p, i in b.r.items():
                if d.get(p, 0) < i:
                    d[p] = i
        prod = lane if lane is not None else eng
        if lane is not None and lane.count > 0:
            d[lane] = lane.count
        for p, i in d.items():
            if p is eng and eng.is_pe:
                continue
            if eng.known.get(p, 0) < i:
                eng.ops.append(("w", p, i))
                eng.known[p] = i
        prod.count += 1
        idx = prod.count
        eng.ops.append(("i", thunk, prod))
        for b in reads:
            b.r[prod] = idx
        for b in writes:
            b.w = (prod, idx)
            b.r = {}

    def dma(self, q, out, in_, reads=(), writes=()):
        lane = q.lanes[q.lane_i % len(q.lanes)]
        q.lane_i += 1
        self.op(q, lambda e, o=out, i=in_: e.dma_start(out=o, in_=i), reads, writes, lane=lane)

    def barrier(self):
        for e in self.engs:
            for p in self.prods:
                if p.count > 0 and e.known.get(p, 0) < p.count and not (p is e):
                    e.ops.append(("w", p, p.count))
                    e.known[p] = p.count

    def finish(self):
        for q in (self.sp, self.pool, self.act):
            for ln in q.lanes:
                if ln.count > 0 and q.known.get(ln, 0) < ln.count:
                    q.ops.append(("w", ln, ln.count))
                    q.known[ln] = ln.count
        for p in self.prods:
            if p.count > 0 and p is not self.sp and self.sp.known.get(p, 0) < p.count:
                self.sp.ops.append(("w", p, p.count))
                self.sp.known[p] = p.count

    def emit(self):
        nc = self.nc

        def replay(eng, e):
            for o in eng.ops:
                if o[0] == "w":
                    e.wait_ge(o[1].sem, o[2] * o[1].unit)
                else:
                    ins = o[1](e)
                    ins.then_inc(o[2].sem, o[2].unit)

        with nc.Block() as block:
            @block.sync
            def _(e):
                replay(self.sp, e)

            @block.scalar
            def _(e):
                replay(self.act, e)

            @block.vector
            def _(e):
                replay(self.dve, e)

            @block.gpsimd
            def _(e):
                replay(self.pool, e)

            @block.tensor
            def _(e):
                replay(self.pe, e)


def host_consts(P):
    nch = P // 128
    cb = np.zeros((128, 5, 128), np.float32)
    j = np.arange(128)[:, None]
    k = np.arange(128)[None, :]
    cb[:, 0] = (j == k)
    cb[:, 1] = 1.0
    cb[:, 2] = (j >= k)
    cb[:, 3] = (j < k)
    cb[:, 4] = ((j // 64) == (k // 64))
    cf = np.zeros((128, 3 * 128 + 8), np.float32)
    cf[:, 0:128] = (j == k)
    cf[:, 128:256] = (k >= j)
    cf[:, 256:384] = (j < k)
    idx = np.arange(128, dtype=np.float64)
    for h in range(4):
        lg = np.log1p(-np.exp2(-5.0 - h))
        cf[:, 384 + h] = np.exp((idx - 127.0) * lg)
        cf[:, 388 + h] = np.exp((127.0 - idx) * lg) * (128.0 ** -0.5)
    half = 64
    inv_freq = (10000.0 ** (-np.arange(half, dtype=np.float32) / half)).astype(np.float32)
    ang = np.arange(P, dtype=np.float32)[:, None] * inv_freq[None, :]
    cos = np.cos(ang).astype(np.float32)
    sin = np.sin(ang).astype(np.float32)
    rot = np.zeros((P, 2, 128), np.float32)
    rot[:, 0, :64] = cos
    rot[:, 0, 64:] = cos
    rot[:, 1, :64] = -sin
    rot[:, 1, 64:] = sin
    return cb.astype(ml_dtypes.bfloat16), cf, rot


def build(SEQ, debug=None, stop=None):
    P = PAD + NMETA + SEQ
    NCHK = P // 128
    assert NCHK % TCH == 0
    NT = NCHK // TCH
    nc = bass.Bass("TRN2", target_bir_lowering=False)

    def din(name, shape, dt=F32):
        return nc.dram_tensor(name, list(shape), dt, kind="ExternalInput").ap()

    def dscr(name, shape, dt):
        return nc.dram_tensor(name, list(shape), dt, kind="Internal").ap()

    x = din("x", [SEQ, D])
    meta = din("meta", [NMETA, D])
    norm_mix_g = din("norm_mix_g", [2, D])
    norm_mlp_g = din("norm_mlp_g", [2, D])
    w_in = din("even_w_in", [D, 5120])
    gn_g = din("even_ret_gn_g", [1024])
    conv_w = din("even_conv_w", [31, D])
    conv_b = din("even_conv_b", [D])
    ln_g = din("even_conv_ln_g", [D])
    ln_b = din("even_conv_ln_b", [D])
    w_out = din("even_w_out", [2048, D])
    w_qkv = din("odd_w_qkv", [D, 3072])
    qn_g = din("odd_q_norm_g", [64])
    kn_g = din("odd_k_norm_g", [64])
    w_o = din("odd_w_o", [D, D])
    w1 = [din("mlp_w1_0", [D, DFF]), din("mlp_w1_1", [D, DFF])]
    w2 = [din("mlp_w2_0", [DFF, D]), din("mlp_w2_1", [DFF, D])]
    c_bf = din("c_bf", [128, 5, 128], BF16)
    c_f = din("c_f", [128, 392])
    c_rot = din("c_rot", [P, 2, 128])
    out = nc.dram_tensor("out", [SEQ, D], F32, kind="ExternalOutput").ap()
    dbg = None
    if debug:
        dbg = {k: nc.dram_tensor("dbg_" + k, list(s), F32, kind="ExternalOutput").ap() for k, s in debug.items()}

    s_win = dscr("s_win", [D, 5120], BF16)
    s_wout = dscr("s_wout", [2048, D], BF16)
    s_w1 = [dscr("s_w1_0", [D, DFF], BF16), dscr("s_w1_1", [D, DFF], BF16)]
    s_w2 = [dscr("s_w2_0", [DFF, D], BF16), dscr("s_w2_1", [DFF, D], BF16)]
    s_wqkv = dscr("s_wqkv", [D, 3072], BF16)
    s_wo = dscr("s_wo", [D, D], BF16)
    s_qT = dscr("s_qT", [8, 128, P], BF16)
    s_kT = dscr("s_kT", [8, 128, P], BF16)
    s_v = dscr("s_v", [P, D], BF16)
    s_h = dscr("s_h", [P, D], F32)

    with ExitStack() as es:
        K = Prog(nc, es)
        pe, act, dve, pool, sp = K.pe, K.act, K.dve, K.pool, K.sp

        def sb(stack, name, shape, dt):
            return stack.enter_context(nc.sbuf_tensor(name, list(shape), dt))

        tp = es.enter_context(nc.psum_tensor("tp", [128, 1024], BF16))
        pp = es.enter_context(nc.psum_tensor("pp", [128, 7, 512], F32))
        b_tp = Buf("tp")
        b_pp = [Buf(f"pp{i}") for i in range(7)]

        cbf = sb(es, "cbf", [128, 5, 128], BF16)
        cf = sb(es, "cf", [128, 392], F32)
        b_c = Buf("consts")
        K.dma(sp, cbf[:], c_bf[:, :, :], writes=[b_c])
        K.dma(sp, cf[:], c_f[:, :], writes=[b_c])
        ident = cbf[:, 0, :]
        ones_bf = cbf[:, 1, :]
        tri = cbf[:, 2, :]
        stri = cbf[:, 3, :]
        blk = cbf[:, 4, :]
        ident_f = cf[:, 0:128]
        maskT = cf[:, 128:256]
        mT = cf[:, 256:384]
        dq = cf[:, 384:388]
        dk = cf[:, 388:392]

        vstage = sb(es, "vstage", [128, 128], F32)
        vstage2 = sb(es, "vstage2", [128, 128], F32)
        convw = sb(es, "convw", [128, 248], F32)
        pvec = sb(es, "pvec", [128, 64], F32)
        gqk = sb(es, "gqk", [128, 2], F32)
        gng_b = sb(es, "gng_b", [128, 1024], F32)
        b_vs, b_vs2, b_convw, b_pvec, b_gqk, b_gng = Buf(), Buf(), Buf(), Buf(), Buf(), Buf()
        K.dma(sp, gng_b[:], gn_g.partition_broadcast(128), writes=[b_gng])
        cw = conv_w.rearrange("w (c p) -> (w c) p", p=128)
        K.dma(sp, vstage[0:124, :], cw[0:124, :], writes=[b_vs])
        K.dma(sp, vstage2[0:124, :], cw[124:248, :], writes=[b_vs2])
        ptf = pp[:, 0, :]
        K.op(pe, lambda e: e.transpose(out=ptf[:, 0:124], in_=vstage[0:124, :], identity=ident_f[0:124, 0:124]),
             reads=[b_vs, b_c], writes=[b_pp[0]])
        K.op(pe, lambda e: e.transpose(out=ptf[:, 124:248], in_=vstage2[0:124, :], identity=ident_f[0:124, 0:124]),
             reads=[b_vs2, b_c], writes=[b_pp[0]])
        K.op(dve, lambda e: e.tensor_copy(out=convw[:], in_=ptf[:, 0:248]), reads=[b_pp[0]], writes=[b_convw])
        vecs = [conv_b, ln_g, ln_b, norm_mix_g[0], norm_mlp_g[0], norm_mix_g[1], norm_mlp_g[1]]
        for i, v in enumerate(vecs):
            K.dma(sp, vstage[8 * i:8 * i + 8, :], v.rearrange("(c p) -> c p", p=128), writes=[b_vs])
        K.op(pe, lambda e: e.transpose(out=pp[:, 1, 0:56], in_=vstage[0:56, :], identity=ident_f[0:56, 0:56]),
             reads=[b_vs, b_c], writes=[b_pp[1]])
        K.op(dve, lambda e: e.tensor_copy(out=pvec[:, 0:56], in_=pp[:, 1, 0:56]), reads=[b_pp[1]], writes=[b_pvec])
        for hh in range(2):
            K.dma(sp, gqk[hh * 64:(hh + 1) * 64, 0:1], qn_g.rearrange("(p o) -> p o", o=1), writes=[b_gqk])
            K.dma(sp, gqk[hh * 64:(hh + 1) * 64, 1:2], kn_g.rearrange("(p o) -> p o", o=1), writes=[b_gqk])
        K.op(dve, lambda e: e.tensor_scalar(out=gqk[:, 0:1], in0=gqk[:, 0:1], scalar1=0.125, scalar2=None, op0=ALU.mult),
             reads=[b_gqk], writes=[b_gqk])
        G_MIX0, G_MLP0, G_MIX1, G_MLP1 = 24, 32, 40, 48

        NSLOT = 3
        wslot = [sb(es, f"wslot{i}", [128, 8, 512], BF16) for i in range(NSLOT)]
        b_wslot = [Buf(f"wslot{i}") for i in range(NSLOT)]
        wctr = [0]
        b_scr = {}

        def scr_buf(t):
            return b_scr.setdefault(id(t), Buf("scr"))

        def load_w(ws, k0, c0, nk=8):
            i = wctr[0] % NSLOT
            wctr[0] += 1
            src = ws.rearrange("(kc p) n -> p kc n", p=128)[:, k0:k0 + nk, c0:c0 + 512]
            K.dma(sp, wslot[i][:, 0:nk, :], src, reads=[scr_buf(ws)], writes=[b_wslot[i]])
            return wslot[i], b_wslot[i]

        class _NS:
            pass

        def alloc_tile(stack, tag):
            X = _NS()
            X.a1T = sb(stack, "a1T" + tag, [128, 32, TN], BF16)
            X.b_a1T = Buf("a1T")
            X.hres = sb(stack, "hres" + tag, [128, TCH, D], F32)
            X.b_hres = [Buf(f"hres{c}") for c in range(TCH)]
            X.hnT = sb(stack, "hnT" + tag, [128, 8, TN], BF16)
            X.b_hnT = [Buf(f"hnT{c}") for c in range(TCH)]
            X.hn_bf = [sb(stack, f"hn_bf{i}" + tag, [128, D], BF16) for i in range(2)]
            X.b_hn_bf = [Buf(), Buf()]
            X.junk = sb(stack, "junk" + tag, [128, D], BF16)
            X.b_junk = Buf()
            X.nstat = sb(stack, "nstat" + tag, [128, 8], F32)
            X.b_nstat = Buf()
            X.rr = [sb(stack, f"rr{i}" + tag, [128, TN], F32) for i in range(2)]
            X.b_rr = [Buf(), Buf()]
            return X

        with ExitStack() as es_a:
            def convert(wsrc, wdst):
                Kr, N = wsrc.shape
                rows = 128 * max(1, 1024 // N)
                for r0 in range(0, Kr, rows):
                    lane = pool.lanes[pool.lane_i % len(pool.lanes)]
                    pool.lane_i += 1
                    K.op(pool, lambda e, o=wdst[r0:r0 + rows, :], i=wsrc[r0:r0 + rows, :]: e.dma_start(
                        out=o, in_=i, max_dma_last_dim=4096), writes=[scr_buf(wdst)], lane=lane)

            convert(w_in, s_win)
            convert(w_out, s_wout)
            convert(w1[0], s_w1[0])
            convert(w2[0], s_w2[0])
            convert(w_qkv, s_wqkv)
            convert(w_o, s_wo)
            convert(w1[1], s_w1[1])
            convert(w2[1], s_w2[1])

            X = alloc_tile(es_a, "a")
            hnT, b_hnT = X.hnT, X.b_hnT
            hres_l = [X.hres, sb(es_a, "hresa2", [128, TCH, D], F32)]
            b_hres_l = [X.b_hres, [Buf(f"hresa2_{c}") for c in range(TCH)]]
            mmctr = [0]
            hnctr = [0]

            mm_set = [[0, 1, 2]]

            def mm_bank():
                st = mm_set[0]
                i = st[mmctr[0] % len(st)]
                mmctr[0] += 1
                return pp[:, i, :], b_pp[i]

            def rmsnorm_T(X, gcol):
                hr, b_hr = X.hres, X.b_hres
                for c in range(TCH):
                    K.op(act, lambda e, c=c: e.activation(out=X.junk[:], in_=hr[:, c, :], func=AF.Square,
                                                          accum_out=X.nstat[:, c:c + 1]),
                         reads=[b_hr[c]], writes=[X.b_junk, X.b_nstat])
                K.op(act, lambda e: e.activation(out=X.nstat[:, 4:4 + TCH], in_=X.nstat[:, 0:TCH], func=AF.Sqrt,
                                                 bias=EPS, scale=1.0 / D),
                     reads=[X.b_nstat], writes=[X.b_nstat])
                K.op(dve, lambda e: e.reciprocal(out=X.nstat[:, 0:TCH], in_=X.nstat[:, 4:4 + TCH]),
                     reads=[X.b_nstat], writes=[X.b_nstat])
                tps = [(tp, b_tp), (pp[:, 3, :].bitcast(BF16), b_pp[3])]
                gbc = pvec[:, gcol:gcol + 8].unsqueeze(2).broadcast_to([128, 8, 128])

                def n_scale(c):
                    i = hnctr[0] % 2
                    hnctr[0] += 1
                    K.op(dve, lambda e, c=c, i=i: e.tensor_scalar(out=X.hn_bf[i][:], in0=hr[:, c, :],
                                                                   scalar1=X.nstat[:, c:c + 1], scalar2=None, op0=ALU.mult),
                         reads=[b_hr[c], X.b_nstat], writes=[X.b_hn_bf[i]])
                    return i

                def n_tr(i, k):
                    tpk, b_tpk = tps[k]

                    def tr(e, i=i, tpk=tpk):
                        ins = None
                        for kc in range(8):
                            ins = e.transpose(out=tpk[:, kc * 128:(kc + 1) * 128], in_=X.hn_bf[i][:, kc * 128:(kc + 1) * 128],
                                              identity=ident)
                        return ins
                    K.op(pe, tr, reads=[X.b_hn_bf[i], b_c], writes=[b_tpk])

                def n_evac(c, k):
                    tpk, b_tpk = tps[k]
                    K.op(dve, lambda e, c=c, tpk=tpk: e.tensor_tensor(
                        out=X.hnT[:, :, c * 128:(c + 1) * 128], in0=tpk.rearrange("p (k t) -> p k t", k=8), in1=gbc,
                        op=ALU.mult), reads=[b_tpk, b_pvec], writes=[X.b_hnT[c]])

                i0 = n_scale(0)
                i1 = n_scale(1)
                n_tr(i0, 0)
                n_tr(i1, 1)
                n_evac(0, 0)
                i2_ = n_scale(2)
                n_tr(i2_, 0)
                n_evac(1, 1)
                n_evac(2, 0)

            def mlp(X, layer, gcol, aux):
                hr, b_hr = X.hres, X.b_hres
                rmsnorm_T(X, gcol)
                mm_set[0] = [3, 4, 5, 6, 0, 1, 2]
                for cb in range(8):
                    wt, bw = load_w(s_w1[layer], 0, cb * 512)
                    for j in range(4):
                        hb = cb * 4 + j
                        bank, bb = mm_bank()

                        def mmf(e, wt=wt, j=j, bank=bank):
                            ins = None
                            for kc in range(8):
                                ins = e.matmul(bank[:, 0:TN], lhsT=wt[:, kc, j * 128:(j + 1) * 128], rhs=X.hnT[:, kc, :],
                                               start=(kc == 0), stop=(kc == 7))
                            return ins
                        K.op(pe, mmf, reads=[bw] + X.b_hnT, writes=[bb])
                        i = hb % 2
                        K.op(act, lambda e, i=i, bank=bank: e.activation(out=X.rr[i][:], in_=bank[:, 0:TN], func=AF.Relu),
                             reads=[bb], writes=[X.b_rr[i]])
                        K.op(aux, lambda e, i=i, hb=hb: e.tensor_tensor(out=X.a1T[:, hb, :], in0=X.rr[i][:], in1=X.rr[i][:],
                                                                         op=ALU.mult),
                             reads=[X.b_rr[i]], writes=[X.b_a1T])
                mm_set[0] = [0, 1, 2]
                for cb in range(2):
                    for kp in range(4):
                        wt, bw = load_w(s_w2[layer], kp * 8, cb * 512)
                        for c in range(TCH):
                            def mmf(e, wt=wt, c=c, kp=kp):
                                ins = None
                                for kc in range(8):
                                    ins = e.matmul(pp[:, c, :], lhsT=X.a1T[:, kp * 8 + kc, c * 128:(c + 1) * 128],
                                                   rhs=wt[:, kc, :], start=(kp == 0 and kc == 0),
                                                   stop=(kp == 3 and kc == 7))
                                return ins
                            K.op(pe, mmf, reads=[bw, X.b_a1T], writes=[b_pp[c]])
                    for c in range(TCH):
                        K.op(dve, lambda e, c=c, cb=cb: e.tensor_tensor(
                            out=hr[:, c, cb * 512:(cb + 1) * 512], in0=hr[:, c, cb * 512:(cb + 1) * 512],
                            in1=pp[:, c, :], op=ALU.add), reads=[b_pp[c], b_hr[c]], writes=[b_hr[c]])

            with ExitStack() as es0:
                B6 = [sb(es0, f"B6_{i}", [128, TCH * D], BF16) for i in range(3)]
                b_B6 = [Buf(f"B6_{i}") for i in range(3)]
                qkd = B6[0][:].rearrange("p (c n) -> p c n", c=TCH)
                vb = B6[1][:].rearrange("p (c n) -> p c n", c=TCH)
                sgb = B6[2][:].rearrange("p (c n) -> p c n", c=TCH)
                b_qkd, b_vb, b_sg = b_B6
                rot = sb(es0, "rot", [128, TCH, 2, 128], F32)
                b_rot = Buf()
                tA = sb(es0, "tA", [128, 512], F32)
                tB = sb(es0, "tB", [128, 512], F32)
                tR = sb(es0, "tR", [128, 512], F32)
                b_tA, b_tB, b_tR = Buf(), Buf(), Buf()
                qkT = [sb(es0, f"qkT{i}", [128, 8, 128], BF16) for i in range(2)]
                b_qkT = [Buf(), Buf()]
                scT = [sb(es0, f"scT{i}", [128, 4, 128], BF16) for i in range(2)]
                b_scT = [Buf(), Buf()]
                Tst = sb(es0, "Tst", [128, 1024], F32)
                Sb = sb(es0, "Sb", [128, 1024], BF16)
                b_Tst, b_Sb = Buf(), Buf()
                gst = sb(es0, "gst", [128, 4, 6], F32)
                gmv = sb(es0, "gmv", [128, 4, 2], F32)
                grs = sb(es0, "grs", [128, 8], F32)
                b_gst, b_gmv, b_grs = Buf(), Buf(), Buf()
                tmpn = sb(es0, "tmpn", [128, 1024], F32)
                b_tmpn = Buf()
                og = sb(es0, "og", [128, 1024], BF16)
                b_og = Buf()
                ocT = sb(es0, "ocT", [128, 16, TN], BF16)
                b_ocT = [Buf(f"ocT{i}") for i in range(16)]
                hdn = sb(es0, "hdn", [128, 8, 30 + TN], BF16)
                b_hdn = [Buf(f"hdn{i}") for i in range(8)]
                dg2 = [sb(es0, f"dg{i}", [128, 31, 128], BF16) for i in range(2)]
                b_dg2 = [[Buf(), Buf(), Buf()], [Buf(), Buf(), Buf()]]
                ysb = sb(es0, "ysb", [128, 8, TN], F32)
                b_ysb = [Buf() for _ in range(8)]
                ybf = [sb(es0, f"ybf{i}", [128, TN], BF16) for i in range(2)]
                ysq = [sb(es0, f"ysq{i}", [128, TN], BF16) for i in range(2)]
                b_ybf, b_ysq = [Buf(), Buf()], [Buf(), Buf()]
                sig = [sb(es0, f"sig{i}", [128, TN], F32) for i in range(2)]
                b_sig = [Buf(), Buf()]
                lmean = sb(es0, "lmean", [128, TN], F32)
                lrstd = sb(es0, "lrstd", [128, TN], F32)
                lt = sb(es0, "lt", [128, TN], F32)
                b_lmean, b_lrstd, b_lt = Buf(), Buf(), Buf()
                ld = [sb(es0, f"ld{i}", [128, TN], F32) for i in range(2)]
                b_ld = [Buf(), Buf()]
                sqh = [sb(es0, f"sqh{i}", [128, TN], BF16) for i in range(3)]
                b_sqh = [Buf(), Buf(), Buf()]
                rsh = [sb(es0, f"rsh{i}", [128, TN], F32) for i in range(3)]
                b_rsh = [Buf(), Buf(), Buf()]

                K.op(dve, lambda e: e.memset(Tst[:], 0.0), writes=[b_Tst])
                K.op(dve, lambda e: e.memset(Sb[:], 0.0), writes=[b_Sb])
                K.op(dve, lambda e: e.memset(hdn[:, :, 0:30], 0.0), writes=b_hdn)
                cgam = [float(np.exp(128.0 * np.log1p(-np.exp2(-5.0 - h)))) for h in range(4)]

                def load_x(tt_):
                    hrx, b_hrx = hres_l[tt_ % 2], b_hres_l[tt_ % 2]
                    if tt_ == 0:
                        K.op(dve, lambda e, hrx=hrx: e.memset(hrx[:, 0, :], 0.0), writes=[b_hrx[0]])
                        K.dma(sp, hrx[PAD:128, 0, :], meta[:, :], writes=[b_hrx[0]])
                        K.dma(sp, hrx[:, 1:TCH, :], x[0:(TCH - 1) * 128, :].rearrange("(c p) d -> p c d", p=128),
                              writes=b_hrx[1:TCH])
                    else:
                        r0 = tt_ * TN - 128
                        K.dma(sp, hrx[:, :, :], x[r0:r0 + TN, :].rearrange("(c p) d -> p c d", p=128), writes=b_hrx)

                for t in range(NT if stop != 'p0t0' else 1):
                    aux = dve
                    X.hres, X.b_hres = hres_l[t % 2], b_hres_l[t % 2]
                    hr, b_hr = X.hres, X.b_hres
                    p0 = t * TN
                    if t == 0:
                        load_x(0)
                    K.dma(sp, rot[:], c_rot[p0:p0 + TN].rearrange("(c p) a d -> p c a d", p=128), writes=[b_rot])

                    rmsnorm_T(X, G_MIX0)
                    def proj_gen(cbs):
                        for cb in cbs:
                            wt, bw = load_w(s_win, 0, cb * 512)
                            for c in range(TCH):
                                bank, bb = mm_bank()

                                def mmf(e, wt=wt, c=c, bank=bank):
                                    ins = None
                                    for kc in range(8):
                                        ins = e.matmul(bank, lhsT=hnT[:, kc, c * 128:(c + 1) * 128], rhs=wt[:, kc, :],
                                                       start=(kc == 0), stop=(kc == 7))
                                    return ins
                                K.op(pe, mmf, reads=[bw, b_hnT[c]], writes=[bb])
                                if cb < 2:
                                    dec = dq if cb == 0 else dk
                                    b3 = bank.rearrange("p (h d) -> p h d", h=4)
                                    cosb = rot[:, c, 0, :].unsqueeze(1).broadcast_to([128, 4, 128])
                                    K.op(dve, lambda e, b3=b3, cosb=cosb: e.tensor_tensor(
                                        out=tA[:].rearrange("p (h d) -> p h d", h=4), in0=b3, in1=cosb, op=ALU.mult),
                                        reads=[bb, b_rot], writes=[b_tA])
                                    s1 = rot[:, c, 1, 0:64].unsqueeze(1).broadcast_to([128, 4, 64])
                                    s2 = rot[:, c, 1, 64:128].unsqueeze(1).broadcast_to([128, 4, 64])
                                    tB3 = tB[:].rearrange("p (h d) -> p h d", h=4)

                                    def rotB(e, b3=b3, s1=s1, s2=s2, tB3=tB3):
                                        e.tensor_tensor(out=tB3[:, :, 0:64], in0=b3[:, :, 64:128], in1=s1, op=ALU.mult)
                                        return e.tensor_tensor(out=tB3[:, :, 64:128], in0=b3[:, :, 0:64], in1=s2, op=ALU.mult)
                                    K.op(dve, rotB, reads=[bb, b_rot], writes=[b_tB])
                                    K.op(aux, lambda e: e.tensor_tensor(out=tR[:], in0=tA[:], in1=tB[:], op=ALU.add),
                                         reads=[b_tA, b_tB], writes=[b_tR])

                                    def decf(e, c=c, cb=cb, dec=dec):
                                        ins = None
                                        for h in range(4):
                                            ins = e.activation(out=qkd[:, c, cb * 512 + h * 128: cb * 512 + (h + 1) * 128],
                                                               in_=tR[:, h * 128:(h + 1) * 128], func=AF.Copy,
                                                               scale=dec[:, h:h + 1])
                                        return ins
                                    K.op(act, decf, reads=[b_tR, b_c], writes=[b_qkd])
                                elif cb < 4:
                                    K.op(act, lambda e, c=c, cb=cb, bank=bank: e.activation(
                                        out=vb[:, c, (cb - 2) * 512:(cb - 1) * 512], in_=bank, func=AF.Copy),
                                        reads=[bb], writes=[b_vb])
                                else:
                                    K.op(act, lambda e, c=c, cb=cb, bank=bank: e.activation(
                                        out=sgb[:, c, (cb - 4) * 512:(cb - 3) * 512], in_=bank, func=AF.Silu),
                                        reads=[bb], writes=[b_sg])
                                    K.op(dve if t < 2 else pool, lambda e, c=c, cb=cb: e.tensor_tensor(
                                        out=sgb[:, c, (cb - 4) * 512:(cb - 3) * 512],
                                        in0=sgb[:, c, (cb - 4) * 512:(cb - 3) * 512],
                                        in1=gng_b[:, (cb - 4) * 512:(cb - 3) * 512], op=ALU.mult),
                                        reads=[b_sg, b_gng], writes=[b_sg])
                                yield


                    mm_set[0] = [0, 1, 2, 3, 4, 5, 6]
                    for _ in proj_gen(range(4)):
                        pass
                    mm_set[0] = [0, 1, 2]

                    def ret_gen():
                        for c in range(TCH):
                            i2 = c % 2

                            def trq(e, c=c):
                                ins = None
                                for kc in range(8):
                                    ins = e.transpose(out=tp[:, kc * 128:(kc + 1) * 128], in_=qkd[:, c, kc * 128:(kc + 1) * 128],
                                                      identity=ident)
                                return ins
                            K.op(pe, trq, reads=[b_qkd, b_c], writes=[b_tp])
                            K.op(act, lambda e, i2=i2: e.activation(out=qkT[i2][:], in_=tp[:].rearrange("p (k t) -> p k t", k=8),
                                                                    func=AF.Copy), reads=[b_tp], writes=[b_qkT[i2]])
                            bank, bb = mm_bank()

                            def scf(e, i2=i2, bank=bank):
                                ins = None
                                for h in range(4):
                                    ins = e.matmul(bank[:, h * 128:(h + 1) * 128], lhsT=qkT[i2][:, 4 + h, :],
                                                   rhs=qkT[i2][:, h, :], start=True, stop=True)
                                return ins
                            K.op(pe, scf, reads=[b_qkT[i2]], writes=[bb])
                            mb = maskT.unsqueeze(1).broadcast_to([128, 4, 128])
                            K.op(dve, lambda e, i2=i2, bank=bank, mb=mb: e.tensor_tensor(
                                out=scT[i2][:], in0=bank.rearrange("p (h t) -> p h t", h=4), in1=mb, op=ALU.mult),
                                reads=[bb, b_c], writes=[b_scT[i2]])
                            yield
                            o_ps = pp[:, 3:5, :].rearrange("p a b -> p (a b)")
                            kv_ps = pp[:, 5:7, :].rearrange("p a b -> p (a b)")

                            def of(e, i2=i2, c=c):
                                ins = None
                                for h in range(4):
                                    e.matmul(o_ps[:, h * 256:(h + 1) * 256], lhsT=scT[i2][:, h, :],
                                             rhs=vb[:, c, h * 256:(h + 1) * 256], start=True, stop=False)
                                    ins = e.matmul(o_ps[:, h * 256:(h + 1) * 256], lhsT=qkT[i2][:, h, :],
                                                   rhs=Sb[:, h * 256:(h + 1) * 256], start=False, stop=True)
                                return ins
                            K.op(pe, of, reads=[b_scT[i2], b_vb, b_qkT[i2], b_Sb], writes=[b_pp[3], b_pp[4]])

                            def kvf(e, c=c):
                                ins = None
                                for h in range(4):
                                    ins = e.matmul(kv_ps[:, h * 256:(h + 1) * 256],
                                                   lhsT=qkd[:, c, 512 + h * 128: 512 + (h + 1) * 128],
                                                   rhs=vb[:, c, h * 256:(h + 1) * 256], start=True, stop=True)
                                return ins
                            K.op(pe, kvf, reads=[b_qkd, b_vb], writes=[b_pp[5], b_pp[6]])

                            def tupd(e):
                                ins = None
                                for h in range(4):
                                    ins = e.scalar_tensor_tensor(out=Tst[:, h * 256:(h + 1) * 256], in0=Tst[:, h * 256:(h + 1) * 256],
                                                                 scalar=cgam[h], in1=kv_ps[:, h * 256:(h + 1) * 256],
                                                                 op0=ALU.mult, op1=ALU.add)
                                return ins
                            K.op(dve, tupd, reads=[b_pp[5], b_pp[6], b_Tst], writes=[b_Tst])

                            def sbf(e):
                                ins = None
                                for h in range(4):
                                    ins = e.activation(out=Sb[:, h * 256:(h + 1) * 256], in_=Tst[:, h * 256:(h + 1) * 256],
                                                       func=AF.Copy, scale=cgam[h])
                                return ins
                            K.op(act, sbf, reads=[b_Tst], writes=[b_Sb])

                            def gstf(e):
                                ins = None
                                for h in range(4):
                                    ins = e.bn_stats(out=gst[:, h, :], in_=o_ps[:, h * 256:(h + 1) * 256])
                                return ins
                            K.op(dve, gstf, reads=[b_pp[3], b_pp[4]], writes=[b_gst])

                            def gagf(e):
                                ins = None
                                for h in range(4):
                                    ins = e.bn_aggr(out=gmv[:, h, :], in_=gst[:, h, :])
                                return ins
                            K.op(dve, gagf, reads=[b_gst], writes=[b_gmv])
                            K.op(act, lambda e: e.activation(out=grs[:, 4:8], in_=gmv[:, :, 1], func=AF.Sqrt, bias=EPS, scale=1.0),
                                 reads=[b_gmv], writes=[b_grs])
                            K.op(dve, lambda e: e.reciprocal(out=grs[:, 0:4], in_=grs[:, 4:8]), reads=[b_grs], writes=[b_grs])

                            def gnf(e):
                                ins = None
                                for h in range(4):
                                    ins = e.tensor_scalar(out=tmpn[:, h * 256:(h + 1) * 256], in0=o_ps[:, h * 256:(h + 1) * 256],
                                                          scalar1=gmv[:, h, 0:1], scalar2=grs[:, h:h + 1],
                                                          op0=ALU.subtract, op1=ALU.mult)
                                return ins
                            K.op(dve, gnf, reads=[b_pp[3], b_pp[4], b_gmv, b_grs], writes=[b_tmpn])
                            K.op(aux, lambda e, c=c: e.tensor_tensor(out=og[:], in0=tmpn[:], in1=sgb[:, c, :], op=ALU.mult),
                                 reads=[b_tmpn, b_sg], writes=[b_og])
                            yield

                            def tro(e):
                                ins = None
                                for kc in range(8):
                                    ins = e.transpose(out=tp[:, kc * 128:(kc + 1) * 128], in_=og[:, kc * 128:(kc + 1) * 128],
                                                      identity=ident)
                                return ins
                            K.op(pe, tro, reads=[b_og, b_c], writes=[b_tp])
                            K.op(act, lambda e, c=c: e.activation(out=ocT[:, 0:8, c * 128:(c + 1) * 128],
                                                                   in_=tp[:].rearrange("p (k t) -> p k t", k=8), func=AF.Copy),
                                 reads=[b_tp], writes=b_ocT[0:8])

                            yield

                    if t > 0:
                        K.op(aux, lambda e: e.tensor_copy(out=hdn[:, :, 0:30], in_=hdn[:, :, TN:TN + 30]),
                             reads=b_hdn, writes=b_hdn)
                    def ag_gen():
                        for j in range(2):
                            wa, bwa = load_w(s_win, 0, 3072 + j * 512)
                            wg, bwg = load_w(s_win, 0, 4096 + j * 512)
                            for cc in range(4):
                                ch = j * 4 + cc
                                banka, bba = mm_bank()
                                bankg, bbg = mm_bank()

                                def mma(e, w=wa, cc=cc, bank=banka):
                                    ins = None
                                    for kc in range(8):
                                        ins = e.matmul(bank[:, 0:TN], lhsT=w[:, kc, cc * 128:(cc + 1) * 128], rhs=hnT[:, kc, :],
                                                       start=(kc == 0), stop=(kc == 7))
                                    return ins
                                K.op(pe, mma, reads=[bwa] + b_hnT, writes=[bba])

                                def mmg(e, w=wg, cc=cc, bank=bankg):
                                    ins = None
                                    for kc in range(8):
                                        ins = e.matmul(bank[:, 0:TN], lhsT=w[:, kc, cc * 128:(cc + 1) * 128], rhs=hnT[:, kc, :],
                                                       start=(kc == 0), stop=(kc == 7))
                                    return ins
                                K.op(pe, mmg, reads=[bwg] + b_hnT, writes=[bbg])
                                i2 = ch % 2
                                K.op(act, lambda e, i2=i2, bank=bankg: e.activation(out=sig[i2][:], in_=bank[:, 0:TN],
                                                                                    func=AF.Sigmoid),
                                     reads=[bbg], writes=[b_sig[i2]])
                                K.op(dve, lambda e, i2=i2, ch=ch, bank=banka: e.tensor_tensor(
                                    out=hdn[:, ch, 30:30 + TN], in0=bank[:, 0:TN], in1=sig[i2][:], op=ALU.mult),
                                    reads=[bba, b_sig[i2]], writes=[b_hdn[ch]])
                                yield
                    import itertools
                    _rg = ret_gen()
                    _fill = itertools.chain(proj_gen(range(4, 6)), ag_gen())
                    _nf = [0]

                    def _fill1():
                        try:
                            next(_fill)
                            _nf[0] += 1
                            return True
                        except StopIteration:
                            return False
                    for _c in range(TCH):
                        for _part in range(3):
                            if _part == 1:
                                while _nf[0] < 4 + _c:
                                    _fill1()
                            _n = 2 if _part == 2 else (0 if (_part == 1 and _c == 0) else 1)
                            for _ in range(_n):
                                _fill1()
                            next(_rg)
                    while _fill1():
                        pass
                    for _ in _rg:
                        pass
                    sum_ps = pp[:, 3, 0:TN]
                    sq_ps = pp[:, 4, 0:TN]
                    def emit_stats(ch):
                        i2 = ch % 2
                        K.op(pe, lambda e, ch=ch, i2=i2: e.matmul(sum_ps, lhsT=ones_bf, rhs=ybf[i2][:], start=(ch == 0),
                                                                  stop=(ch == 7)),
                             reads=[b_ybf[i2], b_c], writes=[b_pp[3]])
                        K.op(pe, lambda e, ch=ch, i2=i2: e.matmul(sq_ps, lhsT=ones_bf, rhs=ysq[i2][:], start=(ch == 0),
                                                                  stop=(ch == 7)),
                             reads=[b_ysq[i2], b_c], writes=[b_pp[4]])
                    def build_dg(ch):
                        cwv = convw[:].rearrange("p (w c) -> p w c", c=8)[:, :, ch]
                        dg = dg2[ch % 2]
                        b_dga, b_dgb, b_dgc = b_dg2[ch % 2]
                        K.op(dve, lambda e, cwv=cwv, dg=dg: e.tensor_tensor(
                            out=dg[:, 0:10, :], in0=ident.unsqueeze(1).broadcast_to([128, 10, 128]),
                            in1=cwv[:, 0:10].unsqueeze(2).broadcast_to([128, 10, 128]), op=ALU.mult),
                            reads=[b_c, b_convw], writes=[b_dga])
                        dg_pe = dve if t < 2 else pool
                        K.op(dg_pe, lambda e, cwv=cwv, dg=dg: e.tensor_tensor(
                            out=dg[:, 10:20, :], in0=ident.unsqueeze(1).broadcast_to([128, 10, 128]),
                            in1=cwv[:, 10:20].unsqueeze(2).broadcast_to([128, 10, 128]), op=ALU.mult),
                            reads=[b_c, b_convw], writes=[b_dgc])

                        def dgb(e, dg=dg, ch=ch):
                            ins = None
                            for w in range(20, 31):
                                ins = e.activation(out=dg[:, w, :], in_=ident, func=AF.Copy,
                                                   scale=convw[:, w * 8 + ch:w * 8 + ch + 1])
                            return ins
                        K.op(act, dgb, reads=[b_c, b_convw], writes=[b_dgb])

                    dg_pe = dve if t < 2 else pool
                    mm_set[0] = [0, 1, 2, 5, 6]
                    build_dg(0)
                    for ch in range(8):
                        dg = dg2[ch % 2]
                        b_dga, b_dgb, b_dgc = b_dg2[ch % 2]
                        bank, bb = mm_bank()

                        def cvf(e, ch=ch, bank=bank, dg=dg):
                            ins = None
                            for w in range(31):
                                ins = e.matmul(bank[:, 0:TN], lhsT=dg[:, w, :], rhs=hdn[:, ch, w:w + TN],
                                               start=(w == 0), stop=(w == 30))
                            return ins
                        K.op(pe, cvf, reads=[b_dga, b_dgb, b_dgc, b_hdn[ch]], writes=[bb])
                        if ch + 1 < 8:
                            build_dg(ch + 1)
                        if ch > 0:
                            emit_stats(ch - 1)
                        i2 = ch % 2
                        K.op(act, lambda e, ch=ch, bank=bank: e.activation(out=ysb[:, ch, :], in_=bank[:, 0:TN],
                                                                          func=AF.Identity, bias=pvec[:, ch:ch + 1]),
                             reads=[bb, b_pvec], writes=[b_ysb[ch]])
                        K.op(dg_pe, lambda e, ch=ch, i2=i2: e.tensor_copy(out=ybf[i2][:], in_=ysb[:, ch, :]),
                             reads=[b_ysb[ch]], writes=[b_ybf[i2]])
                        K.op(act, lambda e, ch=ch, i2=i2, bank=bank: e.activation(out=ysq[i2][:], in_=bank[:, 0:TN],
                                                                                 func=AF.Square, bias=pvec[:, ch:ch + 1]),
                             reads=[bb, b_pvec], writes=[b_ysq[i2]])
                    emit_stats(7)
                    K.op(act, lambda e: e.activation(out=lmean[:], in_=sum_ps, func=AF.Copy, scale=1.0 / D),
                         reads=[b_pp[3]], writes=[b_lmean])
                    K.op(dve, lambda e: e.tensor_tensor(out=lt[:], in0=lmean[:], in1=lmean[:], op=ALU.mult),
                         reads=[b_lmean], writes=[b_lt])
                    K.op(dve, lambda e: e.scalar_tensor_tensor(out=lt[:], in0=sq_ps, scalar=1.0 / D, in1=lt[:],
                                                               op0=ALU.mult, op1=ALU.subtract),
                         reads=[b_pp[4], b_lt], writes=[b_lt])
                    K.op(act, lambda e: e.activation(out=lt[:], in_=lt[:], func=AF.Sqrt, bias=EPS, scale=1.0),
                         reads=[b_lt], writes=[b_lt])
                    K.op(dve, lambda e: e.reciprocal(out=lrstd[:], in_=lt[:]), reads=[b_lt], writes=[b_lrstd])
                    for ch in range(8):
                        i2 = ch % 2
                        K.op(dve, lambda e, ch=ch, i2=i2: e.tensor_tensor(out=ld[i2][:], in0=ysb[:, ch, :], in1=lmean[:],
                                                                          op=ALU.subtract),
                             reads=[b_ysb[ch], b_lmean], writes=[b_ld[i2]])
                        K.op(aux, lambda e, i2=i2: e.tensor_tensor(out=ld[i2][:], in0=ld[i2][:], in1=lrstd[:], op=ALU.mult),
                             reads=[b_ld[i2], b_lrstd], writes=[b_ld[i2]])
                        K.op(act, lambda e, ch=ch, i2=i2: e.activation(out=ocT[:, 8 + ch, :], in_=ld[i2][:], func=AF.Silu,
                                                                       bias=pvec[:, 16 + ch:17 + ch],
                                                                       scale=pvec[:, 8 + ch:9 + ch]),
                             reads=[b_ld[i2], b_pvec], writes=[b_ocT[8 + ch]])

                    mm_set[0] = [0, 1, 2]
                    wo_banks = [[0, 1, 2], [4, 5, 6]]
                    for kp in range(2):
                        for cb in range(2):
                            wt, bw = load_w(s_wout, kp * 8, cb * 512)
                            for c in range(TCH):
                                bi = wo_banks[cb][c]

                                def mmf(e, wt=wt, c=c, kp=kp, bi=bi):
                                    ins = None
                                    for kc in range(8):
                                        ins = e.matmul(pp[:, bi, :], lhsT=ocT[:, kp * 8 + kc, c * 128:(c + 1) * 128],
                                                       rhs=wt[:, kc, :], start=(kp == 0 and kc == 0),
                                                       stop=(kp == 1 and kc == 7))
                                    return ins
                                K.op(pe, mmf, reads=[bw] + b_ocT[kp * 8:kp * 8 + 8], writes=[b_pp[bi]])
                    for cb in range(2):
                        for c in range(TCH):
                            bi = wo_banks[cb][c]
                            K.op(dve, lambda e, c=c, cb=cb, hr=hr, bi=bi: e.tensor_tensor(
                                out=hr[:, c, cb * 512:(cb + 1) * 512], in0=hr[:, c, cb * 512:(cb + 1) * 512],
                                in1=pp[:, bi, :], op=ALU.add), reads=[b_pp[bi], b_hr[c]], writes=[b_hr[c]])
                    if t == 0:
                        K.op(dve, lambda e, hr=hr: e.memset(hr[0:PAD, 0, :], 0.0), reads=[b_hr[0]], writes=[b_hr[0]])
                    mlp(X, 0, G_MLP0, aux)
                    if t + 1 < (NT if stop != 'p0t0' else 1):
                        load_x(t + 1)
                    if debug and "h0" in debug:
                        K.dma(sp, dbg["h0"][p0:p0 + TN, :].rearrange("(c p) d -> p c d", p=128), hr[:, :, :], reads=b_hr)

                    K.dma(sp, s_h[p0:p0 + TN, :].rearrange("(c p) d -> p c d", p=128), hr[:, :, :], reads=b_hr,
                          writes=[scr_buf(s_h)])
                    rmsnorm_T(X, G_MIX1)
                    qTst = B6[0][:].rearrange("p (h t) -> p h t", h=8)
                    kTst = B6[1][:].rearrange("p (h t) -> p h t", h=8)
                    vst = B6[2][:].rearrange("p (c n) -> p c n", c=TCH)
                    mm_set[0] = [0, 1, 2, 3, 4]
                    blk_ctr = [0]
                    qk_items = [(cb, j) for cb in range(4) for j in range(4)]
                    qk_state = {}

                    def qk_A(n):
                        cb, j = qk_items[n]
                        if j == 0:
                            qk_state["w"] = load_w(s_wqkv, 0, cb * 512)
                        wt, bw = qk_state["w"]
                        hp = (cb % 2) * 4 + j
                        bank, bb = mm_bank()

                        def mmf(e, wt=wt, j=j, bank=bank):
                            ins = None
                            for kc in range(8):
                                ins = e.matmul(bank[:, 0:TN], lhsT=wt[:, kc, j * 128:(j + 1) * 128], rhs=hnT[:, kc, :],
                                               start=(kc == 0), stop=(kc == 7))
                            return ins
                        K.op(pe, mmf, reads=[bw] + b_hnT, writes=[bb])
                        i2 = n % 3
                        K.op(act, lambda e, i2=i2, bank=bank: e.activation(out=sqh[i2][:], in_=bank[:, 0:TN],
                                                                          func=AF.Square),
                             reads=[bb], writes=[b_sqh[i2]])
                        qk_state[n] = (bank, bb, hp, cb < 2)

                    def qk_B(n):
                        bank, bb, hp, isq = qk_state[n]
                        dstT = qTst if isq else kTst
                        b_dst = b_B6[0] if isq else b_B6[1]
                        i2 = n % 3
                        _bi = 5 + (blk_ctr[0] % 2)
                        blk_ctr[0] += 1
                        bank2, bb2 = pp[:, _bi, :], b_pp[_bi]
                        K.op(pe, lambda e, i2=i2, bank2=bank2: e.matmul(bank2[:, 0:TN], lhsT=blk, rhs=sqh[i2][:],
                                                                        start=True, stop=True),
                             reads=[b_sqh[i2], b_c], writes=[bb2])
                        K.op(act, lambda e, i2=i2, bank2=bank2: e.activation(out=rsh[i2][:], in_=bank2[:, 0:TN],
                                                                            func=AF.Sqrt, bias=EPS, scale=1.0 / 64),
                             reads=[bb2], writes=[b_rsh[i2]])
                        K.op(dve, lambda e, i2=i2: e.reciprocal(out=rsh[i2][:], in_=rsh[i2][:]),
                             reads=[b_rsh[i2]], writes=[b_rsh[i2]])
                        gc = 0 if isq else 1
                        K.op(dve, lambda e, i2=i2, bank=bank, hp=hp, dstT=dstT, gc=gc: e.scalar_tensor_tensor(
                            out=dstT[:, hp, :], in0=bank[:, 0:TN], scalar=gqk[:, gc:gc + 1], in1=rsh[i2][:],
                            op0=ALU.mult, op1=ALU.mult), reads=[bb, b_rsh[i2], b_gqk], writes=[b_dst])

                    qk_A(0)
                    qk_A(1)
                    for n in range(16):
                        if n + 2 < 16:
                            qk_A(n + 2)
                        qk_B(n)
                    mm_set[0] = [0, 1, 2]
                    for cb in range(2):
                        wt, bw = load_w(s_wqkv, 0, 2048 + cb * 512)
                        for c in range(TCH):
                            bank, bb = mm_bank()

                            def mmf(e, wt=wt, c=c, bank=bank):
                                ins = None
                                for kc in range(8):
                                    ins = e.matmul(bank, lhsT=hnT[:, kc, c * 128:(c + 1) * 128], rhs=wt[:, kc, :],
                                                   start=(kc == 0), stop=(kc == 7))
                                return ins
                            K.op(pe, mmf, reads=[bw, b_hnT[c]], writes=[bb])
                            K.op(act, lambda e, c=c, cb=cb, bank=bank: e.activation(
                                out=vst[:, c, cb * 512:(cb + 1) * 512], in_=bank, func=AF.Copy),
                                reads=[bb], writes=[b_B6[2]])
                    K.dma(sp, s_qT[:, :, p0:p0 + TN].rearrange("h p t -> p h t"), qTst, reads=[b_B6[0]],
                          writes=[scr_buf(s_qT)])
                    K.dma(sp, s_kT[:, :, p0:p0 + TN].rearrange("h p t -> p h t"), kTst, reads=[b_B6[1]],
                          writes=[scr_buf(s_kT)])
                    K.dma(sp, s_v[p0:p0 + TN, :].rearrange("(c p) d -> p c d", p=128), vst, reads=[b_B6[2]],
                          writes=[scr_buf(s_v)])
            K.barrier()
            es_a.close()

            with ExitStack() as es2:
                oT_all = sb(es2, "oT_all", [128, 8, P], BF16)
                b_oT = [Buf(f"oT{i}") for i in range(8)]
                with ExitStack() as es2b:
                    QT = [sb(es2b, f"QT{i}", [128, P], BF16) for i in range(2)]
                    KT = [sb(es2b, f"KT{i}", [128, P], BF16) for i in range(2)]
                    Vh = [sb(es2b, f"Vh{i}", [128, NCHK, 128], BF16) for i in range(2)]
                    b_QT, b_KT, b_Vh = [Buf(), Buf()], [Buf(), Buf()], [Buf(), Buf()]
                    ee = [sb(es2b, f"ee{i}", [128, 2, TN], F32) for i in range(3)]
                    spb = [sb(es2b, f"spb{i}", [128, 2, TN], BF16) for i in range(3)]
                    tt = [sb(es2b, f"tt{i}", [128, 2, TN], F32) for i in range(2)]
                    ww = [sb(es2b, f"ww{i}", [128, 2, TN], BF16) for i in range(2)]
                    b_ee, b_spb, b_tt, b_ww = [Buf(), Buf(), Buf()], [Buf(), Buf(), Buf()], [Buf(), Buf()], [Buf(), Buf()]
                    zps = [pp[:, 0:2, :], pp[:, 2:4, :]]
                    b_z = [[b_pp[0], b_pp[1]], [b_pp[2], b_pp[3]]]
                    accps = pp[:, 4:6, :]
                    b_acc = [b_pp[4], b_pp[5]]
                    ops_ = pp[:, 6, :]
                    b_o = b_pp[6]
                    NG = NT
                    NHP = 8 if stop not in ('p01', 'p0t0') else 0
                    steps = []
                    for hp in range(NHP):
                        for G in range(NG):
                            kb_hi = TCH * G + TCH - 1
                            for kb in range(kb_hi, -1, -1):
                                steps.append((hp, G, kb, kb_hi))
                    nst = len(steps)

                    def emit_load(hp):
                        s = hp % 2
                        K.dma(sp, QT[s][:], s_qT[hp], reads=[scr_buf(s_qT)], writes=[b_QT[s]])
                        K.dma(sp, KT[s][:], s_kT[hp], reads=[scr_buf(s_kT)], writes=[b_KT[s]])
                        for c0 in range(0, NCHK, 11):
                            c1_ = min(NCHK, c0 + 11)
                            K.dma(sp, Vh[s][:, c0:c1_, :],
                                  s_v.rearrange("(c p) d -> p c d", p=128)[:, c0:c1_, hp * 128:(hp + 1) * 128],
                                  reads=[scr_buf(s_v)], writes=[b_Vh[s]])

                    def geo(j):
                        hp, G, kb, kb_hi = steps[j]
                        r = kb - TCH * G
                        q0 = max(r, 0) * 128
                        return hp, G, kb, kb_hi, r, q0, hp % 2, j % 2, j % 3, G * TN

                    def S_zf(j):
                        hp, G, kb, kb_hi, r, q0, s, i2, i3, g0 = geo(j)
                        z = zps[i2]

                        def zf(e, s=s, kb=kb, z=z, q0=q0, g0=g0):
                            ins = None
                            for h in range(2):
                                ins = e.matmul(z[:, h, q0:TN],
                                               lhsT=KT[s][h * 64:(h + 1) * 64, kb * 128:(kb + 1) * 128],
                                               rhs=QT[s][h * 64:(h + 1) * 64, g0 + q0:g0 + TN],
                                               start=True, stop=True)
                            return ins
                        K.op(pe, zf, reads=[b_KT[s], b_QT[s]], writes=b_z[i2])

                    def S_expln(j):
                        hp, G, kb, kb_hi, r, q0, s, i2, i3, g0 = geo(j)
                        z = zps[i2]
                        K.op(act, lambda e, i3=i3, z=z, q0=q0: e.activation(out=ee[i3][:, :, q0:TN], in_=z[:, :, q0:TN],
                                                                           func=AF.Exp),
                             reads=b_z[i2], writes=[b_ee[i3]])
                        K.op(act, lambda e, i3=i3, q0=q0: e.activation(out=spb[i3][:, :, q0:TN], in_=ee[i3][:, :, q0:TN],
                                                                      func=AF.Ln, bias=1.0, scale=1.0),
                             reads=[b_ee[i3]], writes=[b_spb[i3]])
                        if r >= 0:
                            mb = mT.unsqueeze(1).broadcast_to([128, 2, 128])
                            K.op(dve, lambda e, i3=i3, q0=q0, mb=mb: e.tensor_tensor(
                                out=ee[i3][:, :, q0:q0 + 128], in0=ee[i3][:, :, q0:q0 + 128], in1=mb, op=ALU.mult),
                                reads=[b_ee[i3], b_c], writes=[b_ee[i3]])
                            K.op(dve, lambda e, i3=i3, q0=q0, mb=mb: e.tensor_tensor(
                                out=spb[i3][:, :, q0:q0 + 128], in0=spb[i3][:, :, q0:q0 + 128], in1=mb, op=ALU.mult),
                                reads=[b_spb[i3], b_c], writes=[b_spb[i3]])
                        if kb == 0:
                            K.op(dve, lambda e, i3=i3, q0=q0: e.memset(ee[i3][0:PAD, :, q0:TN], 0.0),
                                 reads=[b_ee[i3]], writes=[b_ee[i3]])
                            K.op(dve, lambda e, i3=i3, q0=q0: e.memset(spb[i3][0:PAD, :, q0:TN], 0.0),
                                 reads=[b_spb[i3]], writes=[b_spb[i3]])

                    def S_c1(j):
                        hp, G, kb, kb_hi, r, q0, s, i2, i3, g0 = geo(j)

                        def c1(e, i3=i3, q0=q0, first=(kb == kb_hi)):
                            ins = None
                            for h in range(2):
                                ins = e.matmul(accps[:, h, q0:TN], lhsT=tri, rhs=spb[i3][:, h, q0:TN],
                                               start=first, stop=True, skip_group_check=True)
                            return ins
                        K.op(pe, c1, reads=[b_spb[i3], b_c], writes=b_acc)

                    def S_exp2(j):
                        hp, G, kb, kb_hi, r, q0, s, i2, i3, g0 = geo(j)
                        K.op(act, lambda e, i2=i2, q0=q0: e.activation(out=tt[i2][:, :, q0:TN], in_=accps[:, :, q0:TN],
                                                                      func=AF.Exp, scale=-1.0),
                             reads=b_acc, writes=[b_tt[i2]])

                    def S_c2(j):
                        hp, G, kb, kb_hi, r, q0, s, i2, i3, g0 = geo(j)
                        if kb > 0:
                            def c2(e, i3=i3, q0=q0):
                                ins = None
                                for h in range(2):
                                    ins = e.matmul(accps[:, h, q0:TN], lhsT=stri, rhs=spb[i3][:, h, q0:TN],
                                                   start=False, stop=True, skip_group_check=True)
                                return ins
                            K.op(pe, c2, reads=[b_spb[i3], b_c], writes=b_acc)

                    def S_mult(j):
                        hp, G, kb, kb_hi, r, q0, s, i2, i3, g0 = geo(j)
                        K.op(dve, lambda e, i2=i2, i3=i3, q0=q0: e.tensor_tensor(
                            out=ww[i2][:, :, q0:TN], in0=ee[i3][:, :, q0:TN], in1=tt[i2][:, :, q0:TN], op=ALU.mult),
                            reads=[b_ee[i3], b_tt[i2]], writes=[b_ww[i2]])

                    def S_wv(j):
                        hp, G, kb, kb_hi, r, q0, s, i2, i3, g0 = geo(j)

                        def wv(e, s=s, i2=i2, kb=kb, q0=q0, first=(kb == kb_hi)):
                            ins = None
                            for h in range(2):
                                ins = e.matmul(ops_[h * 64:(h + 1) * 64, q0:TN], lhsT=Vh[s][:, kb, h * 64:(h + 1) * 64],
                                               rhs=ww[i2][:, h, q0:TN], start=first, stop=True, skip_group_check=True)
                            return ins
                        K.op(pe, wv, reads=[b_Vh[s], b_ww[i2]], writes=[b_o])
                        if kb == 0:
                            K.op(dve, lambda e, hp=hp, g0=g0: e.tensor_copy(out=oT_all[:, hp, g0:g0 + TN], in_=ops_[:, 0:TN]),
                                 reads=[b_o], writes=[b_oT[hp]])
                        if (j == 0 or steps[j - 1][0] != hp) and hp + 1 < NHP:
                            emit_load(hp + 1)

                    if nst:
                        emit_load(0)
                        S_zf(0)
                        if nst > 1:
                            S_zf(1)
                        S_expln(0)
                        if nst > 2:
                            S_zf(2)
                        if nst > 1:
                            S_expln(1)
                        S_c1(0)
                    for i in range(nst):
                        S_exp2(i)
                        S_c2(i)
                        if i + 1 < nst:
                            S_c1(i + 1)
                        S_mult(i)
                        if i >= 1:
                            S_wv(i - 1)
                        if i + 3 < nst:
                            S_zf(i + 3)
                        if i + 2 < nst:
                            S_expln(i + 2)
                    if nst:
                        S_wv(nst - 1)
                K.barrier()

                es3 = es2.enter_context(ExitStack())
                X3 = alloc_tile(es3, "b")
                hres3 = [X3.hres, sb(es3, "hresb2", [128, TCH, D], F32)]
                b_hres3 = [X3.b_hres, [Buf(f"hres2_{c}") for c in range(TCH)]]
                for t in range((1 if stop == 'p3t0' else NT) if stop not in ('p01', 'p0t0', 'p2') else 0):
                    X3.hres, X3.b_hres = hres3[t % 2], b_hres3[t % 2]
                    hr3, b_hr3 = X3.hres, X3.b_hres
                    p0 = t * TN
                    K.dma(sp, hr3[:, :, :], s_h[p0:p0 + TN, :].rearrange("(c p) d -> p c d", p=128),
                          reads=[scr_buf(s_h)], writes=b_hr3)
                    for cb in range(2):
                        wt, bw = load_w(s_wo, 0, cb * 512)
                        for c in range(TCH):
                            def mmf(e, wt=wt, c=c, p0=p0):
                                ins = None
                                for kc in range(8):
                                    ins = e.matmul(pp[:, c, :], lhsT=oT_all[:, kc, p0 + c * 128:p0 + (c + 1) * 128],
                                                   rhs=wt[:, kc, :], start=(kc == 0), stop=(kc == 7))
                                return ins
                            K.op(pe, mmf, reads=[bw] + b_oT, writes=[b_pp[c]])
                        for c in range(TCH):
                            K.op(dve, lambda e, c=c, cb=cb, hr3=hr3: e.tensor_tensor(
                                out=hr3[:, c, cb * 512:(cb + 1) * 512], in0=hr3[:, c, cb * 512:(cb + 1) * 512],
                                in1=pp[:, c, :], op=ALU.add), reads=[b_pp[c], b_hr3[c]], writes=[b_hr3[c]])
                    mlp(X3, 1, G_MLP1, pool)
                    if t == 0:
                        K.dma(sp, out[0:(TCH - 1) * 128, :].rearrange("(c p) d -> p c d", p=128), hr3[:, 1:TCH, :],
                              reads=b_hr3[1:TCH])
                    else:
                        r0 = p0 - 128
                        K.dma(sp, out[r0:r0 + TN, :].rearrange("(c p) d -> p c d", p=128), hr3[:, :, :], reads=b_hr3)
            K.finish()
            K.emit()
    return nc


_CACHE = {}


def _get_nc(SEQ, debug=None):
    key = (SEQ, None if debug is None else tuple(sorted(debug)))
    if key not in _CACHE:
        _CACHE[key] = build(SEQ, debug)
    return _CACHE[key]


def make_in_maps(inputs, SEQ, ncores):
    P = PAD + NMETA + SEQ
    cb, cf, rot = host_consts(P)
    f = lambda a: np.ascontiguousarray(np.asarray(a, dtype=np.float32))
    shared = {
        "meta": f(inputs["meta"]),
        "norm_mix_g": f(inputs["norm_mix_g"]),
        "norm_mlp_g": f(inputs["norm_mlp_g"]),
        "even_w_in": f(inputs["even_w_in"][0]),
        "even_ret_gn_g": f(inputs["even_ret_gn_g"][0]).reshape(1024),
        "even_conv_w": f(inputs["even_conv_w"][0]),
        "even_conv_b": f(inputs["even_conv_b"][0]),
        "even_conv_ln_g": f(inputs["even_conv_ln_g"][0]),
        "even_conv_ln_b": f(inputs["even_conv_ln_b"][0]),
        "even_w_out": f(inputs["even_w_out"][0]),
        "odd_w_qkv": f(inputs["odd_w_qkv"][0]),
        "odd_q_norm_g": f(inputs["odd_q_norm_g"][0]),
        "odd_k_norm_g": f(inputs["odd_k_norm_g"][0]),
        "odd_w_o": f(inputs["odd_w_o"][0]),
        "mlp_w1_0": f(inputs["mlp_w1"][0]),
        "mlp_w1_1": f(inputs["mlp_w1"][1]),
        "mlp_w2_0": f(inputs["mlp_w2"][0]),
        "mlp_w2_1": f(inputs["mlp_w2"][1]),
        "c_bf": cb,
        "c_f": cf,
        "c_rot": rot,
    }
    xs = np.asarray(inputs["x"], dtype=np.float32)
    maps = []
    for b in range(ncores):
        m = dict(shared)
        m["x"] = np.ascontiguousarray(xs[b])
        maps.append(m)
    return maps


def kernel(**inputs):
    x = np.asarray(inputs["x"])
    B, SEQ, _ = x.shape
    nc = _get_nc(SEQ)
    maps = make_in_maps(inputs, SEQ, B)
    res = run_bass_kernel_spmd(nc, maps, core_ids=list(range(B)))
    return np.stack([np.asarray(r["out"], dtype=np.float32) for r in res.results], axis=0)
```
